# Optimizing a Trainium2 kernel written in Bass

```python
import math
import jax, jax.numpy as jnp
from jax import lax
import numpy as np


D_MODEL = 1024
BATCH = 16
SEQ = 2048
DEPTH = 4

GRID_W = 64
Q_BLOCK = 128
EPS = 1e-6

A_HEADS = 8
A_KV_HEADS = 2
A_GROUP = A_HEADS // A_KV_HEADS
A_HEAD_DIM = 64
AXIAL_THETA = 10000.0

B_HEADS = 8
B_QK_DIM = 32
B_V_DIM = 2 * B_QK_DIM
B_ROT_DIM = B_QK_DIM // 4
ROPE_THETA = 500000.0

C_HEADS = 8
C_Q_RANK = 384
C_KV_RANK = 256
C_NOPE_DIM = 64
C_ROPE_DIM = 32
C_V_DIM = 64
MLA_THETA = 10000.0

N_BRANCH = 3
BRANCH_W = 512
D_FF = 4 * D_MODEL

A_Q_W = A_HEADS * A_HEAD_DIM
A_KV_W = A_KV_HEADS * A_HEAD_DIM
B_QK_W = B_HEADS * 2 * B_QK_DIM
B_V_W = B_HEADS * B_V_DIM
GATE_W = N_BRANCH * D_MODEL
IN_SIZES = (A_Q_W, A_KV_W, A_KV_W, B_QK_W, B_QK_W, B_V_W, C_Q_RANK, C_KV_RANK, C_ROPE_DIM, GATE_W)
IN_COLS = A_Q_W + 2 * A_KV_W + 2 * B_QK_W + B_V_W + C_Q_RANK + C_KV_RANK + C_ROPE_DIM + GATE_W

kernel_name = 'hybrid_gated_gqa_diff_mla_encoder'


def _rms_norm(x, g):
    xf = x.astype(jnp.float32)
    y = xf * lax.rsqrt(jnp.mean(xf * xf, axis=-1, keepdims=True) + EPS)
    return (y * g.astype(jnp.float32)).astype(x.dtype)


def _angles(pos, dim, theta):
    inv_freq = theta ** (-jnp.arange(0, dim, 2, dtype=jnp.float32) / dim)
    return pos.astype(jnp.float32)[:, None] * inv_freq[None, :]


def _rotate(x, ang):
    cos = jnp.cos(ang)[:, None, :].astype(x.dtype)
    sin = jnp.sin(ang)[:, None, :].astype(x.dtype)
    x1, x2 = jnp.split(x, 2, axis=-1)
    return jnp.concatenate([x1 * cos - x2 * sin, x2 * cos + x1 * sin], axis=-1)


def _softmax_f32(s):
    return jax.nn.softmax(s.astype(jnp.float32), axis=-1)


def _sweep_query_blocks(block_fn, *qs):
    b, s = qs[0].shape[:2]
    nb = s // Q_BLOCK
    blocked = tuple(jnp.swapaxes(q.reshape((b, nb, Q_BLOCK) + q.shape[2:]), 0, 1) for q in qs)
    out = lax.map(lambda qb: block_fn(*qb), blocked)
    out = jnp.swapaxes(out, 0, 1)
    return out.reshape((b, s) + out.shape[3:])


def _split_columns(p):
    out, start = [], 0
    for n in IN_SIZES:
        out.append(p[..., start:start + n])
        start += n
    return out


def _gqa_axial(q, k, v, q_g, k_g, row_ang, col_ang):
    b, s = q.shape[:2]
    half = A_HEAD_DIM // 2

    def axial(t):
        return jnp.concatenate([_rotate(t[..., :half], row_ang), _rotate(t[..., half:], col_ang)], axis=-1)

    q = axial(_rms_norm(q, q_g)) * (A_HEAD_DIM ** -0.5)
    k = axial(_rms_norm(k, k_g))
    q = q.reshape(b, s, A_KV_HEADS, A_GROUP, A_HEAD_DIM)

    def block(qb):
        sc = jnp.einsum('bqkgd,bskd->bkgqs', qb, k)
        p = _softmax_f32(sc).astype(v.dtype)
        return jnp.einsum('bkgqs,bskd->bqkgd', p, v)

    o = _sweep_query_blocks(block, q)
    return o.reshape(b, s, A_HEADS * A_HEAD_DIM)


def _diff_attention(q, k, v, lam_p, sub_g, lambda_init, rot_ang):
    b, s = q.shape[:2]

    def partial_rot(t):
        t = t.reshape(b, s, B_HEADS * 2, B_QK_DIM)
        t = jnp.concatenate([_rotate(t[..., :B_ROT_DIM], rot_ang), t[..., B_ROT_DIM:]], axis=-1)
        return t.reshape(b, s, B_HEADS, 2, B_QK_DIM)

    q = partial_rot(q) * (B_QK_DIM ** -0.5)
    k = partial_rot(k)
    v = v.reshape(b, s, B_HEADS, B_V_DIM)
    lf = lam_p.astype(jnp.float32)
    lam = jnp.exp(jnp.sum(lf[0] * lf[1])) - jnp.exp(jnp.sum(lf[2] * lf[3])) + lambda_init

    def block(qb):
        sc = jnp.einsum('bqhcd,bshcd->bhcqs', qb, k)
        p = _softmax_f32(sc)
        p = (p[:, :, 0] - lam * p[:, :, 1]).astype(v.dtype)
        return jnp.einsum('bhqs,bshe->bqhe', p, v)

    o = _sweep_query_blocks(block, q)
    o = _rms_norm(o, sub_g) * (1.0 - lambda_init)
    return o.reshape(b, s, B_HEADS * B_V_DIM)


def _mla(c_q, c_kv, k_r, q_g, kv_g, w_uq, w_ukv, ang):
    b, s = c_q.shape[:2]
    q = jnp.einsum('bsr,rn->bsn', _rms_norm(c_q, q_g), w_uq).reshape(b, s, C_HEADS, C_NOPE_DIM + C_ROPE_DIM)
    kv = jnp.einsum('bsr,rn->bsn', _rms_norm(c_kv, kv_g), w_ukv).reshape(b, s, C_HEADS, C_NOPE_DIM + C_V_DIM)
    scale = (C_NOPE_DIM + C_ROPE_DIM) ** -0.5
    q_nope = q[..., :C_NOPE_DIM] * scale
    q_rope = _rotate(q[..., C_NOPE_DIM:], ang) * scale
    k_nope, v = kv[..., :C_NOPE_DIM], kv[..., C_NOPE_DIM:]
    k_rope = _rotate(k_r[:, :, None, :], ang)[:, :, 0]

    def block(qn, qr):
        sc = jnp.einsum('bqhd,bshd->bhqs', qn, k_nope) + jnp.einsum('bqhr,bsr->bhqs', qr, k_rope)
        p = _softmax_f32(sc).astype(v.dtype)
        return jnp.einsum('bhqs,bshe->bqhe', p, v)

    o = _sweep_query_blocks(block, q_nope, q_rope)
    return o.reshape(b, s, C_HEADS * C_V_DIM)


def setup_inputs(seed: int = 0) -> dict:
    key = jax.random.key(seed)
    ks = jax.random.split(key, 20)
    f32 = jnp.float32

    def nrm(k, shape, scale):
        return jax.random.normal(k, shape, f32) * scale

    def gain(k, shape):
        return 1.0 + 0.02 * jax.random.normal(k, shape, f32)

    return {
        'x': jax.random.normal(ks[0], (BATCH, SEQ, D_MODEL), f32),
        'ln1_g': gain(ks[1], (DEPTH, D_MODEL)),
        'w_in': nrm(ks[2], (DEPTH, D_MODEL, IN_COLS), D_MODEL ** -0.5),
        'a_q_norm': gain(ks[3], (DEPTH, A_HEAD_DIM)),
        'a_k_norm': gain(ks[4], (DEPTH, A_HEAD_DIM)),
        'b_lambda': nrm(ks[5], (DEPTH, 4, B_QK_DIM), 0.1),
        'b_subln': gain(ks[6], (DEPTH, B_V_DIM)),
        'c_q_norm': gain(ks[7], (DEPTH, C_Q_RANK)),
        'c_kv_norm': gain(ks[8], (DEPTH, C_KV_RANK)),
        'c_w_uq': nrm(ks[9], (DEPTH, C_Q_RANK, C_HEADS * (C_NOPE_DIM + C_ROPE_DIM)), C_Q_RANK ** -0.5),
        'c_w_ukv': nrm(ks[10], (DEPTH, C_KV_RANK, C_HEADS * (C_NOPE_DIM + C_V_DIM)), C_KV_RANK ** -0.5),
        'w_branch': nrm(ks[11], (DEPTH, N_BRANCH, BRANCH_W, D_MODEL), BRANCH_W ** -0.5),
        'w_out': nrm(ks[12], (DEPTH, D_MODEL, D_MODEL), D_MODEL ** -0.5),
        'ln2_g': gain(ks[13], (DEPTH, D_MODEL)),
        'w_ff1': nrm(ks[14], (DEPTH, D_MODEL, D_FF), D_MODEL ** -0.5),
        'w_ff2': nrm(ks[15], (DEPTH, D_FF, D_MODEL), D_FF ** -0.5),
        'final_g': gain(ks[16], (D_MODEL,)),
    }


def reference(x, ln1_g, w_in, a_q_norm, a_k_norm, b_lambda, b_subln, c_q_norm, c_kv_norm,
              c_w_uq, c_w_ukv, w_branch, w_out, ln2_g, w_ff1, w_ff2, final_g):
    b, s, d = x.shape
    n_rows = s // GRID_W
    t = jnp.arange(s)
    row_idx = jnp.repeat(jnp.arange(n_rows), GRID_W)
    col_idx = jnp.tile(jnp.arange(GRID_W), n_rows)
    half = A_HEAD_DIM // 2
    row_ang = _angles(row_idx, half, AXIAL_THETA)
    col_ang = _angles(col_idx, half, AXIAL_THETA)
    b_ang = _angles(t, B_ROT_DIM, ROPE_THETA)
    c_ang = _angles(t, C_ROPE_DIM, MLA_THETA)

    for l in range(DEPTH):
        h = _rms_norm(x, ln1_g[l])
        proj = jnp.einsum('bsd,dn->bsn', h, w_in[l])
        (a_q, a_k, a_v, b_q, b_k, b_v, c_cq, c_ckv, c_kr, gate_logits) = _split_columns(proj)

        y_a = _gqa_axial(a_q.reshape(b, s, A_HEADS, A_HEAD_DIM),
                         a_k.reshape(b, s, A_KV_HEADS, A_HEAD_DIM),
                         a_v.reshape(b, s, A_KV_HEADS, A_HEAD_DIM),
                         a_q_norm[l], a_k_norm[l], row_ang, col_ang)
        lambda_init = 0.8 - 0.6 * math.exp(-0.3 * l)
        y_b = _diff_attention(b_q, b_k, b_v, b_lambda[l], b_subln[l], lambda_init, b_ang)
        y_c = _mla(c_cq, c_ckv, c_kr, c_q_norm[l], c_kv_norm[l], c_w_uq[l], c_w_ukv[l], c_ang)

        y = jnp.stack([y_a, y_b, y_c], axis=2)
        gates = jax.nn.sigmoid(gate_logits.reshape(b, s, N_BRANCH, d))
        merged = jnp.sum(gates * jnp.einsum('bsne,ned->bsnd', y, w_branch[l]), axis=2)
        x = x + jnp.einsum('bsd,de->bse', merged, w_out[l])

        h2 = _rms_norm(x, ln2_g[l])
        ff = jnp.square(jax.nn.relu(jnp.einsum('bsd,df->bsf', h2, w_ff1[l])))
        x = x + jnp.einsum('bsf,fd->bsd', ff, w_ff2[l])

    return _rms_norm(x, final_g)
```

```python
import contextlib
import math
import numpy as np
import concourse.bass as bass
import concourse.mybir as mybir
from concourse.bass_utils import run_bass_kernel_spmd

F32 = mybir.dt.float32
BF16 = mybir.dt.bfloat16
AF = mybir.ActivationFunctionType
ALU = mybir.AluOpType
AX = mybir.AxisListType

ENGS = ("pe", "act", "dve", "pool", "sp")
EW = "dve"


class Buf:
    __slots__ = ("name", "writers", "readers", "dsem", "dcnt", "psum")

    def __init__(self, name, psum=False):
        self.name = name
        self.psum = psum
        self.writers = {}
        self.readers = {}
        self.dsem = None
        self.dcnt = 0


class _Rec:
    def __init__(self):
        self.calls = []

    def __getattr__(self, name):
        def f(*a, **kw):
            self.calls.append((name, a, kw))
            return len(self.calls) - 1
        return f


class _Replay:
    __slots__ = ("calls", "ret")

    def __init__(self, fn):
        r = _Rec()
        self.ret = fn(r)
        self.calls = r.calls

    def __call__(self, eng):
        res = [getattr(eng, n)(*a, **kw) for (n, a, kw) in self.calls]
        if isinstance(self.ret, (list, tuple)):
            return [res[i] for i in self.ret]
        return res[self.ret]


class Op:
    __slots__ = ("eng", "fn", "deps", "marked", "done", "is_dma")

    def __init__(self, eng, fn):
        self.eng = eng
        self.fn = fn
        self.deps = []
        self.marked = False
        self.done = None
        self.is_dma = False


class Prog:
    def __init__(self, nc):
        self.nc = nc
        self.ops = {e: [] for e in ENGS}
        self.ndsem = 0
        self.stack = contextlib.ExitStack()
        self.nbuf = 0

    def sb(self, name, shape, dtype):
        return self.stack.enter_context(self.nc.sbuf_tensor(name, list(shape), dtype))

    def ps(self, name, shape, dtype):
        return self.stack.enter_context(self.nc.psum_tensor(name, list(shape), dtype))

    def buf(self, name=None, psum=False):
        self.nbuf += 1
        return Buf(name or ("b%d" % self.nbuf), psum)

    def _dep(self, op, prod):
        if prod is op:
            return
        if prod.eng == op.eng and not prod.is_dma:
            return
        op.deps.append(prod)
        if not prod.is_dma:
            prod.marked = True

    def op(self, eng, fn, reads=(), writes=(), dma=False, ndma=1):
        o = Op(eng, _Replay(fn))
        o.is_dma = dma
        for b in reads:
            for w in b.writers.values():
                if w.eng == eng and not w.is_dma:
                    if eng != "pe" and not dma:
                        o.deps.append(w)
                        w.marked = True
                else:
                    self._dep(o, w)
            if b.psum:
                for r in b.readers.values():
                    self._dep(o, r)
        for b in writes:
            for r in b.readers.values():
                self._dep(o, r)
            for w in b.writers.values():
                self._dep(o, w)
        for b in reads:
            b.readers[("dma%d" % id(o)) if dma else eng] = o
        for b in writes:
            b.readers.clear()
            b.writers.clear()
            b.writers[("dma" if dma else eng)] = o
        if dma:
            tgt = writes[0] if writes else reads[0]
            if tgt.dsem is None:
                tgt.dsem = "d%d" % self.ndsem
                self.ndsem += 1
            tgt.dcnt += 16 * ndma
            o.done = (tgt.dsem, tgt.dcnt)
        self.ops[eng].append(o)
        return o

    def barrier_op(self, eng, deps):
        o = Op(eng, None)
        for d in deps:
            if d is None:
                continue
            o.deps.append(d)
            if not d.is_dma:
                d.marked = True
        self.ops[eng].append(o)
        return o

    def barrier(self, engs=ENGS):
        engs = [e for e in engs if e in self.ops]
        last = {}
        for e in engs:
            lst = [o for o in self.ops[e] if o.fn is not None]
            last[e] = lst[-1] if lst else None
        for e in engs:
            self.barrier_op(e, [last[f] for f in engs if f != e])

    def emit(self):
        nc = self.nc
        st = self.stack
        sems = {}
        for e in ENGS:
            sems[e] = st.enter_context(nc.semaphore("s_" + e))
        for i in range(self.ndsem):
            sems["d%d" % i] = st.enter_context(nc.semaphore("sd%d" % i))
        for e in ENGS:
            c = 0
            for o in self.ops[e]:
                if o.is_dma:
                    continue
                if o.marked and o.fn is not None:
                    c += 1
                    o.done = (e, c)
                elif o.fn is None:
                    o.done = (e, c)
        self.stats = {e: len(self.ops[e]) for e in ENGS}

        def run(e, eng):
            waited = {}
            for o in self.ops[e]:
                need = {}
                for d in o.deps:
                    k, v = d.done
                    if v > need.get(k, 0):
                        need[k] = v
                for k, v in need.items():
                    if v > waited.get(k, 0):
                        eng.wait_ge(sems[k], v)
                        waited[k] = v
                if o.fn is None:
                    continue
                ins = o.fn(eng)
                if o.is_dma:
                    for i_ in (ins if isinstance(ins, (list, tuple)) else [ins]):
                        i_.then_inc(sems[o.done[0]], 16)
                elif o.marked:
                    ins.then_inc(sems[e], 1)

        with nc.Block() as block:
            @block.tensor
            def _(t):
                run("pe", t)

            @block.scalar
            def _(s):
                run("act", s)

            @block.vector
            def _(v):
                run("dve", v)

            @block.gpsimd
            def _(g):
                run("pool", g)

            @block.sync
            def _(s):
                run("sp", s)
        st.close()


D = 1024
S = 2048
DEPTH = 4
NSEQ = 2
CH = 512
NCH = S // CH
IN_COLS = 6048
EPS = 1e-6
C_AQ, C_AK, C_AV = 0, 512, 640
C_BQ, C_BK, C_BV = 768, 1280, 1792
C_CQ, C_CKV, C_KR = 2304, 2688, 2944
C_G = 2976
NVL = 152
M_ONES, M_BLK64, M_RA, M_RB, M_RC, M_SEL, M_COMB = range(7)


def lambda_init(l):
    return 0.8 - 0.6 * math.exp(-0.3 * l)


class _Stop(Exception):
    pass


def build_program(depth=DEPTH, nseq=NSEQ, wring=2, debug=False, stop_at=None):
    nc = bass.Bass("TRN2", target_bir_lowering=False)
    P = Prog(nc)

    def din(name, shape):
        return nc.dram_tensor(name, list(shape), F32, kind="ExternalInput").ap()

    x_d = din("x", [nseq, S, D])
    w_in_d = din("w_in", [DEPTH, D, IN_COLS])
    w_kdup_d = din("w_kdup", [DEPTH, D, 256])
    w_uq_d = din("c_w_uq", [DEPTH, 384, 768])
    w_uk_d = din("w_uk96", [DEPTH, 256, 768])
    w_ukv_d = din("c_w_ukv", [DEPTH, 256, 1024])
    w_br_d = din("w_branch", [DEPTH, 3, 512, D])
    w_out_d = din("w_out", [DEPTH, D, D])
    w_ff1_d = din("w_ff1", [DEPTH, D, 4 * D])
    w_ff2_d = din("w_ff2", [DEPTH, 4 * D, D])
    pvec_d = din("pvec", [128, DEPTH * NVL + 8])
    cmat_d = din("cmat", [128, 7 * 128])
    ident_d = din("ident", [128, 128])
    tabA_d = din("tabA", [128, 2, S])
    tabB_d = din("tabB", [128, 2, S])
    tabC_d = din("tabC", [128, 2, S])
    out_d = nc.dram_tensor("out", [nseq, S, D], F32, kind="ExternalOutput").ap()

    xT = P.sb("xT", [128, 8, S], F32)
    yT = P.sb("yT", [128, 12, S], BF16)
    pvec = P.sb("pvec_sb", [128, DEPTH * NVL + 8], F32)
    cmat = P.sb("cmat_sb", [128, 7 * 128], BF16)
    ident = P.sb("ident_sb", [128, 128], F32)
    cst = P.sb("cst", [128, 8], F32)
    lam = P.sb("lam", [128, 8], F32)
    rstd1 = P.sb("rstd1", [128, S], F32)
    hT = P.sb("hT", [128, 8, CH], BF16)
    wr = [P.sb("wr%d" % i, [128, 8, 512], BF16) for i in range(wring)]
    SCR = 52224
    scr = P.sb("scr", [128, SCR // 2], BF16)
    ps = P.ps("ps", [128, 8, 512], F32)

    off = [0]

    def carve(nbytes):
        o = off[0]
        off[0] += nbytes
        assert off[0] <= SCR, (off[0], SCR)
        return o

    def v_bf(o, shape):
        n = int(np.prod(shape))
        a = scr[:, o // 2:o // 2 + n]
        if len(shape) == 2:
            return a.rearrange("p (a b) -> p a b", b=shape[1])
        return a

    def v_f32(o, shape):
        n = int(np.prod(shape))
        a = scr[:, o // 2:o // 2 + 2 * n].bitcast(F32)
        if len(shape) == 2:
            return a.rearrange("p (a b) -> p a b", b=shape[1])
        return a

    off[0] = 0
    KT = v_bf(carve(8192), [2, S])
    Vv = v_bf(carve(8192), [16, 256])
    qTall = v_bf(carve(8192), [2, S])
    PT = [v_bf(carve(2048), [2, CH]) for _ in range(3)]
    tab = [v_f32(carve(4096), [2, CH]) for _ in range(2)]
    w_ln = v_f32(carve(2048), [CH])
    w_rs = v_f32(carve(2048), [CH])
    w_t = v_f32(carve(2048), [CH])
    w_u = v_f32(carve(2048), [CH])
    w_r = v_f32(carve(2048), [CH])
    w_sq = v_bf(carve(1024), [CH])
    w_qn = v_bf(carve(1024), [CH])
    w_yn = v_bf(carve(1024), [CH])
    att_end = off[0]
    off[0] = 0
    mT = v_bf(carve(8192), [8, CH])
    hid = v_bf(carve(32768), [32, CH])
    Gs = [v_f32(carve(2048), [CH]) for _ in range(2)]
    tmpm = v_f32(carve(2048), [CH])
    rl = Gs
    w_ln2 = v_f32(carve(2048), [CH])
    w_rs2 = v_f32(carve(2048), [CH])
    loc_end = off[0]
    macc = scr[:, (8192 // 2):(8192 // 2) + 4096].bitcast(F32).rearrange("p (a b) -> p a b", b=CH)
    sq8 = scr[:, (16384 // 2):(16384 // 2) + 4096].rearrange("p (a b) -> p a b", b=CH)
    off[0] = 0
    stage = [v_f32(carve(4096), [D]) for _ in range(2)]
    ofin = v_f32(carve(16384), [8, CH])
    sqf = v_bf(carve(8192), [8, CH])
    w_ln3 = v_f32(carve(2048), [CH])
    w_rs3 = v_f32(carve(2048), [CH])

    bank = [P.buf("bank%d" % i, psum=True) for i in range(8)]
    b_xT = [P.buf("xT%d" % c) for c in range(NCH)]
    b_y = [[P.buf("y%d_%d" % (n, c)) for c in range(NCH)] for n in range(3)]
    b_const = P.buf("const")
    b_pv = P.buf("pvec")
    b_cst = P.buf("cst")
    b_lam = P.buf("lam")
    b_rstd1 = [P.buf("rstd1_%d" % c) for c in range(NCH)]
    b_hT = P.buf("hT")
    b_wr = [P.buf("wr%d" % i) for i in range(wring)]
    b_KT = [P.buf("KT%d" % c) for c in range(NCH)]
    b_V = [P.buf("V%d" % c) for c in range(NCH)]
    b_q = [P.buf("q%d" % c) for c in range(NCH)]
    b_PT = [P.buf("PT%d" % i) for i in range(3)]
    b_tab = [P.buf("tab%d" % i) for i in range(2)]
    b_wln, b_wrs, b_wt, b_wu, b_wrr, b_wsq, b_wqn, b_wyn = [P.buf() for _ in range(8)]
    b_mT = P.buf("mT")
    b_hid = [P.buf("hid%d" % i) for i in range(4)]
    b_Gs = [P.buf(), P.buf()]
    b_tmpm = P.buf()
    b_rl = b_Gs
    b_wln2, b_wrs2 = P.buf(), P.buf()
    b_stage = [P.buf(), P.buf()]
    b_ofin, b_sqf, b_wln3, b_wrs3 = P.buf(), P.buf(), P.buf(), P.buf()

    def cm(i, rows=128, cols=128):
        return cmat[0:rows, i * 128:i * 128 + cols]

    gen_banks = [6, 7]
    gb_i = [0]

    def gbank():
        b = gen_banks[gb_i[0] % len(gen_banks)]
        gb_i[0] += 1
        return b

    wr_i = [0]

    def load_w(parts):
        s = wr_i[0] % wring
        wr_i[0] += 1
        slot = wr[s]

        def fn(e, parts=parts, slot=slot):
            res = []
            for p_ in parts:
                k0, kc, c0, ncol, ap = p_[:5]
                dst = slot[:, k0:k0 + kc, c0:c0 + ncol]
                if len(p_) > 5:
                    dst = p_[5](dst)
                res.append(e.dma_start(out=dst, in_=ap))
            return res
        P.op("pool", fn, writes=[b_wr[s]], dma=True, ndma=len(parts))
        return slot, b_wr[s]

    def wv(ap2d):
        return ap2d.rearrange("(k p) n -> p k n", p=128)

    def mm_group(bk, out_ap, pairs, reads, tp=None):
        def fn(e, pairs=pairs, out_ap=out_ap, tp=tp):
            n = len(pairs)
            ins = None
            for i, (l, r) in enumerate(pairs):
                if tp is None:
                    ins = e.matmul(out_ap, lhsT=l, rhs=r, start=(i == 0), stop=(i == n - 1))
                else:
                    ins = e.matmul(out_ap, lhsT=l, rhs=r, start=(i == 0), stop=(i == n - 1), tile_position=tp)
            return ins
        return P.op("pe", fn, reads=reads, writes=[bank[bk]])

    def rstd_from_bank(bk, np_, scale, out_ap, b_out, ln_ap, b_ln, bias_col=None, extra_reads=()):
        P.op("act", lambda e: e.activation(out=ln_ap[0:np_], in_=ps[0:np_, bk, :], func=AF.Ln, bias=cst[0:np_, 0:1], scale=scale),
             reads=[bank[bk], b_cst], writes=[b_ln])
        if bias_col is None:
            P.op("act", lambda e: e.activation(out=out_ap[0:np_], in_=ln_ap[0:np_], func=AF.Exp, scale=-0.5),
                 reads=[b_ln], writes=[b_out])
        else:
            P.op("act", lambda e: e.activation(out=out_ap[0:np_], in_=ln_ap[0:np_], func=AF.Exp, scale=-0.5, bias=bias_col),
                 reads=[b_ln, b_cst], writes=[b_out])

    tab_i = [0]

    def load_tab(src, c, np_=128, p0=0):
        i = tab_i[0] % 2
        tab_i[0] += 1
        t = tab[i]
        P.op("sp", lambda e: e.dma_start(out=t[0:np_], in_=src[p0:p0 + np_, :, c * CH:(c + 1) * CH]), writes=[b_tab[i]], dma=True)
        return t, b_tab[i]

    def rotary(np_, qn_reads, rmat, t, b_t, out_ap, b_out, krows=None):
        bk = gbank()
        k0, k1 = krows if krows is not None else (0, np_)
        mm_group(bk, ps[0:np_, bk, :], [(rmat, w_qn[k0:k1])], reads=[b_wqn, b_const])
        P.op("dve", lambda e: e.tensor_tensor(out=w_t[0:np_], in0=ps[0:np_, bk, :], in1=t[0:np_, 1, :], op=ALU.mult),
             reads=[bank[bk], b_t], writes=[b_wt])
        P.op(EW, lambda e: e.tensor_tensor(out=w_u[0:np_], in0=w_qn[0:np_], in1=t[0:np_, 0, :], op=ALU.mult),
             reads=[b_wqn, b_t], writes=[b_wu])
        P.op(EW, lambda e: e.tensor_tensor(out=out_ap, in0=w_u[0:np_], in1=w_t[0:np_], op=ALU.add),
             reads=[b_wu, b_wt], writes=[b_out])

    def make_hT(c, gcol0, rstd_ap, b_rstd, l):
        for k in range(8):
            P.op("dve", lambda e, k=k: e.scalar_tensor_tensor(out=hT[:, k, :], in0=xT[:, k, c * CH:(c + 1) * CH],
                                                              scalar=pvec[:, l * NVL + gcol0 + k:l * NVL + gcol0 + k + 1],
                                                              in1=rstd_ap, op0=ALU.mult, op1=ALU.mult),
                 reads=[b_xT[c], b_rstd, b_pv], writes=[b_hT])

    def norm_stats(c, sq_ap, b_sq_list, out_ap, b_out, ln_ap, b_ln):
        P.op("act", lambda e: e.activation(out=sq_ap, in_=xT[:, :, c * CH:(c + 1) * CH], func=AF.Square),
             reads=[b_xT[c]], writes=b_sq_list)
        bk = gbank()
        mm_group(bk, ps[:, bk, :], [(cm(M_ONES), sq_ap[:, k, :]) for k in range(8)], reads=b_sq_list + [b_const])
        rstd_from_bank(bk, 128, 1.0 / D, out_ap, b_out, ln_ap, b_ln)

    def attention(pairs, scale):
        units = [(pi, kb) for pi in range(len(pairs)) for kb in range(16)]

        def qk(u):
            pi, kb = units[u]
            pr = pairs[pi]
            s = u % 2

            def fn(e, pr=pr, kb=kb, s=s):
                ins = None
                for m in range(2):
                    kw = {}
                    if pr["tp"] is not None:
                        kw["tile_position"] = pr["tp"][m]
                    ins = e.matmul(ps[:, 2 * s + m, :], lhsT=pr["k"](m, kb), rhs=pr["q"][m], start=True, stop=True, **kw)
                return ins
            P.op("pe", fn, reads=pr["reads"], writes=[bank[2 * s], bank[2 * s + 1]])

        def ex_pv(u):
            pi, kb = units[u]
            pr = pairs[pi]
            s = u % 2
            pt = PT[u % 3]
            bpt = b_PT[u % 3]
            P.op("act", lambda e: e.activation(out=pt, in_=ps[:, 2 * s:2 * s + 2, :], func=AF.Exp, scale=scale),
                 reads=[bank[2 * s], bank[2 * s + 1]], writes=[bpt])

            def fn(e, pr=pr, kb=kb, pt=pt):
                st_, sp_ = (kb == 0), (kb == 15)
                e.matmul(ps[0:64, 4, :], lhsT=pr["v"](0, kb), rhs=pt[:, 0, :], start=st_, stop=sp_)
                e.matmul(ps[64:128, 4, :], lhsT=pr["v"](1, kb), rhs=pt[:, 1, :], start=st_, stop=sp_, tile_position=(0, 64))
                e.matmul(ps[0:64, 5, :], lhsT=cm(M_ONES, 128, 64), rhs=pt[:, 0, :], start=st_, stop=sp_)
                return e.matmul(ps[64:128, 5, :], lhsT=cm(M_ONES, 128, 64), rhs=pt[:, 1, :], start=st_, stop=sp_,
                                tile_position=(0, 64))
            P.op("pe", fn, reads=[bpt, b_const] + pr["reads"], writes=[bank[4], bank[5]])
            if kb == 15:
                pr["finish"]()

        n = len(units)
        for u in range(n):
            qk(u)
            if u >= 1:
                ex_pv(u - 1)
        ex_pv(n - 1)

    def finish_plain(out_ap, b_out):
        def f():
            P.op("dve", lambda e: e.reciprocal(out=w_r, in_=ps[:, 5, :]), reads=[bank[5]], writes=[b_wrr])
            P.op("dve", lambda e: e.tensor_tensor(out=out_ap, in0=ps[:, 4, :], in1=w_r, op=ALU.mult),
                 reads=[bank[4], b_wrr], writes=[b_out])
        return f

    P.op("sp", lambda e: [e.dma_start(out=pvec[:], in_=pvec_d), e.dma_start(out=ident[:], in_=ident_d)],
         writes=[b_pv], dma=True, ndma=2)
    P.op("pool", lambda e: e.dma_start(out=cmat[:], in_=cmat_d), writes=[b_const], dma=True)
    P.op("dve", lambda e: e.memset(cst[:, 0:1], EPS), writes=[b_cst])
    for l in range(DEPTH):
        P.op("dve", lambda e, l=l: e.memset(cst[:, 1 + l:2 + l], math.log(1.0 - lambda_init(l))), writes=[b_cst])

    def dump_and_stop(tag):
        if stop_at != tag:
            return
        P.barrier()
        outs = []
        for nm, t_, shp, dt_ in (("dbg_KT", KT, [128, 2, S], BF16), ("dbg_q", qTall, [128, 2, S], BF16), ("dbg_V", Vv, [128, 16, 256], BF16),
                                 ("dbg_y", yT[:], [128, 12, S], BF16), ("dbg_wqn", w_qn, [128, CH], BF16), ("dbg_wt", w_t, [128, CH], F32),
                                 ("dbg_wu", w_u, [128, CH], F32)):
            dd = nc.dram_tensor(nm, shp, dt_, kind="ExternalOutput").ap()
            outs.append((dd, t_))
        b_dbg = P.buf("dbg")
        d1 = P.op("sp", lambda e: [e.dma_start(out=a_, in_=b_) for a_, b_ in outs], reads=[b_dbg], dma=True, ndma=len(outs))
        P.barrier_op("sp", [d1])
        raise _Stop()

    try:
      for seq in range(nseq):
          P.barrier()
          gen_banks[:] = [0, 1, 2, 3, 4, 5, 6, 7]
          for t in range(16):
              sgi = t % 2
              P.op("sp", lambda e, t=t, sgi=sgi: e.dma_start(out=stage[sgi], in_=x_d[seq, t * 128:(t + 1) * 128, :]),
                   writes=[b_stage[sgi]], dma=True)
              for h in range(2):
                  bk = gbank()

                  def fn(e, sgi=sgi, h=h, bk=bk):
                      ins = None
                      for j in range(4):
                          ins = e.transpose(out=ps[:, bk, j * 128:(j + 1) * 128], in_=stage[sgi][:, (4 * h + j) * 128:(4 * h + j + 1) * 128],
                                            identity=ident[:])
                      return ins
                  P.op("pe", fn, reads=[b_stage[sgi], b_pv], writes=[bank[bk]])
                  eng = "act" if h == 0 else "dve"
                  if eng == "act":
                      P.op("act", lambda e, t=t, h=h, bk=bk: e.copy(out=xT[:, 4 * h:4 * h + 4, t * 128:(t + 1) * 128],
                                                                    in_=ps[:, bk, :].rearrange("p (a b) -> p a b", b=128)),
                           reads=[bank[bk]], writes=[b_xT[t // 4]])
                  else:
                      P.op("dve", lambda e, t=t, h=h, bk=bk: e.tensor_copy(out=xT[:, 4 * h:4 * h + 4, t * 128:(t + 1) * 128],
                                                                           in_=ps[:, bk, :].rearrange("p (a b) -> p a b", b=128)),
                           reads=[bank[bk]], writes=[b_xT[t // 4]])
          P.barrier()

          for l in range(depth):
              pv0 = l * NVL
              gen_banks[:] = [6, 7]
              bl = pv0 + 24
              P.op("dve", lambda e, bl=bl: e.tensor_tensor(out=w_t[:, 0:32],
                                                          in0=pvec[:, bl:bl + 32], in1=pvec[:, bl + 32:bl + 64], op=ALU.mult),
                   reads=[b_pv], writes=[b_wt])
              P.op("dve", lambda e, bl=bl: e.tensor_tensor(out=w_t[:, 32:64], in0=pvec[:, bl + 64:bl + 96], in1=pvec[:, bl + 96:bl + 128],
                                                          op=ALU.mult), reads=[b_pv], writes=[b_wt])
              P.op("dve", lambda e: e.reduce_sum(out=lam[:, 1:3], in_=w_t[:, 0:64].rearrange("p (a b) -> p a b", b=32), axis=AX.X),
                   reads=[b_wt], writes=[b_lam])
              P.op("act", lambda e: e.activation(out=lam[:, 3:5], in_=lam[:, 1:3], func=AF.Exp), reads=[b_lam], writes=[b_lam])
              P.op("dve", lambda e: e.tensor_tensor(out=lam[:, 5:6], in0=lam[:, 3:4], in1=lam[:, 4:5], op=ALU.subtract),
                   reads=[b_lam], writes=[b_lam])
              P.op("dve", lambda e: e.memset(lam[:, 0:1], 1.0), reads=[b_lam], writes=[b_lam])
              P.op("dve", lambda e, l=l: e.tensor_scalar(out=lam[64:128, 0:1], in0=lam[64:128, 5:6], scalar1=lambda_init(l), scalar2=None,
                                                         op0=ALU.add), reads=[b_lam], writes=[b_lam])

              for c in range(NCH):
                  norm_stats(c, KT[:, 0:2, :].rearrange("p a (c t) -> p (a c) t", t=CH), b_KT,
                             rstd1[:, c * CH:(c + 1) * CH], b_rstd1[c], w_ln, b_wln)

              def cqT(j, c):
                  return yT[:, 4 + j, c * CH:(c + 1) * CH]

              def ckvT(j, c):
                  return yT[:, j, c * CH:(c + 1) * CH]

              def krT(c):
                  return yT[0:32, 2, c * CH:(c + 1) * CH]

              for c in range(NCH):
                  make_hT(c, 0, rstd1[:, c * CH:(c + 1) * CH], b_rstd1[c], l)
                  s1, bs1 = load_w([(0, 8, 0, 512, wv(w_in_d[l, :, C_CQ:C_CQ + 512]))])
                  s2, bs2 = load_w([(0, 8, 0, 160, wv(w_in_d[l, :, C_CQ + 512:C_CQ + 672]))])

                  def wcol(j):
                      if j < 4:
                          return s1, bs1, j * 128
                      return s2, bs2, (j - 4) * 128
                  for (j0, nj, gcol, dst, bdst, ncols) in ((0, 3, 19, cqT, b_y[1][c], 384), (3, 2, 22, ckvT, b_y[0][c], 256)):
                      for jj in range(nj):
                          sl, bsl, co = wcol(j0 + jj)
                          bk = gbank()
                          mm_group(bk, ps[:, bk, :], [(sl[:, k, co:co + 128], hT[:, k, :]) for k in range(8)], reads=[bsl, b_hT])
                          P.op("act", lambda e, bk=bk: e.activation(out=w_sq, in_=ps[:, bk, :], func=AF.Square),
                               reads=[bank[bk]], writes=[b_wsq])
                          P.op("dve", lambda e, bk=bk, jj=jj, gcol=gcol, dst=dst: e.tensor_scalar(
                              out=dst(jj, c), in0=ps[:, bk, :], scalar1=pvec[:, pv0 + gcol + jj:pv0 + gcol + jj + 1], scalar2=None,
                              op0=ALU.mult), reads=[bank[bk], b_pv], writes=[bdst])

                          def fn(e, jj=jj, nj=nj):
                              return e.matmul(ps[:, 5, :], lhsT=cm(M_ONES), rhs=w_sq, start=(jj == 0), stop=(jj == nj - 1))
                          P.op("pe", fn, reads=[b_wsq, b_const], writes=[bank[5]])
                      rstd_from_bank(5, 128, 1.0 / ncols, w_rs, b_wrs, w_ln, b_wln)
                      for jj in range(nj):
                          P.op("dve", lambda e, jj=jj, dst=dst: e.tensor_tensor(out=dst(jj, c), in0=dst(jj, c), in1=w_rs, op=ALU.mult),
                               reads=[bdst, b_wrs], writes=[bdst])
                  sl, bsl, co = s2, bs2, 128
                  bk = gbank()
                  mm_group(bk, ps[0:32, bk, :], [(sl[:, k, co:co + 32], hT[:, k, :]) for k in range(8)], reads=[bsl, b_hT])
                  P.op("act", lambda e, bk=bk: e.copy(out=w_qn[0:32], in_=ps[0:32, bk, :]), reads=[bank[bk]], writes=[b_wqn])
                  t, bt = load_tab(tabC_d, c, 32, 64)
                  rotary(32, None, cmat[0:32, M_RA * 128:M_RA * 128 + 32], t, bt, krT(c), b_y[0][c])

              P.op("dve", lambda e: e.memset(KT[64:128, :, :], 0.0), writes=b_KT)
              P.op("dve", lambda e: e.memset(qTall[64:128, :, :], 0.0), writes=b_q)
              for g in range(4):
                  sl, bsl = load_w([
                      (0, 3, 0, 192, wv(w_uq_d[l, :, g * 192:(g + 1) * 192])),
                      (3, 2, 0, 192, wv(w_uk_d[l, :, g * 192:(g + 1) * 192])),
                      (5, 2, 0, 64, wv(w_ukv_d[l, :, (2 * g) * 128 + 64:(2 * g) * 128 + 128])),
                      (5, 2, 64, 64, wv(w_ukv_d[l, :, (2 * g + 1) * 128 + 64:(2 * g + 1) * 128 + 128])),
                  ])
                  for c in range(NCH):
                      for hh in range(2):
                          bk = gbank()
                          prs = [(sl[:, 3 + kk, hh * 96:(hh + 1) * 96], ckvT(kk, c)) for kk in range(2)]
                          prs.append((cm(M_SEL, 32, 96), krT(c)))

                          def fn(e, prs=prs, bk=bk):
                              e.matmul(ps[0:96, bk, :], lhsT=prs[0][0], rhs=prs[0][1], start=True, stop=False)
                              e.matmul(ps[0:96, bk, :], lhsT=prs[1][0], rhs=prs[1][1], start=False, stop=False)
                              return e.matmul(ps[0:96, bk, :], lhsT=prs[2][0], rhs=prs[2][1], start=False, stop=True)
                          P.op("pe", fn, reads=[bsl, b_y[0][c], b_const], writes=[bank[bk]])
                          P.op("act", lambda e, bk=bk, hh=hh, c=c: e.copy(out=KT[0:96, hh, c * CH:(c + 1) * CH], in_=ps[0:96, bk, :]),
                               reads=[bank[bk]], writes=[b_KT[c]])
                      bk = gbank()

                      def fnv(e, c=c, bk=bk, sl=sl):
                          ins = None
                          for tt in range(4):
                              for kk in range(2):
                                  ins = e.matmul(ps[:, bk, tt * 128:(tt + 1) * 128],
                                                 lhsT=yT[:, kk, c * CH + tt * 128:c * CH + (tt + 1) * 128],
                                                 rhs=sl[:, 5 + kk, 0:128], start=(kk == 0), stop=(kk == 1))
                          return ins
                      P.op("pe", fnv, reads=[bsl, b_y[0][c]], writes=[bank[bk]])
                      P.op("act", lambda e, bk=bk, c=c: e.copy(out=Vv[:, 4 * c:4 * c + 4, 0:128],
                                                               in_=ps[:, bk, :].rearrange("p (a b) -> p a b", b=128)),
                           reads=[bank[bk]], writes=[b_V[c]])
                  for c in range(NCH):
                      t, bt = load_tab(tabC_d, c, 96, 0)
                      for hh in range(2):
                          bk = gbank()
                          mm_group(bk, ps[0:96, bk, :], [(sl[:, kk, hh * 96:(hh + 1) * 96], cqT(kk, c)) for kk in range(3)],
                                   reads=[bsl, b_y[1][c]])
                          P.op("act", lambda e, bk=bk: e.copy(out=w_qn[0:96], in_=ps[0:96, bk, :]), reads=[bank[bk]], writes=[b_wqn])
                          rotary(96, None, cmat[64:96, M_RC * 128:M_RC * 128 + 96], t, bt, qTall[0:96, hh, c * CH:(c + 1) * CH], b_q[c],
                                 krows=(64, 96))
                      pair = dict(
                          q=(qTall[:, 0, c * CH:(c + 1) * CH], qTall[:, 1, c * CH:(c + 1) * CH]),
                          k=lambda m, kb: KT[:, m, kb * 128:(kb + 1) * 128],
                          v=lambda m, kb: Vv[:, kb, m * 64:(m + 1) * 64],
                          tp=None, reads=[b_q[c]] + b_KT + b_V,
                          finish=finish_plain(yT[:, 8 + g, c * CH:(c + 1) * CH], b_y[2][c]))
                      attention([pair], 96.0 ** -0.5)
                      dump_and_stop("C0")

              for gb_ in range(2):
                  sl, bsl = load_w([(0, 8, 0, 256, wv(w_in_d[l, :, C_BQ + 256 * gb_:C_BQ + 256 * gb_ + 256])),
                                    (0, 8, 256, 256, wv(w_in_d[l, :, C_BK + 256 * gb_:C_BK + 256 * gb_ + 256]))])
                  sv, bsv = load_w([(0, 8, 0, 256, wv(w_in_d[l, :, C_BV + 256 * gb_:C_BV + 256 * gb_ + 256]))])
                  for c in range(NCH):
                      make_hT(c, 0, rstd1[:, c * CH:(c + 1) * CH], b_rstd1[c], l)
                      t, bt = load_tab(tabB_d, c)
                      for (co, dst, bdst) in ((0, qTall, b_q[c]), (256, KT, b_KT[c])):
                          for i in range(2):
                              bk = gbank()
                              mm_group(bk, ps[:, bk, :], [(sl[:, k, co + i * 128:co + (i + 1) * 128], hT[:, k, :]) for k in range(8)],
                                       reads=[bsl, b_hT])
                              P.op("act", lambda e, bk=bk: e.copy(out=w_qn, in_=ps[:, bk, :]), reads=[bank[bk]], writes=[b_wqn])
                              rotary(128, None, cm(M_RB), t, bt, dst[:, i, c * CH:(c + 1) * CH], bdst)
                      for tt in range(4):
                          bk = gbank()
                          mm_group(bk, ps[:, bk, 0:256], [(hT[:, k, tt * 128:(tt + 1) * 128], sv[:, k, 0:256]) for k in range(8)],
                                   reads=[bsv, b_hT])
                          P.op("act", lambda e, bk=bk, tt=tt, c=c: e.copy(out=Vv[:, 4 * c + tt, 0:256], in_=ps[:, bk, 0:256]),
                               reads=[bank[bk]], writes=[b_V[c]])
                  for c in range(NCH):
                      pairs = []
                      for hh in range(4):
                          i = hh // 2
                          base = 64 * (hh % 2)

                          def fin(hh=hh, i=i, c=c, gb_=gb_):
                              P.op("dve", lambda e: e.reciprocal(out=w_r, in_=ps[:, 5, :]), reads=[bank[5]], writes=[b_wrr])
                              P.op("dve", lambda e: e.scalar_tensor_tensor(out=w_yn, in0=ps[:, 4, :], scalar=lam[:, 0:1], in1=w_r,
                                                                           op0=ALU.mult, op1=ALU.mult),
                                   reads=[bank[4], b_wrr, b_lam], writes=[b_wyn])
                              half = hh % 2

                              def fz(e, half=half):
                                  if half == 0:
                                      return e.matmul(ps[0:64, 6, :], lhsT=cm(M_COMB, 128, 64), rhs=w_yn, start=True, stop=True)
                                  return e.matmul(ps[64:128, 6, :], lhsT=cm(M_COMB, 128, 64), rhs=w_yn, start=True, stop=True,
                                                  tile_position=(0, 64))
                              P.op("pe", fz, reads=[b_wyn, b_const], writes=[bank[6]])
                              if half == 1:
                                  P.op("act", lambda e: e.activation(out=w_sq, in_=ps[:, 6, :], func=AF.Square),
                                       reads=[bank[6]], writes=[b_wsq])
                                  mm_group(7, ps[:, 7, :], [(cm(M_BLK64), w_sq)], reads=[b_wsq, b_const])
                                  rstd_from_bank(7, 128, 1.0 / 64, w_rs, b_wrs, w_ln, b_wln, bias_col=cst[:, 1 + l:2 + l])
                                  P.op("dve", lambda e: e.scalar_tensor_tensor(
                                      out=yT[:, 4 + 2 * gb_ + i, c * CH:(c + 1) * CH], in0=ps[:, 6, :],
                                      scalar=pvec[:, pv0 + 18:pv0 + 19], in1=w_rs, op0=ALU.mult, op1=ALU.mult),
                                      reads=[bank[6], b_wrs, b_pv], writes=[b_y[1][c]])
                          pairs.append(dict(
                              q=(qTall[base:base + 32, i, c * CH:(c + 1) * CH], qTall[base + 32:base + 64, i, c * CH:(c + 1) * CH]),
                              k=lambda m, kb, i=i, base=base: KT[base + 32 * m:base + 32 * m + 32, i, kb * 128:(kb + 1) * 128],
                              v=lambda m, kb, hh=hh: Vv[:, kb, hh * 64:(hh + 1) * 64],
                              tp=((base, 0), (base + 32, 0)), reads=[b_q[c]] + b_KT + b_V, finish=fin))
                      attention(pairs, 32.0 ** -0.5)

              for ga in range(2):
                  sl, bsl = load_w([(0, 8, 0, 256, wv(w_in_d[l, :, C_AQ + 256 * ga:C_AQ + 256 * ga + 256])),
                                    (0, 8, 256, 128, wv(w_kdup_d[l, :, 128 * ga:128 * ga + 128])),
                                    (0, 8, 384, 64, wv(w_in_d[l, :, C_AV + 64 * ga:C_AV + 64 * ga + 64]))])
                  for c in range(NCH):
                      make_hT(c, 0, rstd1[:, c * CH:(c + 1) * CH], b_rstd1[c], l)
                      t, bt = load_tab(tabA_d, c)
                      for (co, gcol, dst, bdst) in ((0, 16, qTall[:, 0, c * CH:(c + 1) * CH], b_q[c]),
                                                   (128, 16, qTall[:, 1, c * CH:(c + 1) * CH], b_q[c]),
                                                   (256, 17, KT[:, 0, c * CH:(c + 1) * CH], b_KT[c])):
                          bk = gbank()
                          mm_group(bk, ps[:, bk, :], [(sl[:, k, co:co + 128], hT[:, k, :]) for k in range(8)], reads=[bsl, b_hT])
                          P.op("act", lambda e, bk=bk: e.activation(out=w_sq, in_=ps[:, bk, :], func=AF.Square),
                               reads=[bank[bk]], writes=[b_wsq])
                          bk2 = gbank()
                          mm_group(bk2, ps[:, bk2, :], [(cm(M_BLK64), w_sq)], reads=[b_wsq, b_const])
                          rstd_from_bank(bk2, 128, 1.0 / 64, w_rs, b_wrs, w_ln, b_wln)
                          P.op("dve", lambda e, bk=bk, gcol=gcol: e.scalar_tensor_tensor(
                              out=w_qn, in0=ps[:, bk, :], scalar=pvec[:, pv0 + gcol:pv0 + gcol + 1], in1=w_rs,
                              op0=ALU.mult, op1=ALU.mult), reads=[bank[bk], b_wrs, b_pv], writes=[b_wqn])
                          rotary(128, None, cm(M_RA), t, bt, dst, bdst)
                      bk = gbank()

                      def fnv(e, bk=bk, sl=sl):
                          ins = None
                          for tt in range(4):
                              for k in range(8):
                                  ins = e.matmul(ps[:, bk, tt * 64:(tt + 1) * 64], lhsT=hT[:, k, tt * 128:(tt + 1) * 128],
                                                 rhs=sl[:, k, 384:448], start=(k == 0), stop=(k == 7))
                          return ins
                      P.op("pe", fnv, reads=[bsl, b_hT], writes=[bank[bk]])
                      P.op("act", lambda e, bk=bk, c=c: e.copy(out=Vv[:, 4 * c:4 * c + 4, 0:64],
                                                               in_=ps[:, bk, 0:256].rearrange("p (a b) -> p a b", b=64)),
                           reads=[bank[bk]], writes=[b_V[c]])
                  for c in range(NCH):
                      pairs = []
                      for i in range(2):
                          pairs.append(dict(
                              q=(qTall[0:64, i, c * CH:(c + 1) * CH], qTall[64:128, i, c * CH:(c + 1) * CH]),
                              k=lambda m, kb: KT[64 * m:64 * m + 64, 0, kb * 128:(kb + 1) * 128],
                              v=lambda m, kb: Vv[:, kb, 0:64],
                              tp=((0, 0), (64, 0)), reads=[b_q[c]] + b_KT + b_V,
                              finish=finish_plain(yT[:, 2 * ga + i, c * CH:(c + 1) * CH], b_y[0][c])))
                      attention(pairs, 64.0 ** -0.5)

              P.barrier()
              if debug and seq == 0 and l == 0:
                  dbg_y = nc.dram_tensor("dbg_y", [128, 12, S], BF16, kind="ExternalOutput").ap()
                  dbg_r = nc.dram_tensor("dbg_r", [128, S], F32, kind="ExternalOutput").ap()
                  dbg_x = nc.dram_tensor("dbg_x", [128, 8, S], F32, kind="ExternalOutput").ap()
                  dbg_l = nc.dram_tensor("dbg_l", [128, 8], F32, kind="ExternalOutput").ap()
                  b_dbg = P.buf("dbg")
                  d1 = P.op("sp", lambda e: [e.dma_start(out=dbg_y, in_=yT[:]), e.dma_start(out=dbg_r, in_=rstd1[:]),
                                             e.dma_start(out=dbg_x, in_=xT[:]), e.dma_start(out=dbg_l, in_=lam[:])],
                            reads=[b_dbg], dma=True, ndma=4)
                  P.barrier_op("sp", [d1])
                  P.barrier()
              gen_banks[:] = [4, 5, 6, 7]
              for c in range(NCH):
                  cs = slice(c * CH, (c + 1) * CH)
                  make_hT(c, 0, rstd1[:, cs], b_rstd1[c], l)
                  for ch in range(2):
                      for n in range(3):
                          sg, bsg = load_w([(0, 8, 0, 512, wv(w_in_d[l, :, C_G + n * D + ch * 512:C_G + n * D + ch * 512 + 512]))])
                          sb_, bsb = load_w([(0, 4, 0, 512, wv(w_br_d[l, n, :, ch * 512:(ch + 1) * 512]))])
                          for cg in range(4):
                              bg = gbank()
                              mm_group(bg, ps[:, bg, :], [(sg[:, k, cg * 128:(cg + 1) * 128], hT[:, k, :]) for k in range(8)],
                                       reads=[bsg, b_hT])
                              bz = gbank()
                              mm_group(bz, ps[:, bz, :], [(sb_[:, kk, cg * 128:(cg + 1) * 128], yT[:, 4 * n + kk, cs]) for kk in range(4)],
                                       reads=[bsb, b_y[n][c]])
                              gi = (n * 4 + cg) % 2
                              P.op("act", lambda e, bg=bg, gi=gi: e.activation(out=Gs[gi], in_=ps[:, bg, :], func=AF.Sigmoid),
                                   reads=[bank[bg]], writes=[b_Gs[gi]])
                              if n == 0:
                                  P.op("dve", lambda e, bz=bz, gi=gi, cg=cg: e.tensor_tensor(out=macc[:, cg, :], in0=ps[:, bz, :], in1=Gs[gi],
                                                                                           op=ALU.mult),
                                       reads=[bank[bz], b_Gs[gi]], writes=[b_hid[0]])
                              else:
                                  P.op("dve", lambda e, bz=bz, gi=gi: e.tensor_tensor(out=tmpm, in0=ps[:, bz, :], in1=Gs[gi], op=ALU.mult),
                                       reads=[bank[bz], b_Gs[gi]], writes=[b_tmpm])
                                  if n == 1:
                                      P.op(EW, lambda e, cg=cg: e.tensor_tensor(out=macc[:, cg, :], in0=macc[:, cg, :], in1=tmpm, op=ALU.add),
                                           reads=[b_hid[0], b_tmpm], writes=[b_hid[0]])
                                  else:
                                      P.op(EW, lambda e, cg=cg, ch=ch: e.tensor_tensor(out=mT[:, ch * 4 + cg, :], in0=macc[:, cg, :], in1=tmpm,
                                                                                         op=ALU.add),
                                           reads=[b_hid[0], b_tmpm], writes=[b_mT])
                  for ch in range(2):
                      so, bso = load_w([(0, 8, 0, 512, wv(w_out_d[l, :, ch * 512:(ch + 1) * 512]))])
                      for cg in range(4):
                          bk = gbank()
                          mm_group(bk, ps[:, bk, :], [(so[:, k, cg * 128:(cg + 1) * 128], mT[:, k, :]) for k in range(8)], reads=[bso, b_mT])
                          P.op("dve", lambda e, bk=bk, ch=ch, cg=cg: e.tensor_tensor(out=xT[:, ch * 4 + cg, cs], in0=ps[:, bk, :],
                                                                                     in1=xT[:, ch * 4 + cg, cs], op=ALU.add),
                               reads=[bank[bk], b_xT[c]], writes=[b_xT[c]])
                  norm_stats(c, sq8, [b_hid[1]], w_rs2, b_wrs2, w_ln2, b_wln2)
                  make_hT(c, 8, w_rs2, b_wrs2, l)
                  for b in range(8):
                      s1, bs1 = load_w([(0, 8, 0, 512, wv(w_ff1_d[l, :, b * 512:(b + 1) * 512]))])
                      for cg in range(4):
                          j = b * 4 + cg
                          bk = gbank()
                          mm_group(bk, ps[:, bk, :], [(s1[:, k, cg * 128:(cg + 1) * 128], hT[:, k, :]) for k in range(8)], reads=[bs1, b_hT])
                          ri = j % 2
                          P.op("act", lambda e, bk=bk, ri=ri: e.activation(out=rl[ri], in_=ps[:, bk, :], func=AF.Relu),
                               reads=[bank[bk]], writes=[b_rl[ri]])
                          P.op(EW, lambda e, j=j, ri=ri: e.tensor_tensor(out=hid[:, j, :], in0=rl[ri], in1=rl[ri], op=ALU.mult),
                               reads=[b_rl[ri]], writes=[b_hid[j // 8]])
                  for ch in range(2):
                      for kb in range(4):
                          s2, bs2 = load_w([(0, 8, 0, 512, wv(w_ff2_d[l, kb * D:(kb + 1) * D, ch * 512:(ch + 1) * 512]))])
                          for cg in range(4):
                              def fn(e, s2=s2, kb=kb, cg=cg):
                                  ins = None
                                  for k in range(8):
                                      ins = e.matmul(ps[:, cg, :], lhsT=s2[:, k, cg * 128:(cg + 1) * 128], rhs=hid[:, kb * 8 + k, :],
                                                     start=(kb == 0 and k == 0), stop=(kb == 3 and k == 7))
                                  return ins
                              P.op("pe", fn, reads=[bs2, b_hid[kb]], writes=[bank[cg]])
                      for cg in range(4):
                          P.op("dve", lambda e, ch=ch, cg=cg: e.tensor_tensor(out=xT[:, ch * 4 + cg, cs], in0=ps[:, cg, :],
                                                                              in1=xT[:, ch * 4 + cg, cs], op=ALU.add),
                               reads=[bank[cg], b_xT[c]], writes=[b_xT[c]])
              P.barrier()

          gen_banks[:] = [0, 1, 2, 3, 4, 5, 6, 7]
          fg0 = DEPTH * NVL
          last_store = []
          for c in range(NCH):
              norm_stats(c, sqf, [b_sqf], w_rs3, b_wrs3, w_ln3, b_wln3)
              for k in range(8):
                  P.op("dve", lambda e, k=k, c=c: e.scalar_tensor_tensor(out=ofin[:, k, :], in0=xT[:, k, c * CH:(c + 1) * CH],
                                                                         scalar=pvec[:, fg0 + k:fg0 + k + 1], in1=w_rs3,
                                                                         op0=ALU.mult, op1=ALU.mult),
                       reads=[b_xT[c], b_wrs3, b_pv], writes=[b_ofin])
              for tt in range(4):
                  sgi = tt % 2
                  for h in range(2):
                      bk = gbank()

                      def fn(e, tt=tt, h=h, bk=bk):
                          ins = None
                          for j in range(4):
                              ins = e.transpose(out=ps[:, bk, j * 128:(j + 1) * 128], in_=ofin[:, 4 * h + j, tt * 128:(tt + 1) * 128],
                                                identity=ident[:])
                          return ins
                      P.op("pe", fn, reads=[b_ofin, b_pv], writes=[bank[bk]])
                      if h == 0:
                          P.op("act", lambda e, sgi=sgi, bk=bk: e.copy(out=stage[sgi][:, 0:512], in_=ps[:, bk, :]),
                               reads=[bank[bk]], writes=[b_stage[sgi]])
                      else:
                          P.op("dve", lambda e, sgi=sgi, bk=bk: e.tensor_copy(out=stage[sgi][:, 512:1024], in_=ps[:, bk, :]),
                               reads=[bank[bk]], writes=[b_stage[sgi]])
                  t0 = c * CH + tt * 128
                  o = P.op("sp", lambda e, sgi=sgi, t0=t0: e.dma_start(out=out_d[seq, t0:t0 + 128, :], in_=stage[sgi]),
                           reads=[b_stage[sgi]], dma=True)
                  last_store.append(o)
          P.barrier_op("sp", last_store)


    except _Stop:
        pass
    P.emit()
    return nc, P


def _angles(pos, dim, theta):
    inv = (np.float32(theta) ** (-np.arange(0, dim, 2, dtype=np.float32) / np.float32(dim))).astype(np.float32)
    return (pos.astype(np.float32)[:, None] * inv[None, :]).astype(np.float32)


def _tables():
    t = np.arange(S)
    row_idx = (t // 64)
    col_idx = (t % 64)
    row_ang = _angles(row_idx, 32, 10000.0)
    col_ang = _angles(col_idx, 32, 10000.0)
    b_ang = _angles(t, 8, 500000.0)
    c_ang = _angles(t, 32, 10000.0)
    tabA = np.zeros((128, 2, S), np.float32)
    tabB = np.zeros((128, 2, S), np.float32)
    tabC = np.zeros((128, 2, S), np.float32)
    tabB[:, 0, :] = 1.0
    tabC[:, 0, :] = 1.0
    for p in range(128):
        d = p % 64
        ang = row_ang[:, d % 16] if d < 32 else col_ang[:, (d - 32) % 16]
        tabA[p, 0] = np.cos(ang)
        tabA[p, 1] = np.sin(ang)
        d = p % 32
        if d < 8:
            ang = b_ang[:, d % 4]
            tabB[p, 0] = np.cos(ang)
            tabB[p, 1] = np.sin(ang)
        if 64 <= p < 96:
            ang = c_ang[:, (p - 64) % 16]
            tabC[p, 0] = np.cos(ang)
            tabC[p, 1] = np.sin(ang)
    return tabA, tabB, tabC


def _cmat():
    m = np.zeros((7, 128, 128), np.float32)
    m[M_ONES] = 1.0
    m[M_BLK64, 0:64, 0:64] = 1.0
    m[M_BLK64, 64:128, 64:128] = 1.0
    for blk in range(4):
        o = blk * 32
        for i in range(16):
            m[M_RA, o + i + 16, o + i] = -1.0
            m[M_RA, o + i, o + i + 16] = 1.0
        for i in range(4):
            m[M_RB, o + i + 4, o + i] = -1.0
            m[M_RB, o + i, o + i + 4] = 1.0
    for i in range(16):
        m[M_RC, 64 + i + 16, 64 + i] = -1.0
        m[M_RC, 64 + i, 64 + i + 16] = 1.0
    for i in range(32):
        m[M_SEL, i, 64 + i] = 1.0
    for i in range(64):
        m[M_COMB, i, i] = 1.0
        m[M_COMB, 64 + i, i] = -1.0
    return np.ascontiguousarray(m.transpose(1, 0, 2).reshape(128, 7 * 128))


def _pvec(ln1_g, ln2_g, a_q_norm, a_k_norm, b_lambda, b_subln, c_q_norm, c_kv_norm, final_g):
    pv = np.zeros((128, DEPTH * NVL + 8), np.float32)
    idx = np.arange(128)
    for l in range(DEPTH):
        o = l * NVL
        pv[:, o:o + 8] = ln1_g[l].reshape(8, 128).T
        pv[:, o + 8:o + 16] = ln2_g[l].reshape(8, 128).T
        pv[:, o + 16] = a_q_norm[l][idx % 64]
        pv[:, o + 17] = a_k_norm[l][idx % 64]
        pv[:, o + 18] = b_subln[l][idx % 64]
        pv[:, o + 19:o + 22] = c_q_norm[l].reshape(3, 128).T
        pv[:, o + 22:o + 24] = c_kv_norm[l].reshape(2, 128).T
        pv[:, o + 24:o + 152] = b_lambda[l].reshape(1, 128)
    pv[:, DEPTH * NVL:] = final_g.reshape(8, 128).T
    return pv


_CACHE = {}


def kernel(x, ln1_g, w_in, a_q_norm, a_k_norm, b_lambda, b_subln, c_q_norm, c_kv_norm,
           c_w_uq, c_w_ukv, w_branch, w_out, ln2_g, w_ff1, w_ff2, final_g):
    f = lambda a: np.ascontiguousarray(np.asarray(a, dtype=np.float32))
    x = f(x)
    w_in = f(w_in)
    c_w_ukv = f(c_w_ukv)
    n_cores = 8
    w_kdup = np.empty((DEPTH, D, 256), np.float32)
    for j in range(2):
        wk = w_in[:, :, C_AK + 64 * j:C_AK + 64 * j + 64]
        w_kdup[:, :, 128 * j:128 * j + 64] = wk
        w_kdup[:, :, 128 * j + 64:128 * j + 128] = wk
    w_uk96 = np.zeros((DEPTH, 256, 8, 96), np.float32)
    w_uk96[:, :, :, 0:64] = c_w_ukv.reshape(DEPTH, 256, 8, 128)[:, :, :, 0:64]
    w_uk96 = w_uk96.reshape(DEPTH, 256, 768)
    tabA, tabB, tabC = _tables()
    shared = {
        "w_in": w_in, "w_kdup": w_kdup, "c_w_uq": f(c_w_uq), "w_uk96": w_uk96, "c_w_ukv": c_w_ukv,
        "w_branch": f(w_branch), "w_out": f(w_out), "w_ff1": f(w_ff1), "w_ff2": f(w_ff2),
        "pvec": _pvec(f(ln1_g), f(ln2_g), f(a_q_norm), f(a_k_norm), f(b_lambda), f(b_subln), f(c_q_norm), f(c_kv_norm), f(final_g)),
        "cmat": _cmat(), "ident": np.eye(128, dtype=np.float32), "tabA": tabA, "tabB": tabB, "tabC": tabC,
    }
    if "nc" not in _CACHE:
        _CACHE["nc"] = build_program()[0]
    nc = _CACHE["nc"]
    in_maps = []
    for c in range(n_cores):
        m = dict(shared)
        m["x"] = np.ascontiguousarray(x[NSEQ * c:NSEQ * (c + 1)])
        in_maps.append(m)
    res = run_bass_kernel_spmd(nc, in_maps, core_ids=list(range(n_cores)))
    out = np.concatenate([np.asarray(r["out"]) for r in res.results], axis=0)
    return out.astype(np.float32)
```

```python
import contextlib
import math
import numpy as np
import concourse.bass as bass
import concourse.mybir as mybir
from concourse.bass_utils import run_bass_kernel_spmd

F32 = mybir.dt.float32
BF16 = mybir.dt.bfloat16
AF = mybir.ActivationFunctionType
ALU = mybir.AluOpType
AX = mybir.AxisListType

ENGS = ("pe", "act", "dve", "pool", "sp")
EW = "dve"


class Buf:
    __slots__ = ("name", "writers", "readers", "dsem", "dcnt", "psum")

    def __init__(self, name, psum=False):
        self.name = name
        self.psum = psum
        self.writers = {}
        self.readers = {}
        self.dsem = None
        self.dcnt = 0


class _Rec:
    def __init__(self):
        self.calls = []

    def __getattr__(self, name):
        def f(*a, **kw):
            self.calls.append((name, a, kw))
            return len(self.calls) - 1
        return f


class _Replay:
    __slots__ = ("calls", "ret")

    def __init__(self, fn):
        r = _Rec()
        self.ret = fn(r)
        self.calls = r.calls

    def __call__(self, eng):
        res = [getattr(eng, n)(*a, **kw) for (n, a, kw) in self.calls]
        if isinstance(self.ret, (list, tuple)):
            return [res[i] for i in self.ret]
        return res[self.ret]


class Op:
    __slots__ = ("eng", "fn", "deps", "marked", "done", "is_dma")

    def __init__(self, eng, fn):
        self.eng = eng
        self.fn = fn
        self.deps = []
        self.marked = False
        self.done = None
        self.is_dma = False


class Prog:
    def __init__(self, nc):
        self.nc = nc
        self.ops = {e: [] for e in ENGS}
        self.ndsem = 0
        self.stack = contextlib.ExitStack()
        self.nbuf = 0

    def sb(self, name, shape, dtype):
        return self.stack.enter_context(self.nc.sbuf_tensor(name, list(shape), dtype))

    def ps(self, name, shape, dtype):
        return self.stack.enter_context(self.nc.psum_tensor(name, list(shape), dtype))

    def buf(self, name=None, psum=False):
        self.nbuf += 1
        return Buf(name or ("b%d" % self.nbuf), psum)

    def _dep(self, op, prod):
        if prod is op:
            return
        if prod.eng == op.eng and not prod.is_dma:
            return
        op.deps.append(prod)
        if not prod.is_dma:
            prod.marked = True

    def op(self, eng, fn, reads=(), writes=(), dma=False, ndma=1):
        o = Op(eng, _Replay(fn))
        o.is_dma = dma
        for b in reads:
            for w in b.writers.values():
                if w.eng == eng and not w.is_dma:
                    if eng != "pe" and not dma:
                        o.deps.append(w)
                        w.marked = True
                else:
                    self._dep(o, w)
            if b.psum:
                for r in b.readers.values():
                    self._dep(o, r)
        for b in writes:
            for r in b.readers.values():
                self._dep(o, r)
            for w in b.writers.values():
                self._dep(o, w)
        for b in reads:
            b.readers[("dma%d" % id(o)) if dma else eng] = o
        for b in writes:
            b.readers.clear()
            b.writers.clear()
            b.writers[("dma" if dma else eng)] = o
        if dma:
            tgt = writes[0] if writes else reads[0]
            if tgt.dsem is None:
                tgt.dsem = "d%d" % self.ndsem
                self.ndsem += 1
            tgt.dcnt += 16 * ndma
            o.done = (tgt.dsem, tgt.dcnt)
        self.ops[eng].append(o)
        return o

    def barrier_op(self, eng, deps):
        o = Op(eng, None)
        for d in deps:
            if d is None:
                continue
            o.deps.append(d)
            if not d.is_dma:
                d.marked = True
        self.ops[eng].append(o)
        return o

    def barrier(self, engs=ENGS):
        engs = [e for e in engs if e in self.ops]
        last = {}
        for e in engs:
            lst = [o for o in self.ops[e] if o.fn is not None]
            last[e] = lst[-1] if lst else None
        for e in engs:
            self.barrier_op(e, [last[f] for f in engs if f != e])

    def emit(self):
        nc = self.nc
        st = self.stack
        sems = {}
        for e in ENGS:
            sems[e] = st.enter_context(nc.semaphore("s_" + e))
        for i in range(self.ndsem):
            sems["d%d" % i] = st.enter_context(nc.semaphore("sd%d" % i))
        for e in ENGS:
            c = 0
            for o in self.ops[e]:
                if o.is_dma:
                    continue
                if o.marked and o.fn is not None:
                    c += 1
                    o.done = (e, c)
                elif o.fn is None:
                    o.done = (e, c)
        self.stats = {e: len(self.ops[e]) for e in ENGS}

        def run(e, eng):
            waited = {}
            for o in self.ops[e]:
                need = {}
                for d in o.deps:
                    k, v = d.done
                    if v > need.get(k, 0):
                        need[k] = v
                for k, v in need.items():
                    if v > waited.get(k, 0):
                        eng.wait_ge(sems[k], v)
                        waited[k] = v
                if o.fn is None:
                    continue
                ins = o.fn(eng)
                if o.is_dma:
                    for i_ in (ins if isinstance(ins, (list, tuple)) else [ins]):
                        i_.then_inc(sems[o.done[0]], 16)
                elif o.marked:
                    ins.then_inc(sems[e], 1)

        with nc.Block() as block:
            @block.tensor
            def _(t):
                run("pe", t)

            @block.scalar
            def _(s):
                run("act", s)

            @block.vector
            def _(v):
                run("dve", v)

            @block.gpsimd
            def _(g):
                run("pool", g)

            @block.sync
            def _(s):
                run("sp", s)
        st.close()


D = 1024
S = 2048
DEPTH = 4
NSEQ = 2
CH = 512
NCH = S // CH
IN_COLS = 6048
EPS = 1e-6
C_AQ, C_AK, C_AV = 0, 512, 640
C_BQ, C_BK, C_BV = 768, 1280, 1792
C_CQ, C_CKV, C_KR = 2304, 2688, 2944
C_G = 2976
NVL = 152
M_ONES, M_BLK64, M_RA, M_RB, M_RC, M_SEL, M_COMB = range(7)


def lambda_init(l):
    return 0.8 - 0.6 * math.exp(-0.3 * l)


class _Stop(Exception):
    pass


def build_program(depth=DEPTH, nseq=NSEQ, wring=3, debug=False, stop_at=None):
    nc = bass.Bass("TRN2", target_bir_lowering=False)
    P = Prog(nc)

    def din(name, shape):
        return nc.dram_tensor(name, list(shape), F32, kind="ExternalInput").ap()

    x_d = din("x", [nseq, S, D])
    w_in_d = din("w_in", [DEPTH, D, IN_COLS])
    w_kdup_d = din("w_kdup", [DEPTH, D, 256])
    w_uq_d = din("c_w_uq", [DEPTH, 384, 768])
    w_uk_d = din("w_uk96", [DEPTH, 256, 768])
    w_ukv_d = din("c_w_ukv", [DEPTH, 256, 1024])
    w_br_d = din("w_branch", [DEPTH, 3, 512, D])
    w_out_d = din("w_out", [DEPTH, D, D])
    w_ff1_d = din("w_ff1", [DEPTH, D, 4 * D])
    w_ff2_d = din("w_ff2", [DEPTH, 4 * D, D])
    pvec_d = din("pvec", [128, DEPTH * NVL + 8])
    cmat_d = din("cmat", [128, 7 * 128])
    ident_d = din("ident", [128, 128])
    tabA_d = din("tabA", [128, 2, S])
    tabB_d = din("tabB", [128, 2, S])
    tabC_d = din("tabC", [128, 2, S])
    out_d = nc.dram_tensor("out", [nseq, S, D], F32, kind="ExternalOutput").ap()

    xT = P.sb("xT", [128, 8, S], F32)
    yT = P.sb("yT", [128, 12, S], BF16)
    pvec = P.sb("pvec_sb", [128, DEPTH * NVL + 8], F32)
    cmat = P.sb("cmat_sb", [128, 7 * 128], BF16)
    ident = P.sb("ident_sb", [128, 128], F32)
    cst = P.sb("cst", [128, 8], F32)
    lam = P.sb("lam", [128, 8], F32)
    rstd1 = P.sb("rstd1", [128, S], F32)
    hT = P.sb("hT", [128, 8, CH], BF16)
    wr = [P.sb("wr%d" % i, [128, 8, 512], BF16) for i in range(wring)]
    SCR = 52224
    scr = P.sb("scr", [128, SCR // 2], BF16)
    ps = P.ps("ps", [128, 8, 512], F32)

    off = [0]

    def carve(nbytes):
        o = off[0]
        off[0] += nbytes
        assert off[0] <= SCR, (off[0], SCR)
        return o

    def v_bf(o, shape):
        n = int(np.prod(shape))
        a = scr[:, o // 2:o // 2 + n]
        if len(shape) == 2:
            return a.rearrange("p (a b) -> p a b", b=shape[1])
        return a

    def v_f32(o, shape):
        n = int(np.prod(shape))
        a = scr[:, o // 2:o // 2 + 2 * n].bitcast(F32)
        if len(shape) == 2:
            return a.rearrange("p (a b) -> p a b", b=shape[1])
        return a

    off[0] = 0
    KT = v_bf(carve(8192), [2, S])
    Vv = v_bf(carve(8192), [16, 256])
    qTall = v_bf(carve(8192), [2, S])
    PT = [v_bf(carve(2048), [2, CH]) for _ in range(3)]
    tab = [v_f32(carve(4096), [2, CH]) for _ in range(2)]
    w_ln = v_f32(carve(2048), [CH])
    w_rs = v_f32(carve(2048), [CH])
    w_t = v_f32(carve(2048), [CH])
    w_u = v_f32(carve(2048), [CH])
    w_r = v_f32(carve(2048), [CH])
    w_sq = v_bf(carve(1024), [CH])
    w_qn = v_bf(carve(1024), [CH])
    w_yn = v_bf(carve(1024), [CH])
    att_end = off[0]
    off[0] = 0
    mT = v_bf(carve(8192), [8, CH])
    hid = v_bf(carve(32768), [32, CH])
    Gs = [v_f32(carve(2048), [CH]) for _ in range(2)]
    tmpm = v_f32(carve(2048), [CH])
    rl = Gs
    w_ln2 = v_f32(carve(2048), [CH])
    w_rs2 = v_f32(carve(2048), [CH])
    loc_end = off[0]
    macc = scr[:, (8192 // 2):(8192 // 2) + 4096].bitcast(F32).rearrange("p (a b) -> p a b", b=CH)
    sq8 = scr[:, (16384 // 2):(16384 // 2) + 4096].rearrange("p (a b) -> p a b", b=CH)
    off[0] = 0
    stage = [v_f32(carve(4096), [D]) for _ in range(2)]
    ofin = v_f32(carve(16384), [8, CH])
    sqf = v_bf(carve(8192), [8, CH])
    w_ln3 = v_f32(carve(2048), [CH])
    w_rs3 = v_f32(carve(2048), [CH])

    bank = [P.buf("bank%d" % i, psum=True) for i in range(8)]
    b_xT = [P.buf("xT%d" % c) for c in range(NCH)]
    b_y = [[P.buf("y%d_%d" % (n, c)) for c in range(NCH)] for n in range(3)]
    b_const = P.buf("const")
    b_pv = P.buf("pvec")
    b_cst = P.buf("cst")
    b_lam = P.buf("lam")
    b_rstd1 = [P.buf("rstd1_%d" % c) for c in range(NCH)]
    b_hT = P.buf("hT")
    b_wr = [P.buf("wr%d" % i) for i in range(wring)]
    b_KT = [P.buf("KT%d" % c) for c in range(NCH)]
    b_V = [P.buf("V%d" % c) for c in range(NCH)]
    b_q = [P.buf("q%d" % c) for c in range(NCH)]
    b_PT = [P.buf("PT%d" % i) for i in range(3)]
    b_tab = [P.buf("tab%d" % i) for i in range(2)]
    b_wln, b_wrs, b_wt, b_wu, b_wrr, b_wsq, b_wqn, b_wyn = [P.buf() for _ in range(8)]
    b_mT = P.buf("mT")
    b_hid = [P.buf("hid%d" % i) for i in range(4)]
    b_Gs = [P.buf(), P.buf()]
    b_tmpm = P.buf()
    b_rl = b_Gs
    b_wln2, b_wrs2 = P.buf(), P.buf()
    b_stage = [P.buf(), P.buf()]
    b_ofin, b_sqf, b_wln3, b_wrs3 = P.buf(), P.buf(), P.buf(), P.buf()

    def cm(i, rows=128, cols=128):
        return cmat[0:rows, i * 128:i * 128 + cols]

    gen_banks = [6, 7]
    gb_i = [0]

    def gbank():
        b = gen_banks[gb_i[0] % len(gen_banks)]
        gb_i[0] += 1
        return b

    wr_i = [0]

    wsc = nc.dram_tensor("wsc", [DEPTH, 30, 128, 8, 512], BF16).ap()
    b_wsc = [P.buf("wsc%d" % i) for i in range(DEPTH)]
    cached = set()

    def load_w(parts, key=None):
        s = wr_i[0] % wring
        wr_i[0] += 1
        slot = wr[s]
        if key is not None and key in cached:
            kc = parts[0][1]
            P.op("sp", lambda e: e.dma_start(out=slot[:, 0:kc, :], in_=wsc[key[0], key[1], :, 0:kc, :]),
                 reads=[b_wsc[key[0]]], writes=[b_wr[s]], dma=True)
            return slot, b_wr[s]

        def fn(e, parts=parts, slot=slot):
            res = []
            for p_ in parts:
                k0, kc, c0, ncol, ap = p_[:5]
                dst = slot[:, k0:k0 + kc, c0:c0 + ncol]
                if len(p_) > 5:
                    dst = p_[5](dst)
                res.append(e.dma_start(out=dst, in_=ap))
            return res
        P.op("pool", fn, writes=[b_wr[s]], dma=True, ndma=len(parts))
        if key is not None:
            kc = parts[0][1]
            P.op("sp", lambda e: e.dma_start(out=wsc[key[0], key[1], :, 0:kc, :], in_=slot[:, 0:kc, :]),
                 reads=[b_wr[s]], writes=[b_wsc[key[0]]], dma=True)
            cached.add(key)
        return slot, b_wr[s]

    def wv(ap2d):
        return ap2d.rearrange("(k p) n -> p k n", p=128)

    def mm_group(bk, out_ap, pairs, reads, tp=None):
        def fn(e, pairs=pairs, out_ap=out_ap, tp=tp):
            n = len(pairs)
            ins = None
            for i, (l, r) in enumerate(pairs):
                if tp is None:
                    ins = e.matmul(out_ap, lhsT=l, rhs=r, start=(i == 0), stop=(i == n - 1))
                else:
                    ins = e.matmul(out_ap, lhsT=l, rhs=r, start=(i == 0), stop=(i == n - 1), tile_position=tp)
            return ins
        return P.op("pe", fn, reads=reads, writes=[bank[bk]])

    def rstd_from_bank(bk, np_, scale, out_ap, b_out, ln_ap, b_ln, bias_col=None, extra_reads=()):
        P.op("act", lambda e: e.activation(out=ln_ap[0:np_], in_=ps[0:np_, bk, :], func=AF.Ln, bias=cst[0:np_, 0:1], scale=scale),
             reads=[bank[bk], b_cst], writes=[b_ln])
        if bias_col is None:
            P.op("act", lambda e: e.activation(out=out_ap[0:np_], in_=ln_ap[0:np_], func=AF.Exp, scale=-0.5),
                 reads=[b_ln], writes=[b_out])
        else:
            P.op("act", lambda e: e.activation(out=out_ap[0:np_], in_=ln_ap[0:np_], func=AF.Exp, scale=-0.5, bias=bias_col),
                 reads=[b_ln, b_cst], writes=[b_out])

    tab_i = [0]

    def load_tab(src, c, np_=128, p0=0):
        i = tab_i[0] % 2
        tab_i[0] += 1
        t = tab[i]
        P.op("sp", lambda e: e.dma_start(out=t[0:np_], in_=src[p0:p0 + np_, :, c * CH:(c + 1) * CH]), writes=[b_tab[i]], dma=True)
        return t, b_tab[i]

    def rotary(np_, qn_reads, rmat, t, b_t, out_ap, b_out, krows=None):
        bk = gbank()
        k0, k1 = krows if krows is not None else (0, np_)
        mm_group(bk, ps[0:np_, bk, :], [(rmat, w_qn[k0:k1])], reads=[b_wqn, b_const])
        P.op("dve", lambda e: e.tensor_tensor(out=w_t[0:np_], in0=ps[0:np_, bk, :], in1=t[0:np_, 1, :], op=ALU.mult),
             reads=[bank[bk], b_t], writes=[b_wt])
        P.op(EW, lambda e: e.tensor_tensor(out=w_u[0:np_], in0=w_qn[0:np_], in1=t[0:np_, 0, :], op=ALU.mult),
             reads=[b_wqn, b_t], writes=[b_wu])
        P.op(EW, lambda e: e.tensor_tensor(out=out_ap, in0=w_u[0:np_], in1=w_t[0:np_], op=ALU.add),
             reads=[b_wu, b_wt], writes=[b_out])

    def make_hT(c, gcol0, rstd_ap, b_rstd, l):
        for k in range(8):
            P.op("dve", lambda e, k=k: e.scalar_tensor_tensor(out=hT[:, k, :], in0=xT[:, k, c * CH:(c + 1) * CH],
                                                              scalar=pvec[:, l * NVL + gcol0 + k:l * NVL + gcol0 + k + 1],
                                                              in1=rstd_ap, op0=ALU.mult, op1=ALU.mult),
                 reads=[b_xT[c], b_rstd, b_pv], writes=[b_hT])

    def norm_stats(c, sq_ap, b_sq_list, out_ap, b_out, ln_ap, b_ln):
        P.op("act", lambda e: e.activation(out=sq_ap, in_=xT[:, :, c * CH:(c + 1) * CH], func=AF.Square),
             reads=[b_xT[c]], writes=b_sq_list)
        bk = gbank()
        mm_group(bk, ps[:, bk, :], [(cm(M_ONES), sq_ap[:, k, :]) for k in range(8)], reads=b_sq_list + [b_const])
        rstd_from_bank(bk, 128, 1.0 / D, out_ap, b_out, ln_ap, b_ln)

    def attention(pairs, scale):
        units = [(pi, kb) for pi in range(len(pairs)) for kb in range(16)]

        def qk(u):
            pi, kb = units[u]
            pr = pairs[pi]
            s = u % 2

            def fn(e, pr=pr, kb=kb, s=s):
                ins = None
                for m in range(2):
                    kw = {}
                    if pr["tp"] is not None:
                        kw["tile_position"] = pr["tp"][m]
                    ins = e.matmul(ps[:, 2 * s + m, :], lhsT=pr["k"](m, kb), rhs=pr["q"][m], start=True, stop=True, **kw)
                return ins
            P.op("pe", fn, reads=pr["reads"], writes=[bank[2 * s], bank[2 * s + 1]])

        def ex_pv(u):
            pi, kb = units[u]
            pr = pairs[pi]
            s = u % 2
            pt = PT[u % 3]
            bpt = b_PT[u % 3]
            P.op("act", lambda e: e.activation(out=pt, in_=ps[:, 2 * s:2 * s + 2, :], func=AF.Exp, scale=scale),
                 reads=[bank[2 * s], bank[2 * s + 1]], writes=[bpt])

            def fn(e, pr=pr, kb=kb, pt=pt):
                st_, sp_ = (kb == 0), (kb == 15)
                e.matmul(ps[0:64, 4, :], lhsT=pr["v"](0, kb), rhs=pt[:, 0, :], start=st_, stop=sp_)
                e.matmul(ps[64:128, 4, :], lhsT=pr["v"](1, kb), rhs=pt[:, 1, :], start=st_, stop=sp_, tile_position=(0, 64))
                e.matmul(ps[0:64, 5, :], lhsT=cm(M_ONES, 128, 64), rhs=pt[:, 0, :], start=st_, stop=sp_)
                return e.matmul(ps[64:128, 5, :], lhsT=cm(M_ONES, 128, 64), rhs=pt[:, 1, :], start=st_, stop=sp_,
                                tile_position=(0, 64))
            P.op("pe", fn, reads=[bpt, b_const] + pr["reads"], writes=[bank[4], bank[5]])
            if kb == 15:
                pr["finish"]()

        n = len(units)
        for u in range(n):
            qk(u)
            if u >= 1:
                ex_pv(u - 1)
        ex_pv(n - 1)

    def finish_plain(out_ap, b_out):
        def f():
            P.op("dve", lambda e: e.reciprocal(out=w_r, in_=ps[:, 5, :]), reads=[bank[5]], writes=[b_wrr])
            P.op("dve", lambda e: e.tensor_tensor(out=out_ap, in0=ps[:, 4, :], in1=w_r, op=ALU.mult),
                 reads=[bank[4], b_wrr], writes=[b_out])
        return f

    P.op("sp", lambda e: [e.dma_start(out=pvec[:], in_=pvec_d), e.dma_start(out=ident[:], in_=ident_d)],
         writes=[b_pv], dma=True, ndma=2)
    P.op("pool", lambda e: e.dma_start(out=cmat[:], in_=cmat_d), writes=[b_const], dma=True)
    P.op("dve", lambda e: e.memset(cst[:, 0:1], EPS), writes=[b_cst])
    for l in range(DEPTH):
        P.op("dve", lambda e, l=l: e.memset(cst[:, 1 + l:2 + l], math.log(1.0 - lambda_init(l))), writes=[b_cst])

    def dump_and_stop(tag):
        if stop_at != tag:
            return
        P.barrier()
        outs = []
        for nm, t_, shp, dt_ in (("dbg_KT", KT, [128, 2, S], BF16), ("dbg_q", qTall, [128, 2, S], BF16), ("dbg_V", Vv, [128, 16, 256], BF16),
                                 ("dbg_y", yT[:], [128, 12, S], BF16), ("dbg_wqn", w_qn, [128, CH], BF16), ("dbg_wt", w_t, [128, CH], F32),
                                 ("dbg_wu", w_u, [128, CH], F32)):
            dd = nc.dram_tensor(nm, shp, dt_, kind="ExternalOutput").ap()
            outs.append((dd, t_))
        b_dbg = P.buf("dbg")
        d1 = P.op("sp", lambda e: [e.dma_start(out=a_, in_=b_) for a_, b_ in outs], reads=[b_dbg], dma=True, ndma=len(outs))
        P.barrier_op("sp", [d1])
        raise _Stop()

    try:
      for seq in range(nseq):
          P.barrier()
          gen_banks[:] = [0, 1, 2, 3, 4, 5, 6, 7]
          for t in range(16):
              sgi = t % 2
              P.op("sp", lambda e, t=t, sgi=sgi: e.dma_start(out=stage[sgi], in_=x_d[seq, t * 128:(t + 1) * 128, :]),
                   writes=[b_stage[sgi]], dma=True)
              for h in range(2):
                  bk = gbank()

                  def fn(e, sgi=sgi, h=h, bk=bk):
                      ins = None
                      for j in range(4):
                          ins = e.transpose(out=ps[:, bk, j * 128:(j + 1) * 128], in_=stage[sgi][:, (4 * h + j) * 128:(4 * h + j + 1) * 128],
                                            identity=ident[:])
                      return ins
                  P.op("pe", fn, reads=[b_stage[sgi], b_pv], writes=[bank[bk]])
                  eng = "act" if h == 0 else "dve"
                  if eng == "act":
                      P.op("act", lambda e, t=t, h=h, bk=bk: e.copy(out=xT[:, 4 * h:4 * h + 4, t * 128:(t + 1) * 128],
                                                                    in_=ps[:, bk, :].rearrange("p (a b) -> p a b", b=128)),
                           reads=[bank[bk]], writes=[b_xT[t // 4]])
                  else:
                      P.op("dve", lambda e, t=t, h=h, bk=bk: e.tensor_copy(out=xT[:, 4 * h:4 * h + 4, t * 128:(t + 1) * 128],
                                                                           in_=ps[:, bk, :].rearrange("p (a b) -> p a b", b=128)),
                           reads=[bank[bk]], writes=[b_xT[t // 4]])
          P.barrier()

          for l in range(depth):
              pv0 = l * NVL
              gen_banks[:] = [6, 7]
              bl = pv0 + 24
              P.op("dve", lambda e, bl=bl: e.tensor_tensor(out=w_t[:, 0:32],
                                                          in0=pvec[:, bl:bl + 32], in1=pvec[:, bl + 32:bl + 64], op=ALU.mult),
                   reads=[b_pv], writes=[b_wt])
              P.op("dve", lambda e, bl=bl: e.tensor_tensor(out=w_t[:, 32:64], in0=pvec[:, bl + 64:bl + 96], in1=pvec[:, bl + 96:bl + 128],
                                                          op=ALU.mult), reads=[b_pv], writes=[b_wt])
              P.op("dve", lambda e: e.reduce_sum(out=lam[:, 1:3], in_=w_t[:, 0:64].rearrange("p (a b) -> p a b", b=32), axis=AX.X),
                   reads=[b_wt], writes=[b_lam])
              P.op("act", lambda e: e.activation(out=lam[:, 3:5], in_=lam[:, 1:3], func=AF.Exp), reads=[b_lam], writes=[b_lam])
              P.op("dve", lambda e: e.tensor_tensor(out=lam[:, 5:6], in0=lam[:, 3:4], in1=lam[:, 4:5], op=ALU.subtract),
                   reads=[b_lam], writes=[b_lam])
              P.op("dve", lambda e: e.memset(lam[:, 0:1], 1.0), reads=[b_lam], writes=[b_lam])
              P.op("dve", lambda e, l=l: e.tensor_scalar(out=lam[64:128, 0:1], in0=lam[64:128, 5:6], scalar1=lambda_init(l), scalar2=None,
                                                         op0=ALU.add), reads=[b_lam], writes=[b_lam])

              for c in range(NCH):
                  norm_stats(c, KT[:, 0:2, :].rearrange("p a (c t) -> p (a c) t", t=CH), b_KT,
                             rstd1[:, c * CH:(c + 1) * CH], b_rstd1[c], w_ln, b_wln)

              def cqT(j, c):
                  return yT[:, 4 + j, c * CH:(c + 1) * CH]

              def ckvT(j, c):
                  return yT[:, j, c * CH:(c + 1) * CH]

              def krT(c):
                  return yT[0:32, 2, c * CH:(c + 1) * CH]

              for c in range(NCH):
                  make_hT(c, 0, rstd1[:, c * CH:(c + 1) * CH], b_rstd1[c], l)
                  s1, bs1 = load_w([(0, 8, 0, 512, wv(w_in_d[l, :, C_CQ:C_CQ + 512]))])
                  s2, bs2 = load_w([(0, 8, 0, 160, wv(w_in_d[l, :, C_CQ + 512:C_CQ + 672]))])

                  def wcol(j):
                      if j < 4:
                          return s1, bs1, j * 128
                      return s2, bs2, (j - 4) * 128
                  for (j0, nj, gcol, dst, bdst, ncols) in ((0, 3, 19, cqT, b_y[1][c], 384), (3, 2, 22, ckvT, b_y[0][c], 256)):
                      for jj in range(nj):
                          sl, bsl, co = wcol(j0 + jj)
                          bk = gbank()
                          mm_group(bk, ps[:, bk, :], [(sl[:, k, co:co + 128], hT[:, k, :]) for k in range(8)], reads=[bsl, b_hT])
                          P.op("act", lambda e, bk=bk: e.activation(out=w_sq, in_=ps[:, bk, :], func=AF.Square),
                               reads=[bank[bk]], writes=[b_wsq])
                          P.op("dve", lambda e, bk=bk, jj=jj, gcol=gcol, dst=dst: e.tensor_scalar(
                              out=dst(jj, c), in0=ps[:, bk, :], scalar1=pvec[:, pv0 + gcol + jj:pv0 + gcol + jj + 1], scalar2=None,
                              op0=ALU.mult), reads=[bank[bk], b_pv], writes=[bdst])

                          def fn(e, jj=jj, nj=nj):
                              return e.matmul(ps[:, 5, :], lhsT=cm(M_ONES), rhs=w_sq, start=(jj == 0), stop=(jj == nj - 1))
                          P.op("pe", fn, reads=[b_wsq, b_const], writes=[bank[5]])
                      rstd_from_bank(5, 128, 1.0 / ncols, w_rs, b_wrs, w_ln, b_wln)
                      for jj in range(nj):
                          P.op("dve", lambda e, jj=jj, dst=dst: e.tensor_tensor(out=dst(jj, c), in0=dst(jj, c), in1=w_rs, op=ALU.mult),
                               reads=[bdst, b_wrs], writes=[bdst])
                  sl, bsl, co = s2, bs2, 128
                  bk = gbank()
                  mm_group(bk, ps[0:32, bk, :], [(sl[:, k, co:co + 32], hT[:, k, :]) for k in range(8)], reads=[bsl, b_hT])
                  P.op("act", lambda e, bk=bk: e.copy(out=w_qn[0:32], in_=ps[0:32, bk, :]), reads=[bank[bk]], writes=[b_wqn])
                  t, bt = load_tab(tabC_d, c, 32, 64)
                  rotary(32, None, cmat[0:32, M_RA * 128:M_RA * 128 + 32], t, bt, krT(c), b_y[0][c])

              P.op("dve", lambda e: e.memset(KT[64:128, :, :], 0.0), writes=b_KT)
              P.op("dve", lambda e: e.memset(qTall[64:128, :, :], 0.0), writes=b_q)
              for g in range(4):
                  sl, bsl = load_w([
                      (0, 3, 0, 192, wv(w_uq_d[l, :, g * 192:(g + 1) * 192])),
                      (3, 2, 0, 192, wv(w_uk_d[l, :, g * 192:(g + 1) * 192])),
                      (5, 2, 0, 64, wv(w_ukv_d[l, :, (2 * g) * 128 + 64:(2 * g) * 128 + 128])),
                      (5, 2, 64, 64, wv(w_ukv_d[l, :, (2 * g + 1) * 128 + 64:(2 * g + 1) * 128 + 128])),
                  ])
                  gen_banks[:] = [0, 1, 2, 3, 4, 5, 6, 7]
                  for c in range(NCH):
                      for hh in range(2):
                          bk = gbank()
                          prs = [(sl[:, 3 + kk, hh * 96:(hh + 1) * 96], ckvT(kk, c)) for kk in range(2)]
                          prs.append((cm(M_SEL, 32, 96), krT(c)))

                          def fn(e, prs=prs, bk=bk):
                              e.matmul(ps[0:96, bk, :], lhsT=prs[0][0], rhs=prs[0][1], start=True, stop=False)
                              e.matmul(ps[0:96, bk, :], lhsT=prs[1][0], rhs=prs[1][1], start=False, stop=False)
                              return e.matmul(ps[0:96, bk, :], lhsT=prs[2][0], rhs=prs[2][1], start=False, stop=True)
                          P.op("pe", fn, reads=[bsl, b_y[0][c], b_const], writes=[bank[bk]])
                          P.op("act", lambda e, bk=bk, hh=hh, c=c: e.copy(out=KT[0:96, hh, c * CH:(c + 1) * CH], in_=ps[0:96, bk, :]),
                               reads=[bank[bk]], writes=[b_KT[c]])
                      bk = gbank()

                      def fnv(e, c=c, bk=bk, sl=sl):
                          ins = None
                          for tt in range(4):
                              for kk in range(2):
                                  ins = e.matmul(ps[:, bk, tt * 128:(tt + 1) * 128],
                                                 lhsT=yT[:, kk, c * CH + tt * 128:c * CH + (tt + 1) * 128],
                                                 rhs=sl[:, 5 + kk, 0:128], start=(kk == 0), stop=(kk == 1))
                          return ins
                      P.op("pe", fnv, reads=[bsl, b_y[0][c]], writes=[bank[bk]])
                      P.op("act", lambda e, bk=bk, c=c: e.copy(out=Vv[:, 4 * c:4 * c + 4, 0:128],
                                                               in_=ps[:, bk, :].rearrange("p (a b) -> p a b", b=128)),
                           reads=[bank[bk]], writes=[b_V[c]])
                  gen_banks[:] = [0, 1, 2, 3, 4, 5, 6, 7]
                  for c in range(NCH):
                      t, bt = load_tab(tabC_d, c, 96, 0)
                      bks = []
                      for hh in range(2):
                          bk = gbank()
                          bks.append(bk)
                          mm_group(bk, ps[0:96, bk, :], [(sl[:, kk, hh * 96:(hh + 1) * 96], cqT(kk, c)) for kk in range(3)],
                                   reads=[bsl, b_y[1][c]])
                      for hh in range(2):
                          bk = bks[hh]
                          P.op("act", lambda e, bk=bk: e.copy(out=w_qn[0:96], in_=ps[0:96, bk, :]), reads=[bank[bk]], writes=[b_wqn])
                          rotary(96, None, cmat[64:96, M_RC * 128:M_RC * 128 + 96], t, bt, qTall[0:96, hh, c * CH:(c + 1) * CH], b_q[c],
                                 krows=(64, 96))
                  gen_banks[:] = [6, 7]
                  pairs = []
                  for c in range(NCH):
                      pairs.append(dict(
                          q=(qTall[:, 0, c * CH:(c + 1) * CH], qTall[:, 1, c * CH:(c + 1) * CH]),
                          k=lambda m, kb: KT[:, m, kb * 128:(kb + 1) * 128],
                          v=lambda m, kb: Vv[:, kb, m * 64:(m + 1) * 64],
                          tp=None, reads=[b_q[c]] + b_KT + b_V,
                          finish=finish_plain(yT[:, 8 + g, c * CH:(c + 1) * CH], b_y[2][c])))
                  attention(pairs, 96.0 ** -0.5)
                  dump_and_stop("C0")

              for gb_ in range(2):
                  sl, bsl = load_w([(0, 8, 0, 256, wv(w_in_d[l, :, C_BQ + 256 * gb_:C_BQ + 256 * gb_ + 256])),
                                    (0, 8, 256, 256, wv(w_in_d[l, :, C_BK + 256 * gb_:C_BK + 256 * gb_ + 256]))])
                  sv, bsv = load_w([(0, 8, 0, 256, wv(w_in_d[l, :, C_BV + 256 * gb_:C_BV + 256 * gb_ + 256]))])
                  gen_banks[:] = [0, 1, 2, 3, 4, 5, 6, 7]
                  for c in range(NCH):
                      make_hT(c, 0, rstd1[:, c * CH:(c + 1) * CH], b_rstd1[c], l)
                      t, bt = load_tab(tabB_d, c)
                      tl = []
                      for (co, dst, bdst) in ((0, qTall, b_q[c]), (256, KT, b_KT[c])):
                          for i in range(2):
                              bk = gbank()
                              mm_group(bk, ps[:, bk, :], [(sl[:, k, co + i * 128:co + (i + 1) * 128], hT[:, k, :]) for k in range(8)],
                                       reads=[bsl, b_hT])
                              tl.append((bk, dst, bdst, i))
                      for (bk, dst, bdst, i) in tl:
                          P.op("act", lambda e, bk=bk: e.copy(out=w_qn, in_=ps[:, bk, :]), reads=[bank[bk]], writes=[b_wqn])
                          rotary(128, None, cm(M_RB), t, bt, dst[:, i, c * CH:(c + 1) * CH], bdst)
                      for tt in range(4):
                          bk = gbank()
                          mm_group(bk, ps[:, bk, 0:256], [(hT[:, k, tt * 128:(tt + 1) * 128], sv[:, k, 0:256]) for k in range(8)],
                                   reads=[bsv, b_hT])
                          P.op("act", lambda e, bk=bk, tt=tt, c=c: e.copy(out=Vv[:, 4 * c + tt, 0:256], in_=ps[:, bk, 0:256]),
                               reads=[bank[bk]], writes=[b_V[c]])
                  gen_banks[:] = [6, 7]
                  pairs = []
                  for c in range(NCH):
                      for hh in range(4):
                          i = hh // 2
                          base = 64 * (hh % 2)

                          def fin(hh=hh, i=i, c=c, gb_=gb_):
                              P.op("dve", lambda e: e.reciprocal(out=w_r, in_=ps[:, 5, :]), reads=[bank[5]], writes=[b_wrr])
                              P.op("dve", lambda e: e.scalar_tensor_tensor(out=w_yn, in0=ps[:, 4, :], scalar=lam[:, 0:1], in1=w_r,
                                                                           op0=ALU.mult, op1=ALU.mult),
                                   reads=[bank[4], b_wrr, b_lam], writes=[b_wyn])
                              half = hh % 2

                              def fz(e, half=half):
                                  if half == 0:
                                      return e.matmul(ps[0:64, 6, :], lhsT=cm(M_COMB, 128, 64), rhs=w_yn, start=True, stop=True)
                                  return e.matmul(ps[64:128, 6, :], lhsT=cm(M_COMB, 128, 64), rhs=w_yn, start=True, stop=True,
                                                  tile_position=(0, 64))
                              P.op("pe", fz, reads=[b_wyn, b_const], writes=[bank[6]])
                              if half == 1:
                                  P.op("act", lambda e: e.activation(out=w_sq, in_=ps[:, 6, :], func=AF.Square),
                                       reads=[bank[6]], writes=[b_wsq])
                                  mm_group(7, ps[:, 7, :], [(cm(M_BLK64), w_sq)], reads=[b_wsq, b_const])
                                  rstd_from_bank(7, 128, 1.0 / 64, w_rs, b_wrs, w_ln, b_wln, bias_col=cst[:, 1 + l:2 + l])
                                  P.op("dve", lambda e: e.scalar_tensor_tensor(
                                      out=yT[:, 4 + 2 * gb_ + i, c * CH:(c + 1) * CH], in0=ps[:, 6, :],
                                      scalar=pvec[:, pv0 + 18:pv0 + 19], in1=w_rs, op0=ALU.mult, op1=ALU.mult),
                                      reads=[bank[6], b_wrs, b_pv], writes=[b_y[1][c]])
                          pairs.append(dict(
                              q=(qTall[base:base + 32, i, c * CH:(c + 1) * CH], qTall[base + 32:base + 64, i, c * CH:(c + 1) * CH]),
                              k=lambda m, kb, i=i, base=base: KT[base + 32 * m:base + 32 * m + 32, i, kb * 128:(kb + 1) * 128],
                              v=lambda m, kb, hh=hh: Vv[:, kb, hh * 64:(hh + 1) * 64],
                              tp=((base, 0), (base + 32, 0)), reads=[b_q[c]] + b_KT + b_V, finish=fin))
                  attention(pairs, 32.0 ** -0.5)

              for ga in range(2):
                  sl, bsl = load_w([(0, 8, 0, 256, wv(w_in_d[l, :, C_AQ + 256 * ga:C_AQ + 256 * ga + 256])),
                                    (0, 8, 256, 128, wv(w_kdup_d[l, :, 128 * ga:128 * ga + 128])),
                                    (0, 8, 384, 64, wv(w_in_d[l, :, C_AV + 64 * ga:C_AV + 64 * ga + 64]))])
                  gen_banks[:] = [0, 1, 2, 3, 4, 5, 6, 7]
                  for c in range(NCH):
                      make_hT(c, 0, rstd1[:, c * CH:(c + 1) * CH], b_rstd1[c], l)
                      t, bt = load_tab(tabA_d, c)
                      tl = []
                      for (co, gcol, dst, bdst) in ((0, 16, qTall[:, 0, c * CH:(c + 1) * CH], b_q[c]),
                                                   (128, 16, qTall[:, 1, c * CH:(c + 1) * CH], b_q[c]),
                                                   (256, 17, KT[:, 0, c * CH:(c + 1) * CH], b_KT[c])):
                          bk = gbank()
                          mm_group(bk, ps[:, bk, :], [(sl[:, k, co:co + 128], hT[:, k, :]) for k in range(8)], reads=[bsl, b_hT])
                          tl.append((bk, gcol, dst, bdst))
                      for (bk, gcol, dst, bdst) in tl:
                          P.op("act", lambda e, bk=bk: e.activation(out=w_sq, in_=ps[:, bk, :], func=AF.Square),
                               reads=[bank[bk]], writes=[b_wsq])
                          bk2 = gbank()
                          mm_group(bk2, ps[:, bk2, :], [(cm(M_BLK64), w_sq)], reads=[b_wsq, b_const])
                          rstd_from_bank(bk2, 128, 1.0 / 64, w_rs, b_wrs, w_ln, b_wln)
                          P.op("dve", lambda e, bk=bk, gcol=gcol: e.scalar_tensor_tensor(
                              out=w_qn, in0=ps[:, bk, :], scalar=pvec[:, pv0 + gcol:pv0 + gcol + 1], in1=w_rs,
                              op0=ALU.mult, op1=ALU.mult), reads=[bank[bk], b_wrs, b_pv], writes=[b_wqn])
                          rotary(128, None, cm(M_RA), t, bt, dst, bdst)
                      bk = gbank()

                      def fnv(e, bk=bk, sl=sl):
                          ins = None
                          for tt in range(4):
                              for k in range(8):
                                  ins = e.matmul(ps[:, bk, tt * 64:(tt + 1) * 64], lhsT=hT[:, k, tt * 128:(tt + 1) * 128],
                                                 rhs=sl[:, k, 384:448], start=(k == 0), stop=(k == 7))
                          return ins
                      P.op("pe", fnv, reads=[bsl, b_hT], writes=[bank[bk]])
                      P.op("act", lambda e, bk=bk, c=c: e.copy(out=Vv[:, 4 * c:4 * c + 4, 0:64],
                                                               in_=ps[:, bk, 0:256].rearrange("p (a b) -> p a b", b=64)),
                           reads=[bank[bk]], writes=[b_V[c]])
                  gen_banks[:] = [6, 7]
                  pairs = []
                  for c in range(NCH):
                      for i in range(2):
                          pairs.append(dict(
                              q=(qTall[0:64, i, c * CH:(c + 1) * CH], qTall[64:128, i, c * CH:(c + 1) * CH]),
                              k=lambda m, kb: KT[64 * m:64 * m + 64, 0, kb * 128:(kb + 1) * 128],
                              v=lambda m, kb: Vv[:, kb, 0:64],
                              tp=((0, 0), (64, 0)), reads=[b_q[c]] + b_KT + b_V,
                              finish=finish_plain(yT[:, 2 * ga + i, c * CH:(c + 1) * CH], b_y[0][c])))
                  attention(pairs, 64.0 ** -0.5)

              P.barrier()
              if debug and seq == 0 and l == 0:
                  dbg_y = nc.dram_tensor("dbg_y", [128, 12, S], BF16, kind="ExternalOutput").ap()
                  dbg_r = nc.dram_tensor("dbg_r", [128, S], F32, kind="ExternalOutput").ap()
                  dbg_x = nc.dram_tensor("dbg_x", [128, 8, S], F32, kind="ExternalOutput").ap()
                  dbg_l = nc.dram_tensor("dbg_l", [128, 8], F32, kind="ExternalOutput").ap()
                  b_dbg = P.buf("dbg")
                  d1 = P.op("sp", lambda e: [e.dma_start(out=dbg_y, in_=yT[:]), e.dma_start(out=dbg_r, in_=rstd1[:]),
                                             e.dma_start(out=dbg_x, in_=xT[:]), e.dma_start(out=dbg_l, in_=lam[:])],
                            reads=[b_dbg], dma=True, ndma=4)
                  P.barrier_op("sp", [d1])
                  P.barrier()
              gen_banks[:] = [4, 5, 6, 7]
              for c in range(NCH):
                  cs = slice(c * CH, (c + 1) * CH)
                  make_hT(c, 0, rstd1[:, cs], b_rstd1[c], l)
                  for ch in range(2):
                      for n in range(3):
                          sg, bsg = load_w([(0, 8, 0, 512, wv(w_in_d[l, :, C_G + n * D + ch * 512:C_G + n * D + ch * 512 + 512]))], key=(l, ch * 6 + n * 2))
                          sb_, bsb = load_w([(0, 4, 0, 512, wv(w_br_d[l, n, :, ch * 512:(ch + 1) * 512]))], key=(l, ch * 6 + n * 2 + 1))
                          for cg in range(4):
                              bg = gbank()
                              mm_group(bg, ps[:, bg, :], [(sg[:, k, cg * 128:(cg + 1) * 128], hT[:, k, :]) for k in range(8)],
                                       reads=[bsg, b_hT])
                              bz = gbank()
                              mm_group(bz, ps[:, bz, :], [(sb_[:, kk, cg * 128:(cg + 1) * 128], yT[:, 4 * n + kk, cs]) for kk in range(4)],
                                       reads=[bsb, b_y[n][c]])
                              gi = (n * 4 + cg) % 2
                              P.op("act", lambda e, bg=bg, gi=gi: e.activation(out=Gs[gi], in_=ps[:, bg, :], func=AF.Sigmoid),
                                   reads=[bank[bg]], writes=[b_Gs[gi]])
                              if n == 0:
                                  P.op("dve", lambda e, bz=bz, gi=gi, cg=cg: e.tensor_tensor(out=macc[:, cg, :], in0=ps[:, bz, :], in1=Gs[gi],
                                                                                           op=ALU.mult),
                                       reads=[bank[bz], b_Gs[gi]], writes=[b_hid[0]])
                              else:
                                  P.op("dve", lambda e, bz=bz, gi=gi: e.tensor_tensor(out=tmpm, in0=ps[:, bz, :], in1=Gs[gi], op=ALU.mult),
                                       reads=[bank[bz], b_Gs[gi]], writes=[b_tmpm])
                                  if n == 1:
                                      P.op(EW, lambda e, cg=cg: e.tensor_tensor(out=macc[:, cg, :], in0=macc[:, cg, :], in1=tmpm, op=ALU.add),
                                           reads=[b_hid[0], b_tmpm], writes=[b_hid[0]])
                                  else:
                                      P.op(EW, lambda e, cg=cg, ch=ch: e.tensor_tensor(out=mT[:, ch * 4 + cg, :], in0=macc[:, cg, :], in1=tmpm,
                                                                                         op=ALU.add),
                                           reads=[b_hid[0], b_tmpm], writes=[b_mT])
                  for ch in range(2):
                      so, bso = load_w([(0, 8, 0, 512, wv(w_out_d[l, :, ch * 512:(ch + 1) * 512]))], key=(l, 12 + ch))
                      for cg in range(4):
                          bk = gbank()
                          mm_group(bk, ps[:, bk, :], [(so[:, k, cg * 128:(cg + 1) * 128], mT[:, k, :]) for k in range(8)], reads=[bso, b_mT])
                          P.op("dve", lambda e, bk=bk, ch=ch, cg=cg: e.tensor_tensor(out=xT[:, ch * 4 + cg, cs], in0=ps[:, bk, :],
                                                                                     in1=xT[:, ch * 4 + cg, cs], op=ALU.add),
                               reads=[bank[bk], b_xT[c]], writes=[b_xT[c]])
                  norm_stats(c, sq8, [b_hid[1]], w_rs2, b_wrs2, w_ln2, b_wln2)
                  make_hT(c, 8, w_rs2, b_wrs2, l)
                  for b in range(8):
                      s1, bs1 = load_w([(0, 8, 0, 512, wv(w_ff1_d[l, :, b * 512:(b + 1) * 512]))], key=(l, 14 + b))
                      for cg in range(4):
                          j = b * 4 + cg
                          bk = gbank()
                          mm_group(bk, ps[:, bk, :], [(s1[:, k, cg * 128:(cg + 1) * 128], hT[:, k, :]) for k in range(8)], reads=[bs1, b_hT])
                          ri = j % 2
                          P.op("act", lambda e, bk=bk, ri=ri: e.activation(out=rl[ri], in_=ps[:, bk, :], func=AF.Relu),
                               reads=[bank[bk]], writes=[b_rl[ri]])
                          P.op(EW, lambda e, j=j, ri=ri: e.tensor_tensor(out=hid[:, j, :], in0=rl[ri], in1=rl[ri], op=ALU.mult),
                               reads=[b_rl[ri]], writes=[b_hid[j // 8]])
                  for ch in range(2):
                      for kb in range(4):
                          s2, bs2 = load_w([(0, 8, 0, 512, wv(w_ff2_d[l, kb * D:(kb + 1) * D, ch * 512:(ch + 1) * 512]))], key=(l, 22 + ch * 4 + kb))
                          for cg in range(4):
                              def fn(e, s2=s2, kb=kb, cg=cg):
                                  ins = None
                                  for k in range(8):
                                      ins = e.matmul(ps[:, cg, :], lhsT=s2[:, k, cg * 128:(cg + 1) * 128], rhs=hid[:, kb * 8 + k, :],
                                                     start=(kb == 0 and k == 0), stop=(kb == 3 and k == 7))
                                  return ins
                              P.op("pe", fn, reads=[bs2, b_hid[kb]], writes=[bank[cg]])
                      for cg in range(4):
                          P.op("dve", lambda e, ch=ch, cg=cg: e.tensor_tensor(out=xT[:, ch * 4 + cg, cs], in0=ps[:, cg, :],
                                                                              in1=xT[:, ch * 4 + cg, cs], op=ALU.add),
                               reads=[bank[cg], b_xT[c]], writes=[b_xT[c]])
              P.barrier()

          gen_banks[:] = [0, 1, 2, 3, 4, 5, 6, 7]
          fg0 = DEPTH * NVL
          last_store = []
          for c in range(NCH):
              norm_stats(c, sqf, [b_sqf], w_rs3, b_wrs3, w_ln3, b_wln3)
              for k in range(8):
                  P.op("dve", lambda e, k=k, c=c: e.scalar_tensor_tensor(out=ofin[:, k, :], in0=xT[:, k, c * CH:(c + 1) * CH],
                                                                         scalar=pvec[:, fg0 + k:fg0 + k + 1], in1=w_rs3,
                                                                         op0=ALU.mult, op1=ALU.mult),
                       reads=[b_xT[c], b_wrs3, b_pv], writes=[b_ofin])
              for tt in range(4):
                  sgi = tt % 2
                  for h in range(2):
                      bk = gbank()

                      def fn(e, tt=tt, h=h, bk=bk):
                          ins = None
                          for j in range(4):
                              ins = e.transpose(out=ps[:, bk, j * 128:(j + 1) * 128], in_=ofin[:, 4 * h + j, tt * 128:(tt + 1) * 128],
                                                identity=ident[:])
                          return ins
                      P.op("pe", fn, reads=[b_ofin, b_pv], writes=[bank[bk]])
                      if h == 0:
                          P.op("act", lambda e, sgi=sgi, bk=bk: e.copy(out=stage[sgi][:, 0:512], in_=ps[:, bk, :]),
                               reads=[bank[bk]], writes=[b_stage[sgi]])
                      else:
                          P.op("dve", lambda e, sgi=sgi, bk=bk: e.tensor_copy(out=stage[sgi][:, 512:1024], in_=ps[:, bk, :]),
                               reads=[bank[bk]], writes=[b_stage[sgi]])
                  t0 = c * CH + tt * 128
                  o = P.op("sp", lambda e, sgi=sgi, t0=t0: e.dma_start(out=out_d[seq, t0:t0 + 128, :], in_=stage[sgi]),
                           reads=[b_stage[sgi]], dma=True)
                  last_store.append(o)
          P.barrier_op("sp", last_store)


    except _Stop:
        pass
    P.emit()
    return nc, P


def _angles(pos, dim, theta):
    inv = (np.float32(theta) ** (-np.arange(0, dim, 2, dtype=np.float32) / np.float32(dim))).astype(np.float32)
    return (pos.astype(np.float32)[:, None] * inv[None, :]).astype(np.float32)


def _tables():
    t = np.arange(S)
    row_idx = (t // 64)
    col_idx = (t % 64)
    row_ang = _angles(row_idx, 32, 10000.0)
    col_ang = _angles(col_idx, 32, 10000.0)
    b_ang = _angles(t, 8, 500000.0)
    c_ang = _angles(t, 32, 10000.0)
    tabA = np.zeros((128, 2, S), np.float32)
    tabB = np.zeros((128, 2, S), np.float32)
    tabC = np.zeros((128, 2, S), np.float32)
    tabB[:, 0, :] = 1.0
    tabC[:, 0, :] = 1.0
    for p in range(128):
        d = p % 64
        ang = row_ang[:, d % 16] if d < 32 else col_ang[:, (d - 32) % 16]
        tabA[p, 0] = np.cos(ang)
        tabA[p, 1] = np.sin(ang)
        d = p % 32
        if d < 8:
            ang = b_ang[:, d % 4]
            tabB[p, 0] = np.cos(ang)
            tabB[p, 1] = np.sin(ang)
        if 64 <= p < 96:
            ang = c_ang[:, (p - 64) % 16]
            tabC[p, 0] = np.cos(ang)
            tabC[p, 1] = np.sin(ang)
    return tabA, tabB, tabC


def _cmat():
    m = np.zeros((7, 128, 128), np.float32)
    m[M_ONES] = 1.0
    m[M_BLK64, 0:64, 0:64] = 1.0
    m[M_BLK64, 64:128, 64:128] = 1.0
    for blk in range(4):
        o = blk * 32
        for i in range(16):
            m[M_RA, o + i + 16, o + i] = -1.0
            m[M_RA, o + i, o + i + 16] = 1.0
        for i in range(4):
            m[M_RB, o + i + 4, o + i] = -1.0
            m[M_RB, o + i, o + i + 4] = 1.0
    for i in range(16):
        m[M_RC, 64 + i + 16, 64 + i] = -1.0
        m[M_RC, 64 + i, 64 + i + 16] = 1.0
    for i in range(32):
        m[M_SEL, i, 64 + i] = 1.0
    for i in range(64):
        m[M_COMB, i, i] = 1.0
        m[M_COMB, 64 + i, i] = -1.0
    return np.ascontiguousarray(m.transpose(1, 0, 2).reshape(128, 7 * 128))


def _pvec(ln1_g, ln2_g, a_q_norm, a_k_norm, b_lambda, b_subln, c_q_norm, c_kv_norm, final_g):
    pv = np.zeros((128, DEPTH * NVL + 8), np.float32)
    idx = np.arange(128)
    for l in range(DEPTH):
        o = l * NVL
        pv[:, o:o + 8] = ln1_g[l].reshape(8, 128).T
        pv[:, o + 8:o + 16] = ln2_g[l].reshape(8, 128).T
        pv[:, o + 16] = a_q_norm[l][idx % 64]
        pv[:, o + 17] = a_k_norm[l][idx % 64]
        pv[:, o + 18] = b_subln[l][idx % 64]
        pv[:, o + 19:o + 22] = c_q_norm[l].reshape(3, 128).T
        pv[:, o + 22:o + 24] = c_kv_norm[l].reshape(2, 128).T
        pv[:, o + 24:o + 152] = b_lambda[l].reshape(1, 128)
    pv[:, DEPTH * NVL:] = final_g.reshape(8, 128).T
    return pv


_CACHE = {}


def kernel(x, ln1_g, w_in, a_q_norm, a_k_norm, b_lambda, b_subln, c_q_norm, c_kv_norm,
           c_w_uq, c_w_ukv, w_branch, w_out, ln2_g, w_ff1, w_ff2, final_g):
    f = lambda a: np.ascontiguousarray(np.asarray(a, dtype=np.float32))
    x = f(x)
    w_in = f(w_in)
    c_w_ukv = f(c_w_ukv)
    n_cores = 8
    w_kdup = np.empty((DEPTH, D, 256), np.float32)
    for j in range(2):
        wk = w_in[:, :, C_AK + 64 * j:C_AK + 64 * j + 64]
        w_kdup[:, :, 128 * j:128 * j + 64] = wk
        w_kdup[:, :, 128 * j + 64:128 * j + 128] = wk
    w_uk96 = np.zeros((DEPTH, 256, 8, 96), np.float32)
    w_uk96[:, :, :, 0:64] = c_w_ukv.reshape(DEPTH, 256, 8, 128)[:, :, :, 0:64]
    w_uk96 = w_uk96.reshape(DEPTH, 256, 768)
    tabA, tabB, tabC = _tables()
    shared = {
        "w_in": w_in, "w_kdup": w_kdup, "c_w_uq": f(c_w_uq), "w_uk96": w_uk96, "c_w_ukv": c_w_ukv,
        "w_branch": f(w_branch), "w_out": f(w_out), "w_ff1": f(w_ff1), "w_ff2": f(w_ff2),
        "pvec": _pvec(f(ln1_g), f(ln2_g), f(a_q_norm), f(a_k_norm), f(b_lambda), f(b_subln), f(c_q_norm), f(c_kv_norm), f(final_g)),
        "cmat": _cmat(), "ident": np.eye(128, dtype=np.float32), "tabA": tabA, "tabB": tabB, "tabC": tabC,
    }
    if "nc" not in _CACHE:
        _CACHE["nc"] = build_program()[0]
    nc = _CACHE["nc"]
    in_maps = []
    for c in range(n_cores):
        m = dict(shared)
        m["x"] = np.ascontiguousarray(x[NSEQ * c:NSEQ * (c + 1)])
        in_maps.append(m)
    res = run_bass_kernel_spmd(nc, in_maps, core_ids=list(range(n_cores)))
    out = np.concatenate([np.asarray(r["out"]) for r in res.results], axis=0)
    return out.astype(np.float32)
```

```python
import contextlib
import math
import numpy as np
import concourse.bass as bass
import concourse.mybir as mybir
from concourse.bass_utils import run_bass_kernel_spmd

F32 = mybir.dt.float32
BF16 = mybir.dt.bfloat16
AF = mybir.ActivationFunctionType
ALU = mybir.AluOpType
AX = mybir.AxisListType

ENGS = ("pe", "act", "dve", "pool", "sp")
EW = "dve"
EW2 = "dve"
NDUMMY = 0


class Buf:
    __slots__ = ("name", "writers", "readers", "dsem", "dcnt", "psum")

    def __init__(self, name, psum=False):
        self.name = name
        self.psum = psum
        self.writers = {}
        self.readers = {}
        self.dsem = None
        self.dcnt = 0


class _Rec:
    def __init__(self):
        self.calls = []

    def __getattr__(self, name):
        def f(*a, **kw):
            self.calls.append((name, a, kw))
            return len(self.calls) - 1
        return f


class _Replay:
    __slots__ = ("calls", "ret")

    def __init__(self, fn):
        r = _Rec()
        self.ret = fn(r)
        self.calls = r.calls

    def __call__(self, eng):
        res = [getattr(eng, n)(*a, **kw) for (n, a, kw) in self.calls]
        if isinstance(self.ret, (list, tuple)):
            return [res[i] for i in self.ret]
        return res[self.ret]


class Op:
    __slots__ = ("eng", "fn", "deps", "marked", "done", "is_dma")

    def __init__(self, eng, fn):
        self.eng = eng
        self.fn = fn
        self.deps = []
        self.marked = False
        self.done = None
        self.is_dma = False


class Prog:
    def __init__(self, nc):
        self.nc = nc
        self.ops = {e: [] for e in ENGS}
        self.ndsem = 0
        self.stack = contextlib.ExitStack()
        self.nbuf = 0

    def sb(self, name, shape, dtype):
        return self.stack.enter_context(self.nc.sbuf_tensor(name, list(shape), dtype))

    def ps(self, name, shape, dtype):
        return self.stack.enter_context(self.nc.psum_tensor(name, list(shape), dtype))

    def buf(self, name=None, psum=False):
        self.nbuf += 1
        return Buf(name or ("b%d" % self.nbuf), psum)

    def _dep(self, op, prod):
        if prod is op:
            return
        if prod.eng == op.eng and not prod.is_dma:
            return
        op.deps.append(prod)
        if not prod.is_dma:
            prod.marked = True

    def op(self, eng, fn, reads=(), writes=(), dma=False, ndma=1):
        o = Op(eng, _Replay(fn))
        o.is_dma = dma
        for b in reads:
            for w in b.writers.values():
                if w.eng == eng and not w.is_dma:
                    if eng != "pe" and not dma:
                        o.deps.append(w)
                        w.marked = True
                else:
                    self._dep(o, w)
            if b.psum:
                for r in b.readers.values():
                    self._dep(o, r)
        for b in writes:
            for r in b.readers.values():
                self._dep(o, r)
            for w in b.writers.values():
                self._dep(o, w)
        for b in reads:
            b.readers[("dma%d" % id(o)) if dma else eng] = o
        for b in writes:
            b.readers.clear()
            b.writers.clear()
            b.writers[("dma" if dma else eng)] = o
        if dma:
            tgt = writes[0] if writes else reads[0]
            if tgt.dsem is None:
                tgt.dsem = "d%d" % self.ndsem
                self.ndsem += 1
            tgt.dcnt += 16 * ndma
            o.done = (tgt.dsem, tgt.dcnt)
        self.ops[eng].append(o)
        return o

    def barrier_op(self, eng, deps):
        o = Op(eng, None)
        for d in deps:
            if d is None:
                continue
            o.deps.append(d)
            if not d.is_dma:
                d.marked = True
        self.ops[eng].append(o)
        return o

    def barrier(self, engs=ENGS):
        engs = [e for e in engs if e in self.ops]
        last = {}
        for e in engs:
            lst = [o for o in self.ops[e] if o.fn is not None]
            last[e] = lst[-1] if lst else None
        for e in engs:
            self.barrier_op(e, [last[f] for f in engs if f != e])

    def emit(self):
        nc = self.nc
        st = self.stack
        sems = {}
        for e in ENGS:
            sems[e] = st.enter_context(nc.semaphore("s_" + e))
        for i in range(self.ndsem):
            sems["d%d" % i] = st.enter_context(nc.semaphore("sd%d" % i))
        for e in ENGS:
            c = 0
            for o in self.ops[e]:
                if o.is_dma:
                    continue
                if o.marked and o.fn is not None:
                    c += 1
                    o.done = (e, c)
                elif o.fn is None:
                    o.done = (e, c)
        self.stats = {e: len(self.ops[e]) for e in ENGS}

        def run(e, eng):
            waited = {}
            for o in self.ops[e]:
                need = {}
                for d in o.deps:
                    k, v = d.done
                    if v > need.get(k, 0):
                        need[k] = v
                for k, v in need.items():
                    if v > waited.get(k, 0):
                        eng.wait_ge(sems[k], v)
                        waited[k] = v
                if o.fn is None:
                    continue
                ins = o.fn(eng)
                if o.is_dma:
                    for i_ in (ins if isinstance(ins, (list, tuple)) else [ins]):
                        i_.then_inc(sems[o.done[0]], 16)
                elif o.marked:
                    ins.then_inc(sems[e], 1)

        with nc.Block() as block:
            @block.tensor
            def _(t):
                run("pe", t)

            @block.scalar
            def _(s):
                run("act", s)

            @block.vector
            def _(v):
                run("dve", v)

            @block.gpsimd
            def _(g):
                run("pool", g)

            @block.sync
            def _(s):
                run("sp", s)
        st.close()


D = 1024
S = 2048
DEPTH = 4
NSEQ = 2
CH = 512
NCH = S // CH
IN_COLS = 6048
EPS = 1e-6
C_AQ, C_AK, C_AV = 0, 512, 640
C_BQ, C_BK, C_BV = 768, 1280, 1792
C_CQ, C_CKV, C_KR = 2304, 2688, 2944
C_G = 2976
NVL = 152
M_ONES, M_BLK64, M_RA, M_RB, M_RC, M_SEL, M_COMB = range(7)


def lambda_init(l):
    return 0.8 - 0.6 * math.exp(-0.3 * l)


class _Stop(Exception):
    pass


def build_program(depth=DEPTH, nseq=NSEQ, wring=3, debug=False, stop_at=None):
    nc = bass.Bass("TRN2", target_bir_lowering=False)
    P = Prog(nc)

    def din(name, shape):
        return nc.dram_tensor(name, list(shape), F32, kind="ExternalInput").ap()

    x_d = din("x", [nseq, S, D])
    w_in_d = din("w_in", [DEPTH, D, IN_COLS])
    w_kdup_d = din("w_kdup", [DEPTH, D, 256])
    w_uq_d = din("c_w_uq", [DEPTH, 384, 768])
    w_uk_d = din("w_uk96", [DEPTH, 256, 768])
    w_ukv_d = din("c_w_ukv", [DEPTH, 256, 1024])
    w_br_d = din("w_branch", [DEPTH, 3, 512, D])
    w_out_d = din("w_out", [DEPTH, D, D])
    w_ff1_d = din("w_ff1", [DEPTH, D, 4 * D])
    w_ff2_d = din("w_ff2", [DEPTH, 4 * D, D])
    pvec_d = din("pvec", [128, DEPTH * NVL + 8])
    cmat_d = din("cmat", [128, 7 * 128])
    ident_d = din("ident", [128, 128])
    tabA_d = din("tabA", [128, 2, S])
    tabB_d = din("tabB", [128, 2, S])
    tabC_d = din("tabC", [128, 2, S])
    out_d = nc.dram_tensor("out", [nseq, S, D], F32, kind="ExternalOutput").ap()

    xT = P.sb("xT", [128, 8, S], F32)
    yT = P.sb("yT", [128, 12, S], BF16)
    pvec = P.sb("pvec_sb", [128, DEPTH * NVL + 8], F32)
    cmat = P.sb("cmat_sb", [128, 7 * 128], BF16)
    ident = P.sb("ident_sb", [128, 128], F32)
    cst = P.sb("cst", [128, 8], F32)
    lam = P.sb("lam", [128, 8], F32)
    rstd1 = P.sb("rstd1", [128, S], F32)
    hT = P.sb("hT", [128, 8, CH], BF16)
    wr = [P.sb("wr%d" % i, [128, 8, 512], BF16) for i in range(wring)]
    SCR = 52224
    scr = P.sb("scr", [128, SCR // 2], BF16)
    ps = P.ps("ps", [128, 8, 512], F32)

    off = [0]

    def carve(nbytes):
        o = off[0]
        off[0] += nbytes
        assert off[0] <= SCR, (off[0], SCR)
        return o

    def v_bf(o, shape):
        n = int(np.prod(shape))
        a = scr[:, o // 2:o // 2 + n]
        if len(shape) == 2:
            return a.rearrange("p (a b) -> p a b", b=shape[1])
        return a

    def v_f32(o, shape):
        n = int(np.prod(shape))
        a = scr[:, o // 2:o // 2 + 2 * n].bitcast(F32)
        if len(shape) == 2:
            return a.rearrange("p (a b) -> p a b", b=shape[1])
        return a

    off[0] = 0
    KT = v_bf(carve(8192), [2, S])
    Vv = v_bf(carve(8192), [16, 256])
    qTall = v_bf(carve(8192), [2, S])
    PT = [v_bf(carve(2048), [2, CH]) for _ in range(3)]
    tab = [v_f32(carve(4096), [2, CH]) for _ in range(2)]
    w_ln = v_f32(carve(2048), [CH])
    w_rs = v_f32(carve(2048), [CH])
    o_wt = carve(2048)
    w_t = v_f32(o_wt, [CH])
    w_u = v_f32(carve(2048), [CH])
    acc = v_f32(o_wt, [2, CH])
    w_r = v_f32(carve(2048), [CH])
    o_wsq = carve(1024)
    w_sq = v_bf(o_wsq, [CH])
    w_qn = v_bf(carve(1024), [CH])
    accbf = v_bf(o_wsq, [2, CH])
    w_yn = v_bf(carve(1024), [CH])
    att_end = off[0]
    off[0] = 0
    mT = v_bf(carve(8192), [8, CH])
    hid = v_bf(carve(32768), [32, CH])
    Gs = [v_f32(carve(2048), [CH]) for _ in range(2)]
    tmpm = v_f32(carve(2048), [CH])
    rl = Gs
    w_ln2 = v_f32(carve(2048), [CH])
    w_rs2 = v_f32(carve(2048), [CH])
    loc_end = off[0]
    macc = scr[:, (8192 // 2):(8192 // 2) + 4096].bitcast(F32).rearrange("p (a b) -> p a b", b=CH)
    sq8 = scr[:, (16384 // 2):(16384 // 2) + 4096].rearrange("p (a b) -> p a b", b=CH)
    off[0] = 0
    stage = [v_f32(carve(4096), [D]) for _ in range(2)]
    ofin = v_f32(carve(16384), [8, CH])
    sqf = v_bf(carve(8192), [8, CH])
    w_ln3 = v_f32(carve(2048), [CH])
    w_rs3 = v_f32(carve(2048), [CH])

    bank = [P.buf("bank%d" % i, psum=True) for i in range(8)]
    b_xT = [P.buf("xT%d" % c) for c in range(NCH)]
    b_y = [[P.buf("y%d_%d" % (n, c)) for c in range(NCH)] for n in range(3)]
    b_const = P.buf("const")
    b_pv = P.buf("pvec")
    b_cst = P.buf("cst")
    b_lam = P.buf("lam")
    b_rstd1 = [P.buf("rstd1_%d" % c) for c in range(NCH)]
    b_hT = P.buf("hT")
    b_wr = [P.buf("wr%d" % i) for i in range(wring)]
    b_KT = [P.buf("KT%d" % c) for c in range(NCH)]
    b_V = [P.buf("V%d" % c) for c in range(NCH)]
    b_q = [P.buf("q%d" % c) for c in range(NCH)]
    b_PT = [P.buf("PT%d" % i) for i in range(3)]
    b_tab = [P.buf("tab%d" % i) for i in range(2)]
    b_wln, b_wrs, b_wt, b_wu, b_wrr, b_wsq, b_wqn, b_wyn = [P.buf() for _ in range(8)]
    b_mT = P.buf("mT")
    b_hid = [P.buf("hid%d" % i) for i in range(4)]
    b_Gs = [P.buf(), P.buf()]
    b_tmpm = P.buf()
    b_rl = b_Gs
    b_wln2, b_wrs2 = P.buf(), P.buf()
    b_stage = [P.buf(), P.buf()]
    b_ofin, b_sqf, b_wln3, b_wrs3 = P.buf(), P.buf(), P.buf(), P.buf()

    def cm(i, rows=128, cols=128):
        return cmat[0:rows, i * 128:i * 128 + cols]

    gen_banks = [6, 7]
    gb_i = [0]

    def gbank():
        b = gen_banks[gb_i[0] % len(gen_banks)]
        gb_i[0] += 1
        return b

    wr_i = [0]

    wsc = nc.dram_tensor("wsc", [DEPTH, 30, 128, 8, 512], BF16).ap()
    b_wsc = [P.buf("wsc%d" % i) for i in range(DEPTH)]
    cached = set()

    def load_w(parts, key=None):
        s = wr_i[0] % wring
        wr_i[0] += 1
        slot = wr[s]
        if key is not None and key in cached:
            kc = parts[0][1]
            P.op("sp", lambda e: e.dma_start(out=slot[:, 0:kc, :], in_=wsc[key[0], key[1], :, 0:kc, :]),
                 reads=[b_wsc[key[0]]], writes=[b_wr[s]], dma=True)
            return slot, b_wr[s]

        def fn(e, parts=parts, slot=slot):
            res = []
            for p_ in parts:
                k0, kc, c0, ncol, ap = p_[:5]
                dst = slot[:, k0:k0 + kc, c0:c0 + ncol]
                if len(p_) > 5:
                    dst = p_[5](dst)
                res.append(e.dma_start(out=dst, in_=ap))
            return res
        P.op("pool", fn, writes=[b_wr[s]], dma=True, ndma=len(parts))
        if key is not None:
            kc = parts[0][1]
            P.op("sp", lambda e: e.dma_start(out=wsc[key[0], key[1], :, 0:kc, :], in_=slot[:, 0:kc, :]),
                 reads=[b_wr[s]], writes=[b_wsc[key[0]]], dma=True)
            cached.add(key)
        return slot, b_wr[s]

    def wv(ap2d):
        return ap2d.rearrange("(k p) n -> p k n", p=128)

    def mm_group(bk, out_ap, pairs, reads, tp=None):
        def fn(e, pairs=pairs, out_ap=out_ap, tp=tp):
            n = len(pairs)
            ins = None
            for i, (l, r) in enumerate(pairs):
                if tp is None:
                    ins = e.matmul(out_ap, lhsT=l, rhs=r, start=(i == 0), stop=(i == n - 1))
                else:
                    ins = e.matmul(out_ap, lhsT=l, rhs=r, start=(i == 0), stop=(i == n - 1), tile_position=tp)
            return ins
        return P.op("pe", fn, reads=reads, writes=[bank[bk]])

    def rstd_from_bank(bk, np_, scale, out_ap, b_out, ln_ap, b_ln, bias_col=None, extra_reads=()):
        P.op("act", lambda e: e.activation(out=ln_ap[0:np_], in_=ps[0:np_, bk, :], func=AF.Ln, bias=cst[0:np_, 0:1], scale=scale),
             reads=[bank[bk], b_cst], writes=[b_ln])
        if bias_col is None:
            P.op("act", lambda e: e.activation(out=out_ap[0:np_], in_=ln_ap[0:np_], func=AF.Exp, scale=-0.5),
                 reads=[b_ln], writes=[b_out])
        else:
            P.op("act", lambda e: e.activation(out=out_ap[0:np_], in_=ln_ap[0:np_], func=AF.Exp, scale=-0.5, bias=bias_col),
                 reads=[b_ln, b_cst], writes=[b_out])

    tab_i = [0]

    def load_tab(src, c, np_=128, p0=0):
        i = tab_i[0] % 2
        tab_i[0] += 1
        t = tab[i]
        P.op("sp", lambda e: e.dma_start(out=t[0:np_], in_=src[p0:p0 + np_, :, c * CH:(c + 1) * CH]), writes=[b_tab[i]], dma=True)
        return t, b_tab[i]

    def rotary(np_, qn_reads, rmat, t, b_t, out_ap, b_out, krows=None):
        bk = gbank()
        k0, k1 = krows if krows is not None else (0, np_)
        mm_group(bk, ps[0:np_, bk, :], [(rmat, w_qn[k0:k1])], reads=[b_wqn, b_const])
        P.op("dve", lambda e: e.tensor_tensor(out=w_t[0:np_], in0=ps[0:np_, bk, :], in1=t[0:np_, 1, :], op=ALU.mult),
             reads=[bank[bk], b_t], writes=[b_wt])
        P.op(EW2, lambda e: e.tensor_tensor(out=w_u[0:np_], in0=w_qn[0:np_], in1=t[0:np_, 0, :], op=ALU.mult),
             reads=[b_wqn, b_t], writes=[b_wu])
        P.op(EW2, lambda e: e.tensor_tensor(out=out_ap, in0=w_u[0:np_], in1=w_t[0:np_], op=ALU.add),
             reads=[b_wu, b_wt], writes=[b_out])

    def make_hT(c, gcol0, rstd_ap, b_rstd, l):
        for k in range(8):
            P.op("dve", lambda e, k=k: e.scalar_tensor_tensor(out=hT[:, k, :], in0=xT[:, k, c * CH:(c + 1) * CH],
                                                              scalar=pvec[:, l * NVL + gcol0 + k:l * NVL + gcol0 + k + 1],
                                                              in1=rstd_ap, op0=ALU.mult, op1=ALU.mult),
                 reads=[b_xT[c], b_rstd, b_pv], writes=[b_hT])

    def norm_stats(c, sq_ap, b_sq_list, out_ap, b_out, ln_ap, b_ln):
        P.op("act", lambda e: e.activation(out=sq_ap, in_=xT[:, :, c * CH:(c + 1) * CH], func=AF.Square),
             reads=[b_xT[c]], writes=b_sq_list)
        bk = gbank()
        mm_group(bk, ps[:, bk, :], [(cm(M_ONES), sq_ap[:, k, :]) for k in range(8)], reads=b_sq_list + [b_const])
        rstd_from_bank(bk, 128, 1.0 / D, out_ap, b_out, ln_ap, b_ln)

    def attention(pairs, scale):
        units = [(pi, kb) for pi in range(len(pairs)) for kb in range(16)]

        def qk(u):
            pi, kb = units[u]
            pr = pairs[pi]
            s = u % 2

            def fn(e, pr=pr, kb=kb, s=s):
                ins = None
                for m in range(2):
                    kw = {}
                    if pr["tp"] is not None:
                        kw["tile_position"] = pr["tp"][m]
                    ins = e.matmul(ps[:, 2 * s + m, :], lhsT=pr["k"](m, kb), rhs=pr["q"][m], start=True, stop=True, **kw)
                return ins
            P.op("pe", fn, reads=pr["reads"], writes=[bank[2 * s], bank[2 * s + 1]])

        def ex_pv(u):
            pi, kb = units[u]
            pr = pairs[pi]
            s = u % 2
            pt = PT[u % 3]
            bpt = b_PT[u % 3]
            P.op("act", lambda e: e.activation(out=pt, in_=ps[:, 2 * s:2 * s + 2, :], func=AF.Exp, scale=scale),
                 reads=[bank[2 * s], bank[2 * s + 1]], writes=[bpt])

            on_dve = (kb % 2 == 1)

            def fn(e, pr=pr, kb=kb, pt=pt, on_dve=on_dve):
                st_, sp_ = (kb == 0), (kb == 15)
                ins = e.matmul(ps[0:64, 4, :], lhsT=pr["v"](0, kb), rhs=pt[:, 0, :], start=st_, stop=sp_)
                ins = e.matmul(ps[64:128, 4, :], lhsT=pr["v"](1, kb), rhs=pt[:, 1, :], start=st_, stop=sp_, tile_position=(0, 64))
                if not on_dve:
                    e.matmul(ps[0:64, 5, :], lhsT=cm(M_ONES, 128, 64), rhs=pt[:, 0, :], start=st_, stop=False)
                    ins = e.matmul(ps[64:128, 5, :], lhsT=cm(M_ONES, 128, 64), rhs=pt[:, 1, :], start=st_, stop=False,
                                   tile_position=(0, 64))
                return ins
            P.op("pe", fn, reads=[bpt, b_const] + pr["reads"], writes=[bank[4]] if on_dve else [bank[4], bank[5]])
            if on_dve:
                if kb == 1:
                    P.op("dve", lambda e: e.tensor_copy(out=acc, in_=pt), reads=[bpt], writes=[b_wt, b_wu])
                else:
                    P.op("dve", lambda e: e.tensor_tensor(out=acc, in0=acc, in1=pt, op=ALU.add),
                         reads=[bpt, b_wt, b_wu], writes=[b_wt, b_wu])
            if kb == 15:
                P.op("dve", lambda e: e.tensor_copy(out=accbf, in_=acc), reads=[b_wt, b_wu], writes=[b_wsq, b_wqn])

                def fd(e):
                    e.matmul(ps[0:64, 5, :], lhsT=cm(M_ONES, 128, 64), rhs=accbf[:, 0, :], start=False, stop=True)
                    return e.matmul(ps[64:128, 5, :], lhsT=cm(M_ONES, 128, 64), rhs=accbf[:, 1, :], start=False, stop=True,
                                    tile_position=(0, 64))
                P.op("pe", fd, reads=[b_wsq, b_wqn, b_const], writes=[bank[5]])
            if NDUMMY and pr.get("warm", True):
                def fw(e, pt=pt):
                    ins = None
                    for _ in range(NDUMMY):
                        ins = e.matmul(ps[:, 7, :], lhsT=cm(M_ONES), rhs=pt[:, 0, :], start=True, stop=True)
                    return ins
                P.op("pe", fw, reads=[bpt, b_const], writes=[bank[7]])
            if kb == 15:
                pr["finish"]()

        n = len(units)
        for u in range(n):
            qk(u)
            if u >= 1:
                ex_pv(u - 1)
        ex_pv(n - 1)

    def finish_plain(out_ap, b_out):
        def f():
            P.op("dve", lambda e: e.reciprocal(out=w_r, in_=ps[:, 5, :]), reads=[bank[5]], writes=[b_wrr])
            P.op("dve", lambda e: e.tensor_tensor(out=out_ap, in0=ps[:, 4, :], in1=w_r, op=ALU.mult),
                 reads=[bank[4], b_wrr], writes=[b_out])
        return f

    P.op("sp", lambda e: [e.dma_start(out=pvec[:], in_=pvec_d), e.dma_start(out=ident[:], in_=ident_d)],
         writes=[b_pv], dma=True, ndma=2)
    P.op("pool", lambda e: e.dma_start(out=cmat[:], in_=cmat_d), writes=[b_const], dma=True)
    P.op("dve", lambda e: e.memset(cst[:, 0:1], EPS), writes=[b_cst])
    for l in range(DEPTH):
        P.op("dve", lambda e, l=l: e.memset(cst[:, 1 + l:2 + l], math.log(1.0 - lambda_init(l))), writes=[b_cst])

    def dump_and_stop(tag):
        if stop_at != tag:
            return
        P.barrier()
        outs = []
        for nm, t_, shp, dt_ in (("dbg_KT", KT, [128, 2, S], BF16), ("dbg_q", qTall, [128, 2, S], BF16), ("dbg_V", Vv, [128, 16, 256], BF16),
                                 ("dbg_y", yT[:], [128, 12, S], BF16), ("dbg_wqn", w_qn, [128, CH], BF16), ("dbg_wt", w_t, [128, CH], F32),
                                 ("dbg_wu", w_u, [128, CH], F32)):
            dd = nc.dram_tensor(nm, shp, dt_, kind="ExternalOutput").ap()
            outs.append((dd, t_))
        b_dbg = P.buf("dbg")
        d1 = P.op("sp", lambda e: [e.dma_start(out=a_, in_=b_) for a_, b_ in outs], reads=[b_dbg], dma=True, ndma=len(outs))
        P.barrier_op("sp", [d1])
        raise _Stop()

    try:
      for seq in range(nseq):
          P.barrier()
          gen_banks[:] = [0, 1, 2, 3, 4, 5, 6, 7]
          for t in range(16):
              sgi = t % 2
              P.op("sp", lambda e, t=t, sgi=sgi: e.dma_start(out=stage[sgi], in_=x_d[seq, t * 128:(t + 1) * 128, :]),
                   writes=[b_stage[sgi]], dma=True)
              for h in range(2):
                  bk = gbank()

                  def fn(e, sgi=sgi, h=h, bk=bk):
                      ins = None
                      for j in range(4):
                          ins = e.transpose(out=ps[:, bk, j * 128:(j + 1) * 128], in_=stage[sgi][:, (4 * h + j) * 128:(4 * h + j + 1) * 128],
                                            identity=ident[:])
                      return ins
                  P.op("pe", fn, reads=[b_stage[sgi], b_pv], writes=[bank[bk]])
                  eng = "act" if h == 0 else "dve"
                  if eng == "act":
                      P.op("act", lambda e, t=t, h=h, bk=bk: e.copy(out=xT[:, 4 * h:4 * h + 4, t * 128:(t + 1) * 128],
                                                                    in_=ps[:, bk, :].rearrange("p (a b) -> p a b", b=128)),
                           reads=[bank[bk]], writes=[b_xT[t // 4]])
                  else:
                      P.op("dve", lambda e, t=t, h=h, bk=bk: e.tensor_copy(out=xT[:, 4 * h:4 * h + 4, t * 128:(t + 1) * 128],
                                                                           in_=ps[:, bk, :].rearrange("p (a b) -> p a b", b=128)),
                           reads=[bank[bk]], writes=[b_xT[t // 4]])
          P.barrier()

          for l in range(depth):
              pv0 = l * NVL
              gen_banks[:] = [6, 7]
              bl = pv0 + 24
              P.op("dve", lambda e, bl=bl: e.tensor_tensor(out=w_t[:, 0:32],
                                                          in0=pvec[:, bl:bl + 32], in1=pvec[:, bl + 32:bl + 64], op=ALU.mult),
                   reads=[b_pv], writes=[b_wt])
              P.op("dve", lambda e, bl=bl: e.tensor_tensor(out=w_t[:, 32:64], in0=pvec[:, bl + 64:bl + 96], in1=pvec[:, bl + 96:bl + 128],
                                                          op=ALU.mult), reads=[b_pv], writes=[b_wt])
              P.op("dve", lambda e: e.reduce_sum(out=lam[:, 1:3], in_=w_t[:, 0:64].rearrange("p (a b) -> p a b", b=32), axis=AX.X),
                   reads=[b_wt], writes=[b_lam])
              P.op("act", lambda e: e.activation(out=lam[:, 3:5], in_=lam[:, 1:3], func=AF.Exp), reads=[b_lam], writes=[b_lam])
              P.op("dve", lambda e: e.tensor_tensor(out=lam[:, 5:6], in0=lam[:, 3:4], in1=lam[:, 4:5], op=ALU.subtract),
                   reads=[b_lam], writes=[b_lam])
              P.op("dve", lambda e: e.memset(lam[:, 0:1], 1.0), reads=[b_lam], writes=[b_lam])
              P.op("dve", lambda e, l=l: e.tensor_scalar(out=lam[64:128, 0:1], in0=lam[64:128, 5:6], scalar1=lambda_init(l), scalar2=None,
                                                         op0=ALU.add), reads=[b_lam], writes=[b_lam])

              for c in range(NCH):
                  norm_stats(c, KT[:, 0:2, :].rearrange("p a (c t) -> p (a c) t", t=CH), b_KT,
                             rstd1[:, c * CH:(c + 1) * CH], b_rstd1[c], w_ln, b_wln)

              def cqT(j, c):
                  return yT[:, 4 + j, c * CH:(c + 1) * CH]

              def ckvT(j, c):
                  return yT[:, j, c * CH:(c + 1) * CH]

              def krT(c):
                  return yT[0:32, 2, c * CH:(c + 1) * CH]

              gen_banks[:] = [0, 1, 2, 3, 4, 6, 7]
              for c in range(NCH):
                  make_hT(c, 0, rstd1[:, c * CH:(c + 1) * CH], b_rstd1[c], l)
                  s1, bs1 = load_w([(0, 8, 0, 512, wv(w_in_d[l, :, C_CQ:C_CQ + 512]))])
                  s2, bs2 = load_w([(0, 8, 0, 160, wv(w_in_d[l, :, C_CQ + 512:C_CQ + 672]))])

                  def wcol(j):
                      if j < 4:
                          return s1, bs1, j * 128
                      return s2, bs2, (j - 4) * 128
                  for (j0, nj, gcol, dst, bdst, ncols) in ((0, 3, 19, cqT, b_y[1][c], 384), (3, 2, 22, ckvT, b_y[0][c], 256)):
                      for jj in range(nj):
                          sl, bsl, co = wcol(j0 + jj)
                          bk = gbank()
                          mm_group(bk, ps[:, bk, :], [(sl[:, k, co:co + 128], hT[:, k, :]) for k in range(8)], reads=[bsl, b_hT])
                          P.op("act", lambda e, bk=bk: e.activation(out=w_sq, in_=ps[:, bk, :], func=AF.Square),
                               reads=[bank[bk]], writes=[b_wsq])
                          P.op("dve", lambda e, bk=bk, jj=jj, gcol=gcol, dst=dst: e.tensor_scalar(
                              out=dst(jj, c), in0=ps[:, bk, :], scalar1=pvec[:, pv0 + gcol + jj:pv0 + gcol + jj + 1], scalar2=None,
                              op0=ALU.mult), reads=[bank[bk], b_pv], writes=[bdst])

                          def fn(e, jj=jj, nj=nj):
                              return e.matmul(ps[:, 5, :], lhsT=cm(M_ONES), rhs=w_sq, start=(jj == 0), stop=(jj == nj - 1))
                          P.op("pe", fn, reads=[b_wsq, b_const], writes=[bank[5]])
                      rstd_from_bank(5, 128, 1.0 / ncols, w_rs, b_wrs, w_ln, b_wln)
                      for jj in range(nj):
                          P.op("dve", lambda e, jj=jj, dst=dst: e.tensor_tensor(out=dst(jj, c), in0=dst(jj, c), in1=w_rs, op=ALU.mult),
                               reads=[bdst, b_wrs], writes=[bdst])
                  sl, bsl, co = s2, bs2, 128
                  bk = gbank()
                  mm_group(bk, ps[0:32, bk, :], [(sl[:, k, co:co + 32], hT[:, k, :]) for k in range(8)], reads=[bsl, b_hT])
                  P.op("act", lambda e, bk=bk: e.copy(out=w_qn[0:32], in_=ps[0:32, bk, :]), reads=[bank[bk]], writes=[b_wqn])
                  t, bt = load_tab(tabC_d, c, 32, 64)
                  rotary(32, None, cmat[0:32, M_RA * 128:M_RA * 128 + 32], t, bt, krT(c), b_y[0][c])

              P.op("dve", lambda e: e.memset(KT[64:128, :, :], 0.0), writes=b_KT)
              P.op("dve", lambda e: e.memset(qTall[64:128, :, :], 0.0), writes=b_q)
              for g in range(4):
                  sl, bsl = load_w([
                      (0, 3, 0, 192, wv(w_uq_d[l, :, g * 192:(g + 1) * 192])),
                      (3, 2, 0, 192, wv(w_uk_d[l, :, g * 192:(g + 1) * 192])),
                      (5, 2, 0, 64, wv(w_ukv_d[l, :, (2 * g) * 128 + 64:(2 * g) * 128 + 128])),
                      (5, 2, 64, 64, wv(w_ukv_d[l, :, (2 * g + 1) * 128 + 64:(2 * g + 1) * 128 + 128])),
                  ])
                  gen_banks[:] = [0, 1, 2, 3, 4, 5, 6, 7]
                  for c in range(NCH):
                      for hh in range(2):
                          bk = gbank()
                          prs = [(sl[:, 3 + kk, hh * 96:(hh + 1) * 96], ckvT(kk, c)) for kk in range(2)]
                          prs.append((cm(M_SEL, 32, 96), krT(c)))

                          def fn(e, prs=prs, bk=bk):
                              e.matmul(ps[0:96, bk, :], lhsT=prs[0][0], rhs=prs[0][1], start=True, stop=False)
                              e.matmul(ps[0:96, bk, :], lhsT=prs[1][0], rhs=prs[1][1], start=False, stop=False)
                              return e.matmul(ps[0:96, bk, :], lhsT=prs[2][0], rhs=prs[2][1], start=False, stop=True)
                          P.op("pe", fn, reads=[bsl, b_y[0][c], b_const], writes=[bank[bk]])
                          P.op("act", lambda e, bk=bk, hh=hh, c=c: e.copy(out=KT[0:96, hh, c * CH:(c + 1) * CH], in_=ps[0:96, bk, :]),
                               reads=[bank[bk]], writes=[b_KT[c]])
                      bk = gbank()

                      def fnv(e, c=c, bk=bk, sl=sl):
                          ins = None
                          for tt in range(4):
                              for kk in range(2):
                                  ins = e.matmul(ps[:, bk, tt * 128:(tt + 1) * 128],
                                                 lhsT=yT[:, kk, c * CH + tt * 128:c * CH + (tt + 1) * 128],
                                                 rhs=sl[:, 5 + kk, 0:128], start=(kk == 0), stop=(kk == 1))
                          return ins
                      P.op("pe", fnv, reads=[bsl, b_y[0][c]], writes=[bank[bk]])
                      P.op("act", lambda e, bk=bk, c=c: e.copy(out=Vv[:, 4 * c:4 * c + 4, 0:128],
                                                               in_=ps[:, bk, :].rearrange("p (a b) -> p a b", b=128)),
                           reads=[bank[bk]], writes=[b_V[c]])
                  gen_banks[:] = [0, 1, 2, 3, 4, 5, 6, 7]
                  for c in range(NCH):
                      t, bt = load_tab(tabC_d, c, 96, 0)
                      bks = []
                      for hh in range(2):
                          bk = gbank()
                          bks.append(bk)
                          mm_group(bk, ps[0:96, bk, :], [(sl[:, kk, hh * 96:(hh + 1) * 96], cqT(kk, c)) for kk in range(3)],
                                   reads=[bsl, b_y[1][c]])
                      for hh in range(2):
                          bk = bks[hh]
                          P.op("act", lambda e, bk=bk: e.copy(out=w_qn[0:96], in_=ps[0:96, bk, :]), reads=[bank[bk]], writes=[b_wqn])
                          rotary(96, None, cmat[64:96, M_RC * 128:M_RC * 128 + 96], t, bt, qTall[0:96, hh, c * CH:(c + 1) * CH], b_q[c],
                                 krows=(64, 96))
                  gen_banks[:] = [6, 7]
                  pairs = []
                  for c in range(NCH):
                      pairs.append(dict(
                          q=(qTall[:, 0, c * CH:(c + 1) * CH], qTall[:, 1, c * CH:(c + 1) * CH]),
                          k=lambda m, kb: KT[:, m, kb * 128:(kb + 1) * 128],
                          v=lambda m, kb: Vv[:, kb, m * 64:(m + 1) * 64],
                          tp=None, reads=[b_q[c]] + b_KT + b_V,
                          finish=finish_plain(yT[:, 8 + g, c * CH:(c + 1) * CH], b_y[2][c])))
                  attention(pairs, 96.0 ** -0.5)
                  dump_and_stop("C0")

              for gb_ in range(2):
                  sl, bsl = load_w([(0, 8, 0, 256, wv(w_in_d[l, :, C_BQ + 256 * gb_:C_BQ + 256 * gb_ + 256])),
                                    (0, 8, 256, 256, wv(w_in_d[l, :, C_BK + 256 * gb_:C_BK + 256 * gb_ + 256]))])
                  sv, bsv = load_w([(0, 8, 0, 256, wv(w_in_d[l, :, C_BV + 256 * gb_:C_BV + 256 * gb_ + 256]))])
                  gen_banks[:] = [0, 1, 2, 3, 4, 5, 6, 7]
                  for c in range(NCH):
                      make_hT(c, 0, rstd1[:, c * CH:(c + 1) * CH], b_rstd1[c], l)
                      t, bt = load_tab(tabB_d, c)
                      tl = []
                      for (co, dst, bdst) in ((0, qTall, b_q[c]), (256, KT, b_KT[c])):
                          for i in range(2):
                              bk = gbank()
                              mm_group(bk, ps[:, bk, :], [(sl[:, k, co + i * 128:co + (i + 1) * 128], hT[:, k, :]) for k in range(8)],
                                       reads=[bsl, b_hT])
                              tl.append((bk, dst, bdst, i))
                      for (bk, dst, bdst, i) in tl:
                          P.op("act", lambda e, bk=bk: e.copy(out=w_qn, in_=ps[:, bk, :]), reads=[bank[bk]], writes=[b_wqn])
                          rotary(128, None, cm(M_RB), t, bt, dst[:, i, c * CH:(c + 1) * CH], bdst)
                      for tt in range(4):
                          bk = gbank()
                          mm_group(bk, ps[:, bk, 0:256], [(hT[:, k, tt * 128:(tt + 1) * 128], sv[:, k, 0:256]) for k in range(8)],
                                   reads=[bsv, b_hT])
                          P.op("act", lambda e, bk=bk, tt=tt, c=c: e.copy(out=Vv[:, 4 * c + tt, 0:256], in_=ps[:, bk, 0:256]),
                               reads=[bank[bk]], writes=[b_V[c]])
                  gen_banks[:] = [6, 7]
                  pairs = []
                  for c in range(NCH):
                      for hh in range(4):
                          i = hh // 2
                          base = 64 * (hh % 2)

                          def fin(hh=hh, i=i, c=c, gb_=gb_):
                              P.op("dve", lambda e: e.reciprocal(out=w_r, in_=ps[:, 5, :]), reads=[bank[5]], writes=[b_wrr])
                              P.op("dve", lambda e: e.scalar_tensor_tensor(out=w_yn, in0=ps[:, 4, :], scalar=lam[:, 0:1], in1=w_r,
                                                                           op0=ALU.mult, op1=ALU.mult),
                                   reads=[bank[4], b_wrr, b_lam], writes=[b_wyn])
                              half = hh % 2

                              def fz(e, half=half):
                                  if half == 0:
                                      return e.matmul(ps[0:64, 6, :], lhsT=cm(M_COMB, 128, 64), rhs=w_yn, start=True, stop=True)
                                  return e.matmul(ps[64:128, 6, :], lhsT=cm(M_COMB, 128, 64), rhs=w_yn, start=True, stop=True,
                                                  tile_position=(0, 64))
                              P.op("pe", fz, reads=[b_wyn, b_const], writes=[bank[6]])
                              if half == 1:
                                  P.op("act", lambda e: e.activation(out=w_sq, in_=ps[:, 6, :], func=AF.Square),
                                       reads=[bank[6]], writes=[b_wsq])
                                  mm_group(7, ps[:, 7, :], [(cm(M_BLK64), w_sq)], reads=[b_wsq, b_const])
                                  rstd_from_bank(7, 128, 1.0 / 64, w_rs, b_wrs, w_ln, b_wln, bias_col=cst[:, 1 + l:2 + l])
                                  P.op("dve", lambda e: e.scalar_tensor_tensor(
                                      out=yT[:, 4 + 2 * gb_ + i, c * CH:(c + 1) * CH], in0=ps[:, 6, :],
                                      scalar=pvec[:, pv0 + 18:pv0 + 19], in1=w_rs, op0=ALU.mult, op1=ALU.mult),
                                      reads=[bank[6], b_wrs, b_pv], writes=[b_y[1][c]])
                          pairs.append(dict(
                              q=(qTall[base:base + 32, i, c * CH:(c + 1) * CH], qTall[base + 32:base + 64, i, c * CH:(c + 1) * CH]),
                              k=lambda m, kb, i=i, base=base: KT[base + 32 * m:base + 32 * m + 32, i, kb * 128:(kb + 1) * 128],
                              v=lambda m, kb, hh=hh: Vv[:, kb, hh * 64:(hh + 1) * 64],
                              tp=((base, 0), (base + 32, 0)), reads=[b_q[c]] + b_KT + b_V, finish=fin))
                  attention(pairs, 32.0 ** -0.5)

              for ga in range(2):
                  sl, bsl = load_w([(0, 8, 0, 256, wv(w_in_d[l, :, C_AQ + 256 * ga:C_AQ + 256 * ga + 256])),
                                    (0, 8, 256, 128, wv(w_kdup_d[l, :, 128 * ga:128 * ga + 128])),
                                    (0, 8, 384, 64, wv(w_in_d[l, :, C_AV + 64 * ga:C_AV + 64 * ga + 64]))])
                  gen_banks[:] = [0, 1, 2, 3, 4, 5, 6, 7]
                  for c in range(NCH):
                      make_hT(c, 0, rstd1[:, c * CH:(c + 1) * CH], b_rstd1[c], l)
                      t, bt = load_tab(tabA_d, c)
                      tl = []
                      for (co, gcol, dst, bdst) in ((0, 16, qTall[:, 0, c * CH:(c + 1) * CH], b_q[c]),
                                                   (128, 16, qTall[:, 1, c * CH:(c + 1) * CH], b_q[c]),
                                                   (256, 17, KT[:, 0, c * CH:(c + 1) * CH], b_KT[c])):
                          bk = gbank()
                          mm_group(bk, ps[:, bk, :], [(sl[:, k, co:co + 128], hT[:, k, :]) for k in range(8)], reads=[bsl, b_hT])
                          tl.append((bk, gcol, dst, bdst))
                      for (bk, gcol, dst, bdst) in tl:
                          P.op("act", lambda e, bk=bk: e.activation(out=w_sq, in_=ps[:, bk, :], func=AF.Square),
                               reads=[bank[bk]], writes=[b_wsq])
                          bk2 = gbank()
                          mm_group(bk2, ps[:, bk2, :], [(cm(M_BLK64), w_sq)], reads=[b_wsq, b_const])
                          rstd_from_bank(bk2, 128, 1.0 / 64, w_rs, b_wrs, w_ln, b_wln)
                          P.op("dve", lambda e, bk=bk, gcol=gcol: e.scalar_tensor_tensor(
                              out=w_qn, in0=ps[:, bk, :], scalar=pvec[:, pv0 + gcol:pv0 + gcol + 1], in1=w_rs,
                              op0=ALU.mult, op1=ALU.mult), reads=[bank[bk], b_wrs, b_pv], writes=[b_wqn])
                          rotary(128, None, cm(M_RA), t, bt, dst, bdst)
                      bk = gbank()

                      def fnv(e, bk=bk, sl=sl):
                          ins = None
                          for tt in range(4):
                              for k in range(8):
                                  ins = e.matmul(ps[:, bk, tt * 64:(tt + 1) * 64], lhsT=hT[:, k, tt * 128:(tt + 1) * 128],
                                                 rhs=sl[:, k, 384:448], start=(k == 0), stop=(k == 7))
                          return ins
                      P.op("pe", fnv, reads=[bsl, b_hT], writes=[bank[bk]])
                      P.op("act", lambda e, bk=bk, c=c: e.copy(out=Vv[:, 4 * c:4 * c + 4, 0:64],
                                                               in_=ps[:, bk, 0:256].rearrange("p (a b) -> p a b", b=64)),
                           reads=[bank[bk]], writes=[b_V[c]])
                  gen_banks[:] = [6, 7]
                  pairs = []
                  for c in range(NCH):
                      for i in range(2):
                          pairs.append(dict(
                              q=(qTall[0:64, i, c * CH:(c + 1) * CH], qTall[64:128, i, c * CH:(c + 1) * CH]),
                              k=lambda m, kb: KT[64 * m:64 * m + 64, 0, kb * 128:(kb + 1) * 128],
                              v=lambda m, kb: Vv[:, kb, 0:64],
                              tp=((0, 0), (64, 0)), reads=[b_q[c]] + b_KT + b_V,
                              finish=finish_plain(yT[:, 2 * ga + i, c * CH:(c + 1) * CH], b_y[0][c])))
                  attention(pairs, 64.0 ** -0.5)

              P.barrier()
              if debug and seq == 0 and l == 0:
                  dbg_y = nc.dram_tensor("dbg_y", [128, 12, S], BF16, kind="ExternalOutput").ap()
                  dbg_r = nc.dram_tensor("dbg_r", [128, S], F32, kind="ExternalOutput").ap()
                  dbg_x = nc.dram_tensor("dbg_x", [128, 8, S], F32, kind="ExternalOutput").ap()
                  dbg_l = nc.dram_tensor("dbg_l", [128, 8], F32, kind="ExternalOutput").ap()
                  b_dbg = P.buf("dbg")
                  d1 = P.op("sp", lambda e: [e.dma_start(out=dbg_y, in_=yT[:]), e.dma_start(out=dbg_r, in_=rstd1[:]),
                                             e.dma_start(out=dbg_x, in_=xT[:]), e.dma_start(out=dbg_l, in_=lam[:])],
                            reads=[b_dbg], dma=True, ndma=4)
                  P.barrier_op("sp", [d1])
                  P.barrier()
              gen_banks[:] = [4, 5, 6, 7]
              for c in range(NCH):
                  cs = slice(c * CH, (c + 1) * CH)
                  make_hT(c, 0, rstd1[:, cs], b_rstd1[c], l)
                  for ch in range(2):
                      for n in range(3):
                          sg, bsg = load_w([(0, 8, 0, 512, wv(w_in_d[l, :, C_G + n * D + ch * 512:C_G + n * D + ch * 512 + 512]))], key=(l, ch * 6 + n * 2))
                          sb_, bsb = load_w([(0, 4, 0, 512, wv(w_br_d[l, n, :, ch * 512:(ch + 1) * 512]))], key=(l, ch * 6 + n * 2 + 1))
                          for cg in range(4):
                              bg = gbank()
                              mm_group(bg, ps[:, bg, :], [(sg[:, k, cg * 128:(cg + 1) * 128], hT[:, k, :]) for k in range(8)],
                                       reads=[bsg, b_hT])
                              bz = gbank()
                              mm_group(bz, ps[:, bz, :], [(sb_[:, kk, cg * 128:(cg + 1) * 128], yT[:, 4 * n + kk, cs]) for kk in range(4)],
                                       reads=[bsb, b_y[n][c]])
                              gi = (n * 4 + cg) % 2
                              P.op("act", lambda e, bg=bg, gi=gi: e.activation(out=Gs[gi], in_=ps[:, bg, :], func=AF.Sigmoid),
                                   reads=[bank[bg]], writes=[b_Gs[gi]])
                              if n == 0:
                                  P.op("dve", lambda e, bz=bz, gi=gi, cg=cg: e.tensor_tensor(out=macc[:, cg, :], in0=ps[:, bz, :], in1=Gs[gi],
                                                                                           op=ALU.mult),
                                       reads=[bank[bz], b_Gs[gi]], writes=[b_hid[0]])
                              else:
                                  P.op("dve", lambda e, bz=bz, gi=gi: e.tensor_tensor(out=tmpm, in0=ps[:, bz, :], in1=Gs[gi], op=ALU.mult),
                                       reads=[bank[bz], b_Gs[gi]], writes=[b_tmpm])
                                  if n == 1:
                                      P.op(EW, lambda e, cg=cg: e.tensor_tensor(out=macc[:, cg, :], in0=macc[:, cg, :], in1=tmpm, op=ALU.add),
                                           reads=[b_hid[0], b_tmpm], writes=[b_hid[0]])
                                  else:
                                      P.op(EW, lambda e, cg=cg, ch=ch: e.tensor_tensor(out=mT[:, ch * 4 + cg, :], in0=macc[:, cg, :], in1=tmpm,
                                                                                         op=ALU.add),
                                           reads=[b_hid[0], b_tmpm], writes=[b_mT])
                  for ch in range(2):
                      so, bso = load_w([(0, 8, 0, 512, wv(w_out_d[l, :, ch * 512:(ch + 1) * 512]))], key=(l, 12 + ch))
                      for cg in range(4):
                          bk = gbank()
                          mm_group(bk, ps[:, bk, :], [(so[:, k, cg * 128:(cg + 1) * 128], mT[:, k, :]) for k in range(8)], reads=[bso, b_mT])
                          P.op("dve", lambda e, bk=bk, ch=ch, cg=cg: e.tensor_tensor(out=xT[:, ch * 4 + cg, cs], in0=ps[:, bk, :],
                                                                                     in1=xT[:, ch * 4 + cg, cs], op=ALU.add),
                               reads=[bank[bk], b_xT[c]], writes=[b_xT[c]])
                  norm_stats(c, sq8, [b_hid[1]], w_rs2, b_wrs2, w_ln2, b_wln2)
                  make_hT(c, 8, w_rs2, b_wrs2, l)
                  for b in range(8):
                      s1, bs1 = load_w([(0, 8, 0, 512, wv(w_ff1_d[l, :, b * 512:(b + 1) * 512]))], key=(l, 14 + b))
                      for cg in range(4):
                          j = b * 4 + cg
                          bk = gbank()
                          mm_group(bk, ps[:, bk, :], [(s1[:, k, cg * 128:(cg + 1) * 128], hT[:, k, :]) for k in range(8)], reads=[bs1, b_hT])
                          ri = j % 2
                          P.op("act", lambda e, bk=bk, ri=ri: e.activation(out=rl[ri], in_=ps[:, bk, :], func=AF.Relu),
                               reads=[bank[bk]], writes=[b_rl[ri]])
                          P.op(EW2, lambda e, j=j, ri=ri: e.tensor_tensor(out=hid[:, j, :], in0=rl[ri], in1=rl[ri], op=ALU.mult),
                               reads=[b_rl[ri]], writes=[b_hid[j // 8]])
                  for ch in range(2):
                      for kb in range(4):
                          s2, bs2 = load_w([(0, 8, 0, 512, wv(w_ff2_d[l, kb * D:(kb + 1) * D, ch * 512:(ch + 1) * 512]))], key=(l, 22 + ch * 4 + kb))
                          for cg in range(4):
                              def fn(e, s2=s2, kb=kb, cg=cg):
                                  ins = None
                                  for k in range(8):
                                      ins = e.matmul(ps[:, cg, :], lhsT=s2[:, k, cg * 128:(cg + 1) * 128], rhs=hid[:, kb * 8 + k, :],
                                                     start=(kb == 0 and k == 0), stop=(kb == 3 and k == 7))
                                  return ins
                              P.op("pe", fn, reads=[bs2, b_hid[kb]], writes=[bank[cg]])
                      for cg in range(4):
                          P.op("dve", lambda e, ch=ch, cg=cg: e.tensor_tensor(out=xT[:, ch * 4 + cg, cs], in0=ps[:, cg, :],
                                                                              in1=xT[:, ch * 4 + cg, cs], op=ALU.add),
                               reads=[bank[cg], b_xT[c]], writes=[b_xT[c]])
              P.barrier()

          gen_banks[:] = [0, 1, 2, 3, 4, 5, 6, 7]
          fg0 = DEPTH * NVL
          last_store = []
          for c in range(NCH):
              norm_stats(c, sqf, [b_sqf], w_rs3, b_wrs3, w_ln3, b_wln3)
              for k in range(8):
                  P.op("dve", lambda e, k=k, c=c: e.scalar_tensor_tensor(out=ofin[:, k, :], in0=xT[:, k, c * CH:(c + 1) * CH],
                                                                         scalar=pvec[:, fg0 + k:fg0 + k + 1], in1=w_rs3,
                                                                         op0=ALU.mult, op1=ALU.mult),
                       reads=[b_xT[c], b_wrs3, b_pv], writes=[b_ofin])
              for tt in range(4):
                  sgi = tt % 2
                  for h in range(2):
                      bk = gbank()

                      def fn(e, tt=tt, h=h, bk=bk):
                          ins = None
                          for j in range(4):
                              ins = e.transpose(out=ps[:, bk, j * 128:(j + 1) * 128], in_=ofin[:, 4 * h + j, tt * 128:(tt + 1) * 128],
                                                identity=ident[:])
                          return ins
                      P.op("pe", fn, reads=[b_ofin, b_pv], writes=[bank[bk]])
                      if h == 0:
                          P.op("act", lambda e, sgi=sgi, bk=bk: e.copy(out=stage[sgi][:, 0:512], in_=ps[:, bk, :]),
                               reads=[bank[bk]], writes=[b_stage[sgi]])
                      else:
                          P.op("dve", lambda e, sgi=sgi, bk=bk: e.tensor_copy(out=stage[sgi][:, 512:1024], in_=ps[:, bk, :]),
                               reads=[bank[bk]], writes=[b_stage[sgi]])
                  t0 = c * CH + tt * 128
                  o = P.op("sp", lambda e, sgi=sgi, t0=t0: e.dma_start(out=out_d[seq, t0:t0 + 128, :], in_=stage[sgi]),
                           reads=[b_stage[sgi]], dma=True)
                  last_store.append(o)
          P.barrier_op("sp", last_store)


    except _Stop:
        pass
    P.emit()
    return nc, P


def _angles(pos, dim, theta):
    inv = (np.float32(theta) ** (-np.arange(0, dim, 2, dtype=np.float32) / np.float32(dim))).astype(np.float32)
    return (pos.astype(np.float32)[:, None] * inv[None, :]).astype(np.float32)


def _tables():
    t = np.arange(S)
    row_idx = (t // 64)
    col_idx = (t % 64)
    row_ang = _angles(row_idx, 32, 10000.0)
    col_ang = _angles(col_idx, 32, 10000.0)
    b_ang = _angles(t, 8, 500000.0)
    c_ang = _angles(t, 32, 10000.0)
    tabA = np.zeros((128, 2, S), np.float32)
    tabB = np.zeros((128, 2, S), np.float32)
    tabC = np.zeros((128, 2, S), np.float32)
    tabB[:, 0, :] = 1.0
    tabC[:, 0, :] = 1.0
    for p in range(128):
        d = p % 64
        ang = row_ang[:, d % 16] if d < 32 else col_ang[:, (d - 32) % 16]
        tabA[p, 0] = np.cos(ang)
        tabA[p, 1] = np.sin(ang)
        d = p % 32
        if d < 8:
            ang = b_ang[:, d % 4]
            tabB[p, 0] = np.cos(ang)
            tabB[p, 1] = np.sin(ang)
        if 64 <= p < 96:
            ang = c_ang[:, (p - 64) % 16]
            tabC[p, 0] = np.cos(ang)
            tabC[p, 1] = np.sin(ang)
    return tabA, tabB, tabC


def _cmat():
    m = np.zeros((7, 128, 128), np.float32)
    m[M_ONES] = 1.0
    m[M_BLK64, 0:64, 0:64] = 1.0
    m[M_BLK64, 64:128, 64:128] = 1.0
    for blk in range(4):
        o = blk * 32
        for i in range(16):
            m[M_RA, o + i + 16, o + i] = -1.0
            m[M_RA, o + i, o + i + 16] = 1.0
        for i in range(4):
            m[M_RB, o + i + 4, o + i] = -1.0
            m[M_RB, o + i, o + i + 4] = 1.0
    for i in range(16):
        m[M_RC, 64 + i + 16, 64 + i] = -1.0
        m[M_RC, 64 + i, 64 + i + 16] = 1.0
    for i in range(32):
        m[M_SEL, i, 64 + i] = 1.0
    for i in range(64):
        m[M_COMB, i, i] = 1.0
        m[M_COMB, 64 + i, i] = -1.0
    return np.ascontiguousarray(m.transpose(1, 0, 2).reshape(128, 7 * 128))


def _pvec(ln1_g, ln2_g, a_q_norm, a_k_norm, b_lambda, b_subln, c_q_norm, c_kv_norm, final_g):
    pv = np.zeros((128, DEPTH * NVL + 8), np.float32)
    idx = np.arange(128)
    for l in range(DEPTH):
        o = l * NVL
        pv[:, o:o + 8] = ln1_g[l].reshape(8, 128).T
        pv[:, o + 8:o + 16] = ln2_g[l].reshape(8, 128).T
        pv[:, o + 16] = a_q_norm[l][idx % 64]
        pv[:, o + 17] = a_k_norm[l][idx % 64]
        pv[:, o + 18] = b_subln[l][idx % 64]
        pv[:, o + 19:o + 22] = c_q_norm[l].reshape(3, 128).T
        pv[:, o + 22:o + 24] = c_kv_norm[l].reshape(2, 128).T
        pv[:, o + 24:o + 152] = b_lambda[l].reshape(1, 128)
    pv[:, DEPTH * NVL:] = final_g.reshape(8, 128).T
    return pv


_CACHE = {}


def kernel(x, ln1_g, w_in, a_q_norm, a_k_norm, b_lambda, b_subln, c_q_norm, c_kv_norm,
           c_w_uq, c_w_ukv, w_branch, w_out, ln2_g, w_ff1, w_ff2, final_g):
    f = lambda a: np.ascontiguousarray(np.asarray(a, dtype=np.float32))
    x = f(x)
    w_in = f(w_in)
    c_w_ukv = f(c_w_ukv)
    n_cores = 8
    w_kdup = np.empty((DEPTH, D, 256), np.float32)
    for j in range(2):
        wk = w_in[:, :, C_AK + 64 * j:C_AK + 64 * j + 64]
        w_kdup[:, :, 128 * j:128 * j + 64] = wk
        w_kdup[:, :, 128 * j + 64:128 * j + 128] = wk
    w_uk96 = np.zeros((DEPTH, 256, 8, 96), np.float32)
    w_uk96[:, :, :, 0:64] = c_w_ukv.reshape(DEPTH, 256, 8, 128)[:, :, :, 0:64]
    w_uk96 = w_uk96.reshape(DEPTH, 256, 768)
    tabA, tabB, tabC = _tables()
    shared = {
        "w_in": w_in, "w_kdup": w_kdup, "c_w_uq": f(c_w_uq), "w_uk96": w_uk96, "c_w_ukv": c_w_ukv,
        "w_branch": f(w_branch), "w_out": f(w_out), "w_ff1": f(w_ff1), "w_ff2": f(w_ff2),
        "pvec": _pvec(f(ln1_g), f(ln2_g), f(a_q_norm), f(a_k_norm), f(b_lambda), f(b_subln), f(c_q_norm), f(c_kv_norm), f(final_g)),
        "cmat": _cmat(), "ident": np.eye(128, dtype=np.float32), "tabA": tabA, "tabB": tabB, "tabC": tabC,
    }
    if "nc" not in _CACHE:
        _CACHE["nc"] = build_program()[0]
    nc = _CACHE["nc"]
    in_maps = []
    for c in range(n_cores):
        m = dict(shared)
        m["x"] = np.ascontiguousarray(x[NSEQ * c:NSEQ * (c + 1)])
        in_maps.append(m)
    res = run_bass_kernel_spmd(nc, in_maps, core_ids=list(range(n_cores)))
    out = np.concatenate([np.asarray(r["out"]) for r in res.results], axis=0)
    return out.astype(np.float32)
```

```python
import contextlib
import math
import numpy as np
import concourse.bass as bass
import concourse.mybir as mybir
from concourse.bass_utils import run_bass_kernel_spmd

F32 = mybir.dt.float32
BF16 = mybir.dt.bfloat16
AF = mybir.ActivationFunctionType
ALU = mybir.AluOpType
AX = mybir.AxisListType

ENGS = ("pe", "act", "dve", "pool", "sp")
EW = "dve"
EW2 = "dve"
NDUMMY = 0


class Buf:
    __slots__ = ("name", "writers", "readers", "dsem", "dcnt", "psum")

    def __init__(self, name, psum=False):
        self.name = name
        self.psum = psum
        self.writers = {}
        self.readers = {}
        self.dsem = None
        self.dcnt = 0


class _Rec:
    def __init__(self):
        self.calls = []

    def __getattr__(self, name):
        def f(*a, **kw):
            self.calls.append((name, a, kw))
            return len(self.calls) - 1
        return f


class _Replay:
    __slots__ = ("calls", "ret")

    def __init__(self, fn):
        r = _Rec()
        self.ret = fn(r)
        self.calls = r.calls

    def __call__(self, eng):
        res = [getattr(eng, n)(*a, **kw) for (n, a, kw) in self.calls]
        if isinstance(self.ret, (list, tuple)):
            return [res[i] for i in self.ret]
        return res[self.ret]


class Op:
    __slots__ = ("eng", "fn", "deps", "marked", "done", "is_dma")

    def __init__(self, eng, fn):
        self.eng = eng
        self.fn = fn
        self.deps = []
        self.marked = False
        self.done = None
        self.is_dma = False


class Prog:
    def __init__(self, nc):
        self.nc = nc
        self.ops = {e: [] for e in ENGS}
        self.ndsem = 0
        self.stack = contextlib.ExitStack()
        self.nbuf = 0

    def sb(self, name, shape, dtype):
        return self.stack.enter_context(self.nc.sbuf_tensor(name, list(shape), dtype))

    def ps(self, name, shape, dtype):
        return self.stack.enter_context(self.nc.psum_tensor(name, list(shape), dtype))

    def buf(self, name=None, psum=False):
        self.nbuf += 1
        return Buf(name or ("b%d" % self.nbuf), psum)

    def _dep(self, op, prod):
        if prod is op:
            return
        if prod.eng == op.eng and not prod.is_dma:
            return
        op.deps.append(prod)
        if not prod.is_dma:
            prod.marked = True

    def op(self, eng, fn, reads=(), writes=(), dma=False, ndma=1):
        o = Op(eng, _Replay(fn))
        o.is_dma = dma
        for b in reads:
            for w in b.writers.values():
                if w.eng == eng and not w.is_dma:
                    if eng != "pe" and not dma:
                        o.deps.append(w)
                        w.marked = True
                else:
                    self._dep(o, w)
            if b.psum:
                for r in b.readers.values():
                    self._dep(o, r)
        for b in writes:
            for r in b.readers.values():
                self._dep(o, r)
            for w in b.writers.values():
                self._dep(o, w)
        for b in reads:
            b.readers[("dma%d" % id(o)) if dma else eng] = o
        for b in writes:
            b.readers.clear()
            b.writers.clear()
            b.writers[("dma" if dma else eng)] = o
        if dma:
            tgt = writes[0] if writes else reads[0]
            if tgt.dsem is None:
                tgt.dsem = "d%d" % self.ndsem
                self.ndsem += 1
            tgt.dcnt += 16 * ndma
            o.done = (tgt.dsem, tgt.dcnt)
        self.ops[eng].append(o)
        return o

    def barrier_op(self, eng, deps):
        o = Op(eng, None)
        for d in deps:
            if d is None:
                continue
            o.deps.append(d)
            if not d.is_dma:
                d.marked = True
        self.ops[eng].append(o)
        return o

    def barrier(self, engs=ENGS):
        engs = [e for e in engs if e in self.ops]
        last = {}
        for e in engs:
            lst = [o for o in self.ops[e] if o.fn is not None]
            last[e] = lst[-1] if lst else None
        for e in engs:
            self.barrier_op(e, [last[f] for f in engs if f != e])

    def emit(self):
        nc = self.nc
        st = self.stack
        sems = {}
        for e in ENGS:
            sems[e] = st.enter_context(nc.semaphore("s_" + e))
        for i in range(self.ndsem):
            sems["d%d" % i] = st.enter_context(nc.semaphore("sd%d" % i))
        for e in ENGS:
            c = 0
            for o in self.ops[e]:
                if o.is_dma:
                    continue
                if o.marked and o.fn is not None:
                    c += 1
                    o.done = (e, c)
                elif o.fn is None:
                    o.done = (e, c)
        self.stats = {e: len(self.ops[e]) for e in ENGS}

        def run(e, eng):
            waited = {}
            for o in self.ops[e]:
                need = {}
                for d in o.deps:
                    k, v = d.done
                    if v > need.get(k, 0):
                        need[k] = v
                for k, v in need.items():
                    if v > waited.get(k, 0):
                        eng.wait_ge(sems[k], v)
                        waited[k] = v
                if o.fn is None:
                    continue
                ins = o.fn(eng)
                if o.is_dma:
                    for i_ in (ins if isinstance(ins, (list, tuple)) else [ins]):
                        i_.then_inc(sems[o.done[0]], 16)
                elif o.marked:
                    ins.then_inc(sems[e], 1)

        with nc.Block() as block:
            @block.tensor
            def _(t):
                run("pe", t)

            @block.scalar
            def _(s):
                run("act", s)

            @block.vector
            def _(v):
                run("dve", v)

            @block.gpsimd
            def _(g):
                run("pool", g)

            @block.sync
            def _(s):
                run("sp", s)
        st.close()


D = 1024
S = 2048
DEPTH = 4
NSEQ = 2
CH = 512
NCH = S // CH
IN_COLS = 6048
EPS = 1e-6
C_AQ, C_AK, C_AV = 0, 512, 640
C_BQ, C_BK, C_BV = 768, 1280, 1792
C_CQ, C_CKV, C_KR = 2304, 2688, 2944
C_G = 2976
NVL = 152
M_ONES, M_BLK64, M_RA, M_RB, M_RC, M_SEL, M_COMB = range(7)


def lambda_init(l):
    return 0.8 - 0.6 * math.exp(-0.3 * l)


class _Stop(Exception):
    pass


def build_program(depth=DEPTH, nseq=NSEQ, wring=3, debug=False, stop_at=None):
    nc = bass.Bass("TRN2", target_bir_lowering=False)
    P = Prog(nc)

    def din(name, shape):
        return nc.dram_tensor(name, list(shape), F32, kind="ExternalInput").ap()

    x_d = din("x", [nseq, S, D])
    w_in_d = din("w_in", [DEPTH, D, IN_COLS])
    w_kdup_d = din("w_kdup", [DEPTH, D, 256])
    w_uq_d = din("c_w_uq", [DEPTH, 384, 768])
    w_uk_d = din("w_uk96", [DEPTH, 256, 768])
    w_ukv_d = din("c_w_ukv", [DEPTH, 256, 1024])
    w_br_d = din("w_branch", [DEPTH, 3, 512, D])
    w_out_d = din("w_out", [DEPTH, D, D])
    w_ff1_d = din("w_ff1", [DEPTH, D, 4 * D])
    w_ff2_d = din("w_ff2", [DEPTH, 4 * D, D])
    pvec_d = din("pvec", [128, DEPTH * NVL + 8])
    cmat_d = din("cmat", [128, 7 * 128])
    ident_d = din("ident", [128, 128])
    tabA_d = din("tabA", [128, 2, S])
    tabB_d = din("tabB", [128, 2, S])
    tabC_d = din("tabC", [128, 2, S])
    out_d = nc.dram_tensor("out", [nseq, S, D], F32, kind="ExternalOutput").ap()

    xT = P.sb("xT", [128, 8, S], F32)
    yT = P.sb("yT", [128, 12, S], BF16)
    pvec = P.sb("pvec_sb", [128, DEPTH * NVL + 8], F32)
    cmat = P.sb("cmat_sb", [128, 7 * 128], BF16)
    ident = P.sb("ident_sb", [128, 128], F32)
    cst = P.sb("cst", [128, 8], F32)
    lam = P.sb("lam", [128, 8], F32)
    rstd1 = P.sb("rstd1", [128, S], F32)
    hT = P.sb("hT", [128, 8, CH], BF16)
    wr = [P.sb("wr%d" % i, [128, 8, 512], BF16) for i in range(wring)]
    SCR = 52224
    scr = P.sb("scr", [128, SCR // 2], BF16)
    ps = P.ps("ps", [128, 8, 512], F32)

    off = [0]

    def carve(nbytes):
        o = off[0]
        off[0] += nbytes
        assert off[0] <= SCR, (off[0], SCR)
        return o

    def v_bf(o, shape):
        n = int(np.prod(shape))
        a = scr[:, o // 2:o // 2 + n]
        if len(shape) == 2:
            return a.rearrange("p (a b) -> p a b", b=shape[1])
        return a

    def v_f32(o, shape):
        n = int(np.prod(shape))
        a = scr[:, o // 2:o // 2 + 2 * n].bitcast(F32)
        if len(shape) == 2:
            return a.rearrange("p (a b) -> p a b", b=shape[1])
        return a

    off[0] = 0
    KT = v_bf(carve(8192), [2, S])
    Vv = v_bf(carve(8192), [16, 256])
    qTall = v_bf(carve(8192), [2, S])
    PT = [v_bf(carve(2048), [2, CH]) for _ in range(3)]
    tab = [v_f32(carve(4096), [2, CH]) for _ in range(2)]
    w_ln = v_f32(carve(2048), [CH])
    w_rs = v_f32(carve(2048), [CH])
    o_wt = carve(2048)
    w_t = v_f32(o_wt, [CH])
    w_u = v_f32(carve(2048), [CH])
    acc = v_f32(o_wt, [2, CH])
    w_r = v_f32(carve(2048), [CH])
    o_wsq = carve(1024)
    w_sq = v_bf(o_wsq, [CH])
    w_qn = v_bf(carve(1024), [CH])
    accbf = v_bf(o_wsq, [2, CH])
    w_yn = v_bf(carve(1024), [CH])
    att_end = off[0]
    off[0] = 0
    mT = v_bf(carve(8192), [8, CH])
    hid = v_bf(carve(32768), [32, CH])
    Gs = [v_f32(carve(2048), [CH]) for _ in range(2)]
    tmpm = v_f32(carve(2048), [CH])
    rl = Gs
    w_ln2 = v_f32(carve(2048), [CH])
    w_rs2 = v_f32(carve(2048), [CH])
    loc_end = off[0]
    macc = scr[:, (8192 // 2):(8192 // 2) + 4096].bitcast(F32).rearrange("p (a b) -> p a b", b=CH)
    sq8 = scr[:, (16384 // 2):(16384 // 2) + 4096].rearrange("p (a b) -> p a b", b=CH)
    off[0] = 0
    stage = [v_f32(carve(4096), [D]) for _ in range(2)]
    ofin = v_f32(carve(16384), [8, CH])
    sqf = v_bf(carve(8192), [8, CH])
    w_ln3 = v_f32(carve(2048), [CH])
    w_rs3 = v_f32(carve(2048), [CH])

    bank = [P.buf("bank%d" % i, psum=True) for i in range(8)]
    b_xT = [P.buf("xT%d" % c) for c in range(NCH)]
    b_y = [[P.buf("y%d_%d" % (n, c)) for c in range(NCH)] for n in range(3)]
    b_const = P.buf("const")
    b_pv = P.buf("pvec")
    b_cst = P.buf("cst")
    b_lam = P.buf("lam")
    b_rstd1 = [P.buf("rstd1_%d" % c) for c in range(NCH)]
    b_hT = P.buf("hT")
    b_wr = [P.buf("wr%d" % i) for i in range(wring)]
    b_KT = [P.buf("KT%d" % c) for c in range(NCH)]
    b_V = [P.buf("V%d" % c) for c in range(NCH)]
    b_q = [P.buf("q%d" % c) for c in range(NCH)]
    b_PT = [P.buf("PT%d" % i) for i in range(3)]
    b_tab = [P.buf("tab%d" % i) for i in range(2)]
    b_wln, b_wrs, b_wt, b_wu, b_wrr, b_wsq, b_wqn, b_wyn = [P.buf() for _ in range(8)]
    b_mT = P.buf("mT")
    b_hid = [P.buf("hid%d" % i) for i in range(4)]
    b_Gs = [P.buf(), P.buf()]
    b_tmpm = P.buf()
    b_rl = b_Gs
    b_wln2, b_wrs2 = P.buf(), P.buf()
    b_stage = [P.buf(), P.buf()]
    b_ofin, b_sqf, b_wln3, b_wrs3 = P.buf(), P.buf(), P.buf(), P.buf()

    def cm(i, rows=128, cols=128):
        return cmat[0:rows, i * 128:i * 128 + cols]

    gen_banks = [6, 7]
    gb_i = [0]

    def gbank():
        b = gen_banks[gb_i[0] % len(gen_banks)]
        gb_i[0] += 1
        return b

    wr_i = [0]

    wsc = nc.dram_tensor("wsc", [DEPTH, 30, 128, 8, 512], BF16).ap()
    b_wsc = [P.buf("wsc%d" % i) for i in range(DEPTH)]
    cached = set()

    def load_w(parts, key=None):
        s = wr_i[0] % wring
        wr_i[0] += 1
        slot = wr[s]
        if key is not None and key in cached:
            kc = parts[0][1]
            P.op("sp", lambda e: e.dma_start(out=slot[:, 0:kc, :], in_=wsc[key[0], key[1], :, 0:kc, :]),
                 reads=[b_wsc[key[0]]], writes=[b_wr[s]], dma=True)
            return slot, b_wr[s]

        def fn(e, parts=parts, slot=slot):
            res = []
            for p_ in parts:
                k0, kc, c0, ncol, ap = p_[:5]
                dst = slot[:, k0:k0 + kc, c0:c0 + ncol]
                if len(p_) > 5:
                    dst = p_[5](dst)
                res.append(e.dma_start(out=dst, in_=ap))
            return res
        P.op("pool", fn, writes=[b_wr[s]], dma=True, ndma=len(parts))
        if key is not None:
            kc = parts[0][1]
            P.op("sp", lambda e: e.dma_start(out=wsc[key[0], key[1], :, 0:kc, :], in_=slot[:, 0:kc, :]),
                 reads=[b_wr[s]], writes=[b_wsc[key[0]]], dma=True)
            cached.add(key)
        return slot, b_wr[s]

    def wv(ap2d):
        return ap2d.rearrange("(k p) n -> p k n", p=128)

    def mm_group(bk, out_ap, pairs, reads, tp=None):
        def fn(e, pairs=pairs, out_ap=out_ap, tp=tp):
            n = len(pairs)
            ins = None
            for i, (l, r) in enumerate(pairs):
                if tp is None:
                    ins = e.matmul(out_ap, lhsT=l, rhs=r, start=(i == 0), stop=(i == n - 1))
                else:
                    ins = e.matmul(out_ap, lhsT=l, rhs=r, start=(i == 0), stop=(i == n - 1), tile_position=tp)
            return ins
        return P.op("pe", fn, reads=reads, writes=[bank[bk]])

    def rstd_from_bank(bk, np_, scale, out_ap, b_out, ln_ap, b_ln, bias_col=None, extra_reads=()):
        P.op("act", lambda e: e.activation(out=ln_ap[0:np_], in_=ps[0:np_, bk, :], func=AF.Ln, bias=cst[0:np_, 0:1], scale=scale),
             reads=[bank[bk], b_cst], writes=[b_ln])
        if bias_col is None:
            P.op("act", lambda e: e.activation(out=out_ap[0:np_], in_=ln_ap[0:np_], func=AF.Exp, scale=-0.5),
                 reads=[b_ln], writes=[b_out])
        else:
            P.op("act", lambda e: e.activation(out=out_ap[0:np_], in_=ln_ap[0:np_], func=AF.Exp, scale=-0.5, bias=bias_col),
                 reads=[b_ln, b_cst], writes=[b_out])

    tab_i = [0]

    def load_tab(src, c, np_=128, p0=0):
        i = tab_i[0] % 2
        tab_i[0] += 1
        t = tab[i]
        P.op("sp", lambda e: e.dma_start(out=t[0:np_], in_=src[p0:p0 + np_, :, c * CH:(c + 1) * CH]), writes=[b_tab[i]], dma=True)
        return t, b_tab[i]

    def rotary(np_, qn_reads, rmat, t, b_t, out_ap, b_out, krows=None):
        bk = gbank()
        k0, k1 = krows if krows is not None else (0, np_)
        mm_group(bk, ps[0:np_, bk, :], [(rmat, w_qn[k0:k1])], reads=[b_wqn, b_const])
        P.op("dve", lambda e: e.tensor_tensor(out=w_t[0:np_], in0=ps[0:np_, bk, :], in1=t[0:np_, 1, :], op=ALU.mult),
             reads=[bank[bk], b_t], writes=[b_wt])
        P.op(EW2, lambda e: e.tensor_tensor(out=w_u[0:np_], in0=w_qn[0:np_], in1=t[0:np_, 0, :], op=ALU.mult),
             reads=[b_wqn, b_t], writes=[b_wu])
        P.op(EW2, lambda e: e.tensor_tensor(out=out_ap, in0=w_u[0:np_], in1=w_t[0:np_], op=ALU.add),
             reads=[b_wu, b_wt], writes=[b_out])

    def make_hT(c, gcol0, rstd_ap, b_rstd, l):
        for k in range(8):
            P.op("dve", lambda e, k=k: e.scalar_tensor_tensor(out=hT[:, k, :], in0=xT[:, k, c * CH:(c + 1) * CH],
                                                              scalar=pvec[:, l * NVL + gcol0 + k:l * NVL + gcol0 + k + 1],
                                                              in1=rstd_ap, op0=ALU.mult, op1=ALU.mult),
                 reads=[b_xT[c], b_rstd, b_pv], writes=[b_hT])

    def norm_stats(c, sq_ap, b_sq_list, out_ap, b_out, ln_ap, b_ln):
        P.op("act", lambda e: e.activation(out=sq_ap, in_=xT[:, :, c * CH:(c + 1) * CH], func=AF.Square),
             reads=[b_xT[c]], writes=b_sq_list)
        bk = gbank()
        mm_group(bk, ps[:, bk, :], [(cm(M_ONES), sq_ap[:, k, :]) for k in range(8)], reads=b_sq_list + [b_const])
        rstd_from_bank(bk, 128, 1.0 / D, out_ap, b_out, ln_ap, b_ln)

    def attention(pairs, scale):
        units = [(pi, kb) for pi in range(len(pairs)) for kb in range(16)]

        def qk(u):
            pi, kb = units[u]
            pr = pairs[pi]
            s = u % 2

            def fn(e, pr=pr, kb=kb, s=s):
                ins = None
                for m in range(2):
                    kw = {}
                    if pr["tp"] is not None:
                        kw["tile_position"] = pr["tp"][m]
                    ins = e.matmul(ps[:, 2 * s + m, :], lhsT=pr["k"](m, kb), rhs=pr["q"][m], start=True, stop=True, **kw)
                return ins
            P.op("pe", fn, reads=pr["reads"], writes=[bank[2 * s], bank[2 * s + 1]])

        def ex_pv(u, nxt):
            pi, kb = units[u]
            pr = pairs[pi]
            s = u % 2
            pt = PT[u % 3]
            bpt = b_PT[u % 3]
            P.op("act", lambda e: e.activation(out=pt, in_=ps[:, 2 * s:2 * s + 2, :], func=AF.Exp, scale=scale),
                 reads=[bank[2 * s], bank[2 * s + 1]], writes=[bpt])
            if nxt is not None:
                qk(nxt)

            on_dve = (kb % 2 == 1)

            def fn(e, pr=pr, kb=kb, pt=pt, on_dve=on_dve):
                st_, sp_ = (kb == 0), (kb == 15)
                ins = e.matmul(ps[0:64, 4, :], lhsT=pr["v"](0, kb), rhs=pt[:, 0, :], start=st_, stop=sp_)
                ins = e.matmul(ps[64:128, 4, :], lhsT=pr["v"](1, kb), rhs=pt[:, 1, :], start=st_, stop=sp_, tile_position=(0, 64))
                if not on_dve:
                    e.matmul(ps[0:64, 5, :], lhsT=cm(M_ONES, 128, 64), rhs=pt[:, 0, :], start=st_, stop=False)
                    ins = e.matmul(ps[64:128, 5, :], lhsT=cm(M_ONES, 128, 64), rhs=pt[:, 1, :], start=st_, stop=False,
                                   tile_position=(0, 64))
                return ins
            P.op("pe", fn, reads=[bpt, b_const] + pr["reads"], writes=[bank[4]] if on_dve else [bank[4], bank[5]])
            if on_dve:
                if kb == 1:
                    P.op("dve", lambda e: e.tensor_copy(out=acc, in_=pt), reads=[bpt], writes=[b_wt, b_wu])
                else:
                    P.op("dve", lambda e: e.tensor_tensor(out=acc, in0=acc, in1=pt, op=ALU.add),
                         reads=[bpt, b_wt, b_wu], writes=[b_wt, b_wu])
            if kb == 15:
                P.op("dve", lambda e: e.tensor_copy(out=accbf, in_=acc), reads=[b_wt, b_wu], writes=[b_wsq, b_wqn])

                def fd(e):
                    e.matmul(ps[0:64, 5, :], lhsT=cm(M_ONES, 128, 64), rhs=accbf[:, 0, :], start=False, stop=True)
                    return e.matmul(ps[64:128, 5, :], lhsT=cm(M_ONES, 128, 64), rhs=accbf[:, 1, :], start=False, stop=True,
                                    tile_position=(0, 64))
                P.op("pe", fd, reads=[b_wsq, b_wqn, b_const], writes=[bank[5]])
            if NDUMMY and pr.get("warm", True):
                def fw(e, pt=pt):
                    ins = None
                    for _ in range(NDUMMY):
                        ins = e.matmul(ps[:, 7, :], lhsT=cm(M_ONES), rhs=pt[:, 0, :], start=True, stop=True)
                    return ins
                P.op("pe", fw, reads=[bpt, b_const], writes=[bank[7]])
            if kb == 15:
                pr["finish"]()

        n = len(units)
        qk(0)
        if n > 1:
            qk(1)
        for u in range(n):
            ex_pv(u, (u + 2) if u + 2 < n else None)

    def finish_plain(out_ap, b_out):
        def f():
            P.op("dve", lambda e: e.reciprocal(out=w_r, in_=ps[:, 5, :]), reads=[bank[5]], writes=[b_wrr])
            P.op("dve", lambda e: e.tensor_tensor(out=out_ap, in0=ps[:, 4, :], in1=w_r, op=ALU.mult),
                 reads=[bank[4], b_wrr], writes=[b_out])
        return f

    P.op("sp", lambda e: [e.dma_start(out=pvec[:], in_=pvec_d), e.dma_start(out=ident[:], in_=ident_d)],
         writes=[b_pv], dma=True, ndma=2)
    P.op("pool", lambda e: e.dma_start(out=cmat[:], in_=cmat_d), writes=[b_const], dma=True)
    P.op("dve", lambda e: e.memset(cst[:, 0:1], EPS), writes=[b_cst])
    for l in range(DEPTH):
        P.op("dve", lambda e, l=l: e.memset(cst[:, 1 + l:2 + l], math.log(1.0 - lambda_init(l))), writes=[b_cst])

    def dump_and_stop(tag):
        if stop_at != tag:
            return
        P.barrier()
        outs = []
        for nm, t_, shp, dt_ in (("dbg_KT", KT, [128, 2, S], BF16), ("dbg_q", qTall, [128, 2, S], BF16), ("dbg_V", Vv, [128, 16, 256], BF16),
                                 ("dbg_y", yT[:], [128, 12, S], BF16), ("dbg_wqn", w_qn, [128, CH], BF16), ("dbg_wt", w_t, [128, CH], F32),
                                 ("dbg_wu", w_u, [128, CH], F32)):
            dd = nc.dram_tensor(nm, shp, dt_, kind="ExternalOutput").ap()
            outs.append((dd, t_))
        b_dbg = P.buf("dbg")
        d1 = P.op("sp", lambda e: [e.dma_start(out=a_, in_=b_) for a_, b_ in outs], reads=[b_dbg], dma=True, ndma=len(outs))
        P.barrier_op("sp", [d1])
        raise _Stop()

    try:
      for seq in range(nseq):
          P.barrier()
          gen_banks[:] = [0, 1, 2, 3, 4, 5, 6, 7]
          for t in range(16):
              sgi = t % 2
              P.op("sp", lambda e, t=t, sgi=sgi: e.dma_start(out=stage[sgi], in_=x_d[seq, t * 128:(t + 1) * 128, :]),
                   writes=[b_stage[sgi]], dma=True)
              for h in range(2):
                  bk = gbank()

                  def fn(e, sgi=sgi, h=h, bk=bk):
                      ins = None
                      for j in range(4):
                          ins = e.transpose(out=ps[:, bk, j * 128:(j + 1) * 128], in_=stage[sgi][:, (4 * h + j) * 128:(4 * h + j + 1) * 128],
                                            identity=ident[:])
                      return ins
                  P.op("pe", fn, reads=[b_stage[sgi], b_pv], writes=[bank[bk]])
                  eng = "act" if h == 0 else "dve"
                  if eng == "act":
                      P.op("act", lambda e, t=t, h=h, bk=bk: e.copy(out=xT[:, 4 * h:4 * h + 4, t * 128:(t + 1) * 128],
                                                                    in_=ps[:, bk, :].rearrange("p (a b) -> p a b", b=128)),
                           reads=[bank[bk]], writes=[b_xT[t // 4]])
                  else:
                      P.op("dve", lambda e, t=t, h=h, bk=bk: e.tensor_copy(out=xT[:, 4 * h:4 * h + 4, t * 128:(t + 1) * 128],
                                                                           in_=ps[:, bk, :].rearrange("p (a b) -> p a b", b=128)),
                           reads=[bank[bk]], writes=[b_xT[t // 4]])
          P.barrier()

          for l in range(depth):
              pv0 = l * NVL
              gen_banks[:] = [6, 7]
              bl = pv0 + 24
              P.op("dve", lambda e, bl=bl: e.tensor_tensor(out=w_t[:, 0:32],
                                                          in0=pvec[:, bl:bl + 32], in1=pvec[:, bl + 32:bl + 64], op=ALU.mult),
                   reads=[b_pv], writes=[b_wt])
              P.op("dve", lambda e, bl=bl: e.tensor_tensor(out=w_t[:, 32:64], in0=pvec[:, bl + 64:bl + 96], in1=pvec[:, bl + 96:bl + 128],
                                                          op=ALU.mult), reads=[b_pv], writes=[b_wt])
              P.op("dve", lambda e: e.reduce_sum(out=lam[:, 1:3], in_=w_t[:, 0:64].rearrange("p (a b) -> p a b", b=32), axis=AX.X),
                   reads=[b_wt], writes=[b_lam])
              P.op("act", lambda e: e.activation(out=lam[:, 3:5], in_=lam[:, 1:3], func=AF.Exp), reads=[b_lam], writes=[b_lam])
              P.op("dve", lambda e: e.tensor_tensor(out=lam[:, 5:6], in0=lam[:, 3:4], in1=lam[:, 4:5], op=ALU.subtract),
                   reads=[b_lam], writes=[b_lam])
              P.op("dve", lambda e: e.memset(lam[:, 0:1], 1.0), reads=[b_lam], writes=[b_lam])
              P.op("dve", lambda e, l=l: e.tensor_scalar(out=lam[64:128, 0:1], in0=lam[64:128, 5:6], scalar1=lambda_init(l), scalar2=None,
                                                         op0=ALU.add), reads=[b_lam], writes=[b_lam])

              for c in range(NCH):
                  norm_stats(c, KT[:, 0:2, :].rearrange("p a (c t) -> p (a c) t", t=CH), b_KT,
                             rstd1[:, c * CH:(c + 1) * CH], b_rstd1[c], w_ln, b_wln)

              def cqT(j, c):
                  return yT[:, 4 + j, c * CH:(c + 1) * CH]

              def ckvT(j, c):
                  return yT[:, j, c * CH:(c + 1) * CH]

              def krT(c):
                  return yT[0:32, 2, c * CH:(c + 1) * CH]

              gen_banks[:] = [0, 1, 2, 3, 4, 6, 7]
              for c in range(NCH):
                  make_hT(c, 0, rstd1[:, c * CH:(c + 1) * CH], b_rstd1[c], l)
                  s1, bs1 = load_w([(0, 8, 0, 512, wv(w_in_d[l, :, C_CQ:C_CQ + 512]))])
                  s2, bs2 = load_w([(0, 8, 0, 160, wv(w_in_d[l, :, C_CQ + 512:C_CQ + 672]))])

                  def wcol(j):
                      if j < 4:
                          return s1, bs1, j * 128
                      return s2, bs2, (j - 4) * 128
                  for (j0, nj, gcol, dst, bdst, ncols) in ((0, 3, 19, cqT, b_y[1][c], 384), (3, 2, 22, ckvT, b_y[0][c], 256)):
                      for jj in range(nj):
                          sl, bsl, co = wcol(j0 + jj)
                          bk = gbank()
                          mm_group(bk, ps[:, bk, :], [(sl[:, k, co:co + 128], hT[:, k, :]) for k in range(8)], reads=[bsl, b_hT])
                          P.op("act", lambda e, bk=bk: e.activation(out=w_sq, in_=ps[:, bk, :], func=AF.Square),
                               reads=[bank[bk]], writes=[b_wsq])
                          P.op("dve", lambda e, bk=bk, jj=jj, gcol=gcol, dst=dst: e.tensor_scalar(
                              out=dst(jj, c), in0=ps[:, bk, :], scalar1=pvec[:, pv0 + gcol + jj:pv0 + gcol + jj + 1], scalar2=None,
                              op0=ALU.mult), reads=[bank[bk], b_pv], writes=[bdst])

                          def fn(e, jj=jj, nj=nj):
                              return e.matmul(ps[:, 5, :], lhsT=cm(M_ONES), rhs=w_sq, start=(jj == 0), stop=(jj == nj - 1))
                          P.op("pe", fn, reads=[b_wsq, b_const], writes=[bank[5]])
                      rstd_from_bank(5, 128, 1.0 / ncols, w_rs, b_wrs, w_ln, b_wln)
                      for jj in range(nj):
                          P.op("dve", lambda e, jj=jj, dst=dst: e.tensor_tensor(out=dst(jj, c), in0=dst(jj, c), in1=w_rs, op=ALU.mult),
                               reads=[bdst, b_wrs], writes=[bdst])
                  sl, bsl, co = s2, bs2, 128
                  bk = gbank()
                  mm_group(bk, ps[0:32, bk, :], [(sl[:, k, co:co + 32], hT[:, k, :]) for k in range(8)], reads=[bsl, b_hT])
                  P.op("act", lambda e, bk=bk: e.copy(out=w_qn[0:32], in_=ps[0:32, bk, :]), reads=[bank[bk]], writes=[b_wqn])
                  t, bt = load_tab(tabC_d, c, 32, 64)
                  rotary(32, None, cmat[0:32, M_RA * 128:M_RA * 128 + 32], t, bt, krT(c), b_y[0][c])

              P.op("dve", lambda e: e.memset(KT[64:128, :, :], 0.0), writes=b_KT)
              P.op("dve", lambda e: e.memset(qTall[64:128, :, :], 0.0), writes=b_q)
              for g in range(4):
                  sl, bsl = load_w([
                      (0, 3, 0, 192, wv(w_uq_d[l, :, g * 192:(g + 1) * 192])),
                      (3, 2, 0, 192, wv(w_uk_d[l, :, g * 192:(g + 1) * 192])),
                      (5, 2, 0, 64, wv(w_ukv_d[l, :, (2 * g) * 128 + 64:(2 * g) * 128 + 128])),
                      (5, 2, 64, 64, wv(w_ukv_d[l, :, (2 * g + 1) * 128 + 64:(2 * g + 1) * 128 + 128])),
                  ])
                  gen_banks[:] = [0, 1, 2, 3, 4, 5, 6, 7]
                  for c in range(NCH):
                      for hh in range(2):
                          bk = gbank()
                          prs = [(sl[:, 3 + kk, hh * 96:(hh + 1) * 96], ckvT(kk, c)) for kk in range(2)]
                          prs.append((cm(M_SEL, 32, 96), krT(c)))

                          def fn(e, prs=prs, bk=bk):
                              e.matmul(ps[0:96, bk, :], lhsT=prs[0][0], rhs=prs[0][1], start=True, stop=False)
                              e.matmul(ps[0:96, bk, :], lhsT=prs[1][0], rhs=prs[1][1], start=False, stop=False)
                              return e.matmul(ps[0:96, bk, :], lhsT=prs[2][0], rhs=prs[2][1], start=False, stop=True)
                          P.op("pe", fn, reads=[bsl, b_y[0][c], b_const], writes=[bank[bk]])
                          P.op("act", lambda e, bk=bk, hh=hh, c=c: e.copy(out=KT[0:96, hh, c * CH:(c + 1) * CH], in_=ps[0:96, bk, :]),
                               reads=[bank[bk]], writes=[b_KT[c]])
                      bk = gbank()

                      def fnv(e, c=c, bk=bk, sl=sl):
                          ins = None
                          for tt in range(4):
                              for kk in range(2):
                                  ins = e.matmul(ps[:, bk, tt * 128:(tt + 1) * 128],
                                                 lhsT=yT[:, kk, c * CH + tt * 128:c * CH + (tt + 1) * 128],
                                                 rhs=sl[:, 5 + kk, 0:128], start=(kk == 0), stop=(kk == 1))
                          return ins
                      P.op("pe", fnv, reads=[bsl, b_y[0][c]], writes=[bank[bk]])
                      P.op("act", lambda e, bk=bk, c=c: e.copy(out=Vv[:, 4 * c:4 * c + 4, 0:128],
                                                               in_=ps[:, bk, :].rearrange("p (a b) -> p a b", b=128)),
                           reads=[bank[bk]], writes=[b_V[c]])
                  gen_banks[:] = [0, 1, 2, 3, 4, 5, 6, 7]
                  for c in range(NCH):
                      t, bt = load_tab(tabC_d, c, 96, 0)
                      bks = []
                      for hh in range(2):
                          bk = gbank()
                          bks.append(bk)
                          mm_group(bk, ps[0:96, bk, :], [(sl[:, kk, hh * 96:(hh + 1) * 96], cqT(kk, c)) for kk in range(3)],
                                   reads=[bsl, b_y[1][c]])
                      for hh in range(2):
                          bk = bks[hh]
                          P.op("act", lambda e, bk=bk: e.copy(out=w_qn[0:96], in_=ps[0:96, bk, :]), reads=[bank[bk]], writes=[b_wqn])
                          rotary(96, None, cmat[64:96, M_RC * 128:M_RC * 128 + 96], t, bt, qTall[0:96, hh, c * CH:(c + 1) * CH], b_q[c],
                                 krows=(64, 96))
                  gen_banks[:] = [6, 7]
                  pairs = []
                  for c in range(NCH):
                      pairs.append(dict(
                          q=(qTall[:, 0, c * CH:(c + 1) * CH], qTall[:, 1, c * CH:(c + 1) * CH]),
                          k=lambda m, kb: KT[:, m, kb * 128:(kb + 1) * 128],
                          v=lambda m, kb: Vv[:, kb, m * 64:(m + 1) * 64],
                          tp=None, reads=[b_q[c]] + b_KT + b_V,
                          finish=finish_plain(yT[:, 8 + g, c * CH:(c + 1) * CH], b_y[2][c])))
                  attention(pairs, 96.0 ** -0.5)
                  dump_and_stop("C0")

              for gb_ in range(2):
                  sl, bsl = load_w([(0, 8, 0, 256, wv(w_in_d[l, :, C_BQ + 256 * gb_:C_BQ + 256 * gb_ + 256])),
                                    (0, 8, 256, 256, wv(w_in_d[l, :, C_BK + 256 * gb_:C_BK + 256 * gb_ + 256]))])
                  sv, bsv = load_w([(0, 8, 0, 256, wv(w_in_d[l, :, C_BV + 256 * gb_:C_BV + 256 * gb_ + 256]))])
                  gen_banks[:] = [0, 1, 2, 3, 4, 5, 6, 7]
                  for c in range(NCH):
                      make_hT(c, 0, rstd1[:, c * CH:(c + 1) * CH], b_rstd1[c], l)
                      t, bt = load_tab(tabB_d, c)
                      tl = []
                      for (co, dst, bdst) in ((0, qTall, b_q[c]), (256, KT, b_KT[c])):
                          for i in range(2):
                              bk = gbank()
                              mm_group(bk, ps[:, bk, :], [(sl[:, k, co + i * 128:co + (i + 1) * 128], hT[:, k, :]) for k in range(8)],
                                       reads=[bsl, b_hT])
                              tl.append((bk, dst, bdst, i))
                      for (bk, dst, bdst, i) in tl:
                          P.op("act", lambda e, bk=bk: e.copy(out=w_qn, in_=ps[:, bk, :]), reads=[bank[bk]], writes=[b_wqn])
                          rotary(128, None, cm(M_RB), t, bt, dst[:, i, c * CH:(c + 1) * CH], bdst)
                      for tt in range(4):
                          bk = gbank()
                          mm_group(bk, ps[:, bk, 0:256], [(hT[:, k, tt * 128:(tt + 1) * 128], sv[:, k, 0:256]) for k in range(8)],
                                   reads=[bsv, b_hT])
                          P.op("act", lambda e, bk=bk, tt=tt, c=c: e.copy(out=Vv[:, 4 * c + tt, 0:256], in_=ps[:, bk, 0:256]),
                               reads=[bank[bk]], writes=[b_V[c]])
                  gen_banks[:] = [6, 7]
                  pairs = []
                  for c in range(NCH):
                      for hh in range(4):
                          i = hh // 2
                          base = 64 * (hh % 2)

                          def fin(hh=hh, i=i, c=c, gb_=gb_):
                              P.op("dve", lambda e: e.reciprocal(out=w_r, in_=ps[:, 5, :]), reads=[bank[5]], writes=[b_wrr])
                              P.op("dve", lambda e: e.scalar_tensor_tensor(out=w_yn, in0=ps[:, 4, :], scalar=lam[:, 0:1], in1=w_r,
                                                                           op0=ALU.mult, op1=ALU.mult),
                                   reads=[bank[4], b_wrr, b_lam], writes=[b_wyn])
                              half = hh % 2

                              def fz(e, half=half):
                                  if half == 0:
                                      return e.matmul(ps[0:64, 6, :], lhsT=cm(M_COMB, 128, 64), rhs=w_yn, start=True, stop=True)
                                  return e.matmul(ps[64:128, 6, :], lhsT=cm(M_COMB, 128, 64), rhs=w_yn, start=True, stop=True,
                                                  tile_position=(0, 64))
                              P.op("pe", fz, reads=[b_wyn, b_const], writes=[bank[6]])
                              if half == 1:
                                  P.op("act", lambda e: e.activation(out=w_sq, in_=ps[:, 6, :], func=AF.Square),
                                       reads=[bank[6]], writes=[b_wsq])
                                  mm_group(7, ps[:, 7, :], [(cm(M_BLK64), w_sq)], reads=[b_wsq, b_const])
                                  rstd_from_bank(7, 128, 1.0 / 64, w_rs, b_wrs, w_ln, b_wln, bias_col=cst[:, 1 + l:2 + l])
                                  P.op("dve", lambda e: e.scalar_tensor_tensor(
                                      out=yT[:, 4 + 2 * gb_ + i, c * CH:(c + 1) * CH], in0=ps[:, 6, :],
                                      scalar=pvec[:, pv0 + 18:pv0 + 19], in1=w_rs, op0=ALU.mult, op1=ALU.mult),
                                      reads=[bank[6], b_wrs, b_pv], writes=[b_y[1][c]])
                          pairs.append(dict(
                              q=(qTall[base:base + 32, i, c * CH:(c + 1) * CH], qTall[base + 32:base + 64, i, c * CH:(c + 1) * CH]),
                              k=lambda m, kb, i=i, base=base: KT[base + 32 * m:base + 32 * m + 32, i, kb * 128:(kb + 1) * 128],
                              v=lambda m, kb, hh=hh: Vv[:, kb, hh * 64:(hh + 1) * 64],
                              tp=((base, 0), (base + 32, 0)), reads=[b_q[c]] + b_KT + b_V, finish=fin))
                  attention(pairs, 32.0 ** -0.5)

              for ga in range(2):
                  sl, bsl = load_w([(0, 8, 0, 256, wv(w_in_d[l, :, C_AQ + 256 * ga:C_AQ + 256 * ga + 256])),
                                    (0, 8, 256, 128, wv(w_kdup_d[l, :, 128 * ga:128 * ga + 128])),
                                    (0, 8, 384, 64, wv(w_in_d[l, :, C_AV + 64 * ga:C_AV + 64 * ga + 64]))])
                  gen_banks[:] = [0, 1, 2, 3, 4, 5, 6, 7]
                  for c in range(NCH):
                      make_hT(c, 0, rstd1[:, c * CH:(c + 1) * CH], b_rstd1[c], l)
                      t, bt = load_tab(tabA_d, c)
                      tl = []
                      for (co, gcol, dst, bdst) in ((0, 16, qTall[:, 0, c * CH:(c + 1) * CH], b_q[c]),
                                                   (128, 16, qTall[:, 1, c * CH:(c + 1) * CH], b_q[c]),
                                                   (256, 17, KT[:, 0, c * CH:(c + 1) * CH], b_KT[c])):
                          bk = gbank()
                          mm_group(bk, ps[:, bk, :], [(sl[:, k, co:co + 128], hT[:, k, :]) for k in range(8)], reads=[bsl, b_hT])
                          tl.append((bk, gcol, dst, bdst))
                      for (bk, gcol, dst, bdst) in tl:
                          P.op("act", lambda e, bk=bk: e.activation(out=w_sq, in_=ps[:, bk, :], func=AF.Square),
                               reads=[bank[bk]], writes=[b_wsq])
                          bk2 = gbank()
                          mm_group(bk2, ps[:, bk2, :], [(cm(M_BLK64), w_sq)], reads=[b_wsq, b_const])
                          rstd_from_bank(bk2, 128, 1.0 / 64, w_rs, b_wrs, w_ln, b_wln)
                          P.op("dve", lambda e, bk=bk, gcol=gcol: e.scalar_tensor_tensor(
                              out=w_qn, in0=ps[:, bk, :], scalar=pvec[:, pv0 + gcol:pv0 + gcol + 1], in1=w_rs,
                              op0=ALU.mult, op1=ALU.mult), reads=[bank[bk], b_wrs, b_pv], writes=[b_wqn])
                          rotary(128, None, cm(M_RA), t, bt, dst, bdst)
                      bk = gbank()

                      def fnv(e, bk=bk, sl=sl):
                          ins = None
                          for tt in range(4):
                              for k in range(8):
                                  ins = e.matmul(ps[:, bk, tt * 64:(tt + 1) * 64], lhsT=hT[:, k, tt * 128:(tt + 1) * 128],
                                                 rhs=sl[:, k, 384:448], start=(k == 0), stop=(k == 7))
                          return ins
                      P.op("pe", fnv, reads=[bsl, b_hT], writes=[bank[bk]])
                      P.op("act", lambda e, bk=bk, c=c: e.copy(out=Vv[:, 4 * c:4 * c + 4, 0:64],
                                                               in_=ps[:, bk, 0:256].rearrange("p (a b) -> p a b", b=64)),
                           reads=[bank[bk]], writes=[b_V[c]])
                  gen_banks[:] = [6, 7]
                  pairs = []
                  for c in range(NCH):
                      for i in range(2):
                          pairs.append(dict(
                              q=(qTall[0:64, i, c * CH:(c + 1) * CH], qTall[64:128, i, c * CH:(c + 1) * CH]),
                              k=lambda m, kb: KT[64 * m:64 * m + 64, 0, kb * 128:(kb + 1) * 128],
                              v=lambda m, kb: Vv[:, kb, 0:64],
                              tp=((0, 0), (64, 0)), reads=[b_q[c]] + b_KT + b_V,
                              finish=finish_plain(yT[:, 2 * ga + i, c * CH:(c + 1) * CH], b_y[0][c])))
                  attention(pairs, 64.0 ** -0.5)

              P.barrier()
              if debug and seq == 0 and l == 0:
                  dbg_y = nc.dram_tensor("dbg_y", [128, 12, S], BF16, kind="ExternalOutput").ap()
                  dbg_r = nc.dram_tensor("dbg_r", [128, S], F32, kind="ExternalOutput").ap()
                  dbg_x = nc.dram_tensor("dbg_x", [128, 8, S], F32, kind="ExternalOutput").ap()
                  dbg_l = nc.dram_tensor("dbg_l", [128, 8], F32, kind="ExternalOutput").ap()
                  b_dbg = P.buf("dbg")
                  d1 = P.op("sp", lambda e: [e.dma_start(out=dbg_y, in_=yT[:]), e.dma_start(out=dbg_r, in_=rstd1[:]),
                                             e.dma_start(out=dbg_x, in_=xT[:]), e.dma_start(out=dbg_l, in_=lam[:])],
                            reads=[b_dbg], dma=True, ndma=4)
                  P.barrier_op("sp", [d1])
                  P.barrier()
              gen_banks[:] = [4, 5, 6, 7]
              for c in range(NCH):
                  cs = slice(c * CH, (c + 1) * CH)
                  make_hT(c, 0, rstd1[:, cs], b_rstd1[c], l)
                  for ch in range(2):
                      for n in range(3):
                          sg, bsg = load_w([(0, 8, 0, 512, wv(w_in_d[l, :, C_G + n * D + ch * 512:C_G + n * D + ch * 512 + 512]))], key=(l, ch * 6 + n * 2))
                          sb_, bsb = load_w([(0, 4, 0, 512, wv(w_br_d[l, n, :, ch * 512:(ch + 1) * 512]))], key=(l, ch * 6 + n * 2 + 1))
                          for cg in range(4):
                              bg = gbank()
                              mm_group(bg, ps[:, bg, :], [(sg[:, k, cg * 128:(cg + 1) * 128], hT[:, k, :]) for k in range(8)],
                                       reads=[bsg, b_hT])
                              bz = gbank()
                              mm_group(bz, ps[:, bz, :], [(sb_[:, kk, cg * 128:(cg + 1) * 128], yT[:, 4 * n + kk, cs]) for kk in range(4)],
                                       reads=[bsb, b_y[n][c]])
                              gi = (n * 4 + cg) % 2
                              P.op("act", lambda e, bg=bg, gi=gi: e.activation(out=Gs[gi], in_=ps[:, bg, :], func=AF.Sigmoid),
                                   reads=[bank[bg]], writes=[b_Gs[gi]])
                              if n == 0:
                                  P.op("dve", lambda e, bz=bz, gi=gi, cg=cg: e.tensor_tensor(out=macc[:, cg, :], in0=ps[:, bz, :], in1=Gs[gi],
                                                                                           op=ALU.mult),
                                       reads=[bank[bz], b_Gs[gi]], writes=[b_hid[0]])
                              else:
                                  P.op("dve", lambda e, bz=bz, gi=gi: e.tensor_tensor(out=tmpm, in0=ps[:, bz, :], in1=Gs[gi], op=ALU.mult),
                                       reads=[bank[bz], b_Gs[gi]], writes=[b_tmpm])
                                  if n == 1:
                                      P.op(EW, lambda e, cg=cg: e.tensor_tensor(out=macc[:, cg, :], in0=macc[:, cg, :], in1=tmpm, op=ALU.add),
                                           reads=[b_hid[0], b_tmpm], writes=[b_hid[0]])
                                  else:
                                      P.op(EW, lambda e, cg=cg, ch=ch: e.tensor_tensor(out=mT[:, ch * 4 + cg, :], in0=macc[:, cg, :], in1=tmpm,
                                                                                         op=ALU.add),
                                           reads=[b_hid[0], b_tmpm], writes=[b_mT])
                  for ch in range(2):
                      so, bso = load_w([(0, 8, 0, 512, wv(w_out_d[l, :, ch * 512:(ch + 1) * 512]))], key=(l, 12 + ch))
                      for cg in range(4):
                          bk = gbank()
                          mm_group(bk, ps[:, bk, :], [(so[:, k, cg * 128:(cg + 1) * 128], mT[:, k, :]) for k in range(8)], reads=[bso, b_mT])
                          P.op("dve", lambda e, bk=bk, ch=ch, cg=cg: e.tensor_tensor(out=xT[:, ch * 4 + cg, cs], in0=ps[:, bk, :],
                                                                                     in1=xT[:, ch * 4 + cg, cs], op=ALU.add),
                               reads=[bank[bk], b_xT[c]], writes=[b_xT[c]])
                  norm_stats(c, sq8, [b_hid[1]], w_rs2, b_wrs2, w_ln2, b_wln2)
                  make_hT(c, 8, w_rs2, b_wrs2, l)
                  for b in range(8):
                      s1, bs1 = load_w([(0, 8, 0, 512, wv(w_ff1_d[l, :, b * 512:(b + 1) * 512]))], key=(l, 14 + b))
                      for cg in range(4):
                          j = b * 4 + cg
                          bk = gbank()
                          mm_group(bk, ps[:, bk, :], [(s1[:, k, cg * 128:(cg + 1) * 128], hT[:, k, :]) for k in range(8)], reads=[bs1, b_hT])
                          ri = j % 2
                          P.op("act", lambda e, bk=bk, ri=ri: e.activation(out=rl[ri], in_=ps[:, bk, :], func=AF.Relu),
                               reads=[bank[bk]], writes=[b_rl[ri]])
                          P.op(EW2, lambda e, j=j, ri=ri: e.tensor_tensor(out=hid[:, j, :], in0=rl[ri], in1=rl[ri], op=ALU.mult),
                               reads=[b_rl[ri]], writes=[b_hid[j // 8]])
                  for ch in range(2):
                      for kb in range(4):
                          s2, bs2 = load_w([(0, 8, 0, 512, wv(w_ff2_d[l, kb * D:(kb + 1) * D, ch * 512:(ch + 1) * 512]))], key=(l, 22 + ch * 4 + kb))
                          for cg in range(4):
                              def fn(e, s2=s2, kb=kb, cg=cg):
                                  ins = None
                                  for k in range(8):
                                      ins = e.matmul(ps[:, cg, :], lhsT=s2[:, k, cg * 128:(cg + 1) * 128], rhs=hid[:, kb * 8 + k, :],
                                                     start=(kb == 0 and k == 0), stop=(kb == 3 and k == 7))
                                  return ins
                              P.op("pe", fn, reads=[bs2, b_hid[kb]], writes=[bank[cg]])
                      for cg in range(4):
                          P.op("dve", lambda e, ch=ch, cg=cg: e.tensor_tensor(out=xT[:, ch * 4 + cg, cs], in0=ps[:, cg, :],
                                                                              in1=xT[:, ch * 4 + cg, cs], op=ALU.add),
                               reads=[bank[cg], b_xT[c]], writes=[b_xT[c]])
              P.barrier()

          gen_banks[:] = [0, 1, 2, 3, 4, 5, 6, 7]
          fg0 = DEPTH * NVL
          last_store = []
          for c in range(NCH):
              norm_stats(c, sqf, [b_sqf], w_rs3, b_wrs3, w_ln3, b_wln3)
              for k in range(8):
                  P.op("dve", lambda e, k=k, c=c: e.scalar_tensor_tensor(out=ofin[:, k, :], in0=xT[:, k, c * CH:(c + 1) * CH],
                                                                         scalar=pvec[:, fg0 + k:fg0 + k + 1], in1=w_rs3,
                                                                         op0=ALU.mult, op1=ALU.mult),
                       reads=[b_xT[c], b_wrs3, b_pv], writes=[b_ofin])
              for tt in range(4):
                  sgi = tt % 2
                  for h in range(2):
                      bk = gbank()

                      def fn(e, tt=tt, h=h, bk=bk):
                          ins = None
                          for j in range(4):
                              ins = e.transpose(out=ps[:, bk, j * 128:(j + 1) * 128], in_=ofin[:, 4 * h + j, tt * 128:(tt + 1) * 128],
                                                identity=ident[:])
                          return ins
                      P.op("pe", fn, reads=[b_ofin, b_pv], writes=[bank[bk]])
                      if h == 0:
                          P.op("act", lambda e, sgi=sgi, bk=bk: e.copy(out=stage[sgi][:, 0:512], in_=ps[:, bk, :]),
                               reads=[bank[bk]], writes=[b_stage[sgi]])
                      else:
                          P.op("dve", lambda e, sgi=sgi, bk=bk: e.tensor_copy(out=stage[sgi][:, 512:1024], in_=ps[:, bk, :]),
                               reads=[bank[bk]], writes=[b_stage[sgi]])
                  t0 = c * CH + tt * 128
                  o = P.op("sp", lambda e, sgi=sgi, t0=t0: e.dma_start(out=out_d[seq, t0:t0 + 128, :], in_=stage[sgi]),
                           reads=[b_stage[sgi]], dma=True)
                  last_store.append(o)
          P.barrier_op("sp", last_store)


    except _Stop:
        pass
    P.emit()
    return nc, P


def _angles(pos, dim, theta):
    inv = (np.float32(theta) ** (-np.arange(0, dim, 2, dtype=np.float32) / np.float32(dim))).astype(np.float32)
    return (pos.astype(np.float32)[:, None] * inv[None, :]).astype(np.float32)


def _tables():
    t = np.arange(S)
    row_idx = (t // 64)
    col_idx = (t % 64)
    row_ang = _angles(row_idx, 32, 10000.0)
    col_ang = _angles(col_idx, 32, 10000.0)
    b_ang = _angles(t, 8, 500000.0)
    c_ang = _angles(t, 32, 10000.0)
    tabA = np.zeros((128, 2, S), np.float32)
    tabB = np.zeros((128, 2, S), np.float32)
    tabC = np.zeros((128, 2, S), np.float32)
    tabB[:, 0, :] = 1.0
    tabC[:, 0, :] = 1.0
    for p in range(128):
        d = p % 64
        ang = row_ang[:, d % 16] if d < 32 else col_ang[:, (d - 32) % 16]
        tabA[p, 0] = np.cos(ang)
        tabA[p, 1] = np.sin(ang)
        d = p % 32
        if d < 8:
            ang = b_ang[:, d % 4]
            tabB[p, 0] = np.cos(ang)
            tabB[p, 1] = np.sin(ang)
        if 64 <= p < 96:
            ang = c_ang[:, (p - 64) % 16]
            tabC[p, 0] = np.cos(ang)
            tabC[p, 1] = np.sin(ang)
    return tabA, tabB, tabC


def _cmat():
    m = np.zeros((7, 128, 128), np.float32)
    m[M_ONES] = 1.0
    m[M_BLK64, 0:64, 0:64] = 1.0
    m[M_BLK64, 64:128, 64:128] = 1.0
    for blk in range(4):
        o = blk * 32
        for i in range(16):
            m[M_RA, o + i + 16, o + i] = -1.0
            m[M_RA, o + i, o + i + 16] = 1.0
        for i in range(4):
            m[M_RB, o + i + 4, o + i] = -1.0
            m[M_RB, o + i, o + i + 4] = 1.0
    for i in range(16):
        m[M_RC, 64 + i + 16, 64 + i] = -1.0
        m[M_RC, 64 + i, 64 + i + 16] = 1.0
    for i in range(32):
        m[M_SEL, i, 64 + i] = 1.0
    for i in range(64):
        m[M_COMB, i, i] = 1.0
        m[M_COMB, 64 + i, i] = -1.0
    return np.ascontiguousarray(m.transpose(1, 0, 2).reshape(128, 7 * 128))


def _pvec(ln1_g, ln2_g, a_q_norm, a_k_norm, b_lambda, b_subln, c_q_norm, c_kv_norm, final_g):
    pv = np.zeros((128, DEPTH * NVL + 8), np.float32)
    idx = np.arange(128)
    for l in range(DEPTH):
        o = l * NVL
        pv[:, o:o + 8] = ln1_g[l].reshape(8, 128).T
        pv[:, o + 8:o + 16] = ln2_g[l].reshape(8, 128).T
        pv[:, o + 16] = a_q_norm[l][idx % 64]
        pv[:, o + 17] = a_k_norm[l][idx % 64]
        pv[:, o + 18] = b_subln[l][idx % 64]
        pv[:, o + 19:o + 22] = c_q_norm[l].reshape(3, 128).T
        pv[:, o + 22:o + 24] = c_kv_norm[l].reshape(2, 128).T
        pv[:, o + 24:o + 152] = b_lambda[l].reshape(1, 128)
    pv[:, DEPTH * NVL:] = final_g.reshape(8, 128).T
    return pv


_CACHE = {}


def kernel(x, ln1_g, w_in, a_q_norm, a_k_norm, b_lambda, b_subln, c_q_norm, c_kv_norm,
           c_w_uq, c_w_ukv, w_branch, w_out, ln2_g, w_ff1, w_ff2, final_g):
    f = lambda a: np.ascontiguousarray(np.asarray(a, dtype=np.float32))
    x = f(x)
    w_in = f(w_in)
    c_w_ukv = f(c_w_ukv)
    n_cores = 8
    w_kdup = np.empty((DEPTH, D, 256), np.float32)
    for j in range(2):
        wk = w_in[:, :, C_AK + 64 * j:C_AK + 64 * j + 64]
        w_kdup[:, :, 128 * j:128 * j + 64] = wk
        w_kdup[:, :, 128 * j + 64:128 * j + 128] = wk
    w_uk96 = np.zeros((DEPTH, 256, 8, 96), np.float32)
    w_uk96[:, :, :, 0:64] = c_w_ukv.reshape(DEPTH, 256, 8, 128)[:, :, :, 0:64]
    w_uk96 = w_uk96.reshape(DEPTH, 256, 768)
    tabA, tabB, tabC = _tables()
    shared = {
        "w_in": w_in, "w_kdup": w_kdup, "c_w_uq": f(c_w_uq), "w_uk96": w_uk96, "c_w_ukv": c_w_ukv,
        "w_branch": f(w_branch), "w_out": f(w_out), "w_ff1": f(w_ff1), "w_ff2": f(w_ff2),
        "pvec": _pvec(f(ln1_g), f(ln2_g), f(a_q_norm), f(a_k_norm), f(b_lambda), f(b_subln), f(c_q_norm), f(c_kv_norm), f(final_g)),
        "cmat": _cmat(), "ident": np.eye(128, dtype=np.float32), "tabA": tabA, "tabB": tabB, "tabC": tabC,
    }
    if "nc" not in _CACHE:
        _CACHE["nc"] = build_program()[0]
    nc = _CACHE["nc"]
    in_maps = []
    for c in range(n_cores):
        m = dict(shared)
        m["x"] = np.ascontiguousarray(x[NSEQ * c:NSEQ * (c + 1)])
        in_maps.append(m)
    res = run_bass_kernel_spmd(nc, in_maps, core_ids=list(range(n_cores)))
    out = np.concatenate([np.asarray(r["out"]) for r in res.results], axis=0)
    return out.astype(np.float32)
```
